# Optimizing a Trainium2 kernel written in Bass

```python
import jax, jax.numpy as jnp
from jax import lax
import numpy as np

D_MODEL = 1024
BATCH = 16
SEQ = 2048
DEPTH = 1

GDN_HEADS = 8
GDN_DK = 128
GDN_DV = 128
GDN_CONV = 4
GDN_CHUNK = 64
FOX_HEADS = 8
FOX_DH = 128
FOX_BLOCK = 128
D_FF = 4 * D_MODEL
EPS = 1e-6

GDN_QK_W = GDN_HEADS * GDN_DK
GDN_V_W = GDN_HEADS * GDN_DV
FOX_W = FOX_HEADS * FOX_DH
IN_SPLITS = (GDN_QK_W, GDN_QK_W, GDN_V_W, GDN_V_W, GDN_HEADS, GDN_HEADS,
             FOX_W, FOX_W, FOX_W, FOX_HEADS, D_MODEL, D_MODEL)
N_IN = sum(IN_SPLITS)

kernel_name = "hybrid_gdn_fox_gated_merge_block"


def rmsnorm(x, g):
    xf = x.astype(jnp.float32)
    y = xf * lax.rsqrt(jnp.mean(xf * xf, axis=-1, keepdims=True) + EPS)
    return (y * g.astype(jnp.float32)).astype(x.dtype)


def l2norm(x):
    xf = x.astype(jnp.float32)
    return xf * lax.rsqrt(jnp.sum(xf * xf, axis=-1, keepdims=True) + EPS)


def to_heads(x, n_heads):
    b, t, w = x.shape
    return x.reshape(b, t, n_heads, w // n_heads).transpose(0, 2, 1, 3)


def causal_depthwise_conv(x, w):
    k_width = w.shape[0]
    t = x.shape[1]
    xp = jnp.pad(x, ((0, 0), (k_width - 1, 0), (0, 0)))
    y = xp[:, 0:t] * w[0]
    for i in range(1, k_width):
        y = y + xp[:, i:i + t] * w[i]
    return y


def gated_delta_rule_chunked(q, k, v, g, beta):
    b, h, t, dk = q.shape
    dv = v.shape[-1]
    c = GDN_CHUNK
    n = t // c
    q = q.reshape(b, h, n, c, dk)
    k = k.reshape(b, h, n, c, dk)
    v = v.reshape(b, h, n, c, dv)
    g = g.reshape(b, h, n, c)
    beta = beta.reshape(b, h, n, c)

    G = jnp.cumsum(g, axis=-1)
    diff = G[..., :, None] - G[..., None, :]
    tri_incl = jnp.tril(jnp.ones((c, c), dtype=bool))
    tri_strict = jnp.tril(jnp.ones((c, c), dtype=bool), -1)
    decay = jnp.where(tri_incl, jnp.exp(jnp.where(tri_incl, diff, 0.0)), 0.0)

    kk = jnp.einsum('bhncd,bhnsd->bhncs', k, k)
    a_mat = jnp.where(tri_strict, beta[..., None] * kk * decay, 0.0)
    eye = jnp.eye(c, dtype=jnp.float32)
    rhs = jnp.concatenate([v * beta[..., None], k * (beta * jnp.exp(G))[..., None]], axis=-1)
    a_full = jnp.broadcast_to(a_mat + eye, (b, h, n, c, c))
    sol = lax.linalg.triangular_solve(a_full, rhs, left_side=True, lower=True,
                                      unit_diagonal=True)
    value, k_cum = sol[..., :dv], sol[..., dv:]

    attn_intra = jnp.einsum('bhncd,bhnsd->bhncs', q, k) * decay
    q_dec = q * jnp.exp(G)[..., None]
    g_last = G[..., -1:]
    k_dec = k * jnp.exp(g_last - G)[..., None]
    chunk_decay = jnp.exp(g_last[..., 0])

    def step(s, inp):
        value_c, kcum_c, attn_c, qdec_c, kdec_c, dec_c = inp
        v_new = value_c - jnp.einsum('bhcd,bhde->bhce', kcum_c, s)
        o = (jnp.einsum('bhcd,bhde->bhce', qdec_c, s)
             + jnp.einsum('bhcs,bhse->bhce', attn_c, v_new))
        s = s * dec_c[..., None, None] + jnp.einsum('bhcd,bhce->bhde', kdec_c, v_new)
        return s, o

    xs = tuple(jnp.moveaxis(a, 2, 0) for a in (value, k_cum, attn_intra, q_dec, k_dec, chunk_decay))
    s0 = jnp.zeros((b, h, dk, dv), jnp.float32)
    _, o = lax.scan(step, s0, xs)
    return jnp.moveaxis(o, 0, 2).reshape(b, h, t, dv)


def forgetting_attention(q, k, v, log_f):
    t = q.shape[2]
    cum = jnp.cumsum(log_f, axis=-1)
    scale = FOX_DH ** -0.5
    outs = []
    for i in range(t // FOX_BLOCK):
        q0 = i * FOX_BLOCK
        q1 = q0 + FOX_BLOCK
        s = (jnp.einsum('bhqd,bhkd->bhqk', q[:, :, q0:q1], k[:, :, :q1]) * scale
             + cum[:, :, q0:q1, None] - cum[:, :, None, :q1])
        mask = (q0 + jnp.arange(FOX_BLOCK))[:, None] >= jnp.arange(q1)[None, :]
        p = jax.nn.softmax(jnp.where(mask, s, -jnp.inf), axis=-1)
        outs.append(jnp.einsum('bhqk,bhkd->bhqd', p, v[:, :, :q1]))
    return jnp.concatenate(outs, axis=2)


def setup_inputs(seed: int = 0) -> dict:
    key = jax.random.key(seed)
    ks = jax.random.split(key, 20)
    f32 = jnp.float32

    def nrm(k, shape, fan_in):
        return jax.random.normal(k, shape, f32) * (fan_in ** -0.5)

    def gain(k, shape):
        return 1.0 + 0.02 * jax.random.normal(k, shape, f32)

    dt = jnp.exp(jax.random.uniform(ks[5], (DEPTH, GDN_HEADS), f32,
                                    minval=float(np.log(1e-3)), maxval=float(np.log(1e-1))))
    return {
        "x": jax.random.normal(ks[0], (BATCH, SEQ, D_MODEL), f32),
        "norm_mix_g": gain(ks[1], (DEPTH, D_MODEL)),
        "w_in": nrm(ks[2], (DEPTH, D_MODEL, N_IN), D_MODEL),
        "gdn_conv_w": nrm(ks[3], (DEPTH, GDN_CONV, 2 * GDN_QK_W + GDN_V_W), GDN_CONV),
        "gdn_a_log": jnp.log(jax.random.uniform(ks[4], (DEPTH, GDN_HEADS), f32, minval=1.0, maxval=16.0)),
        "gdn_dt_bias": dt + jnp.log(-jnp.expm1(-dt)),
        "gdn_norm_g": gain(ks[6], (DEPTH, GDN_DV)),
        "fox_q_norm_g": gain(ks[7], (DEPTH, FOX_DH)),
        "fox_k_norm_g": gain(ks[8], (DEPTH, FOX_DH)),
        "fox_f_bias": jax.random.uniform(ks[9], (DEPTH, FOX_HEADS), f32, minval=2.0, maxval=6.0),
        "w_proj_gdn": nrm(ks[10], (DEPTH, GDN_V_W, D_MODEL), GDN_V_W),
        "w_proj_fox": nrm(ks[11], (DEPTH, FOX_W, D_MODEL), FOX_W),
        "w_out": nrm(ks[12], (DEPTH, D_MODEL, D_MODEL), D_MODEL),
        "norm_mlp_g": gain(ks[13], (DEPTH, D_MODEL)),
        "w_up": nrm(ks[14], (DEPTH, D_MODEL, D_FF), D_MODEL),
        "w_down": nrm(ks[15], (DEPTH, D_FF, D_MODEL), D_FF),
    }


def reference(x, norm_mix_g, w_in, gdn_conv_w, gdn_a_log, gdn_dt_bias, gdn_norm_g,
              fox_q_norm_g, fox_k_norm_g, fox_f_bias, w_proj_gdn, w_proj_fox, w_out,
              norm_mlp_g, w_up, w_down):
    f32 = jnp.float32
    b, t, _ = x.shape
    split_idx = [int(i) for i in np.cumsum(IN_SPLITS)[:-1]]
    for l in range(DEPTH):
        u = rmsnorm(x, norm_mix_g[l])
        proj = u @ w_in[l]
        (gq, gk, gv, gz, ga, gb, fq, fk, fv, ff, gate_a, gate_b) = jnp.split(proj, split_idx, axis=-1)

        qkv = jax.nn.silu(causal_depthwise_conv(jnp.concatenate([gq, gk, gv], axis=-1), gdn_conv_w[l]))
        cq, ck, cv = jnp.split(qkv, [GDN_QK_W, 2 * GDN_QK_W], axis=-1)
        qh = l2norm(to_heads(cq, GDN_HEADS)) * (GDN_DK ** -0.5)
        kh = l2norm(to_heads(ck, GDN_HEADS))
        vh = to_heads(cv, GDN_HEADS).astype(f32)
        beta = jax.nn.sigmoid(gb.astype(f32)).transpose(0, 2, 1)
        g = (-jnp.exp(gdn_a_log[l].astype(f32))
             * jax.nn.softplus(ga.astype(f32) + gdn_dt_bias[l].astype(f32))).transpose(0, 2, 1)
        o_a = gated_delta_rule_chunked(qh, kh, vh, g, beta).transpose(0, 2, 1, 3)
        z = gz.reshape(b, t, GDN_HEADS, GDN_DV).astype(f32)
        o_a = rmsnorm(o_a, gdn_norm_g[l]) * jax.nn.silu(z)
        y_a = o_a.reshape(b, t, GDN_V_W).astype(x.dtype) @ w_proj_gdn[l]

        fqh = rmsnorm(to_heads(fq, FOX_HEADS), fox_q_norm_g[l]).astype(f32)
        fkh = rmsnorm(to_heads(fk, FOX_HEADS), fox_k_norm_g[l]).astype(f32)
        fvh = to_heads(fv, FOX_HEADS).astype(f32)
        log_f = jax.nn.log_sigmoid(ff.astype(f32) + fox_f_bias[l].astype(f32)).transpose(0, 2, 1)
        o_b = forgetting_attention(fqh, fkh, fvh, log_f).transpose(0, 2, 1, 3)
        y_b = o_b.reshape(b, t, FOX_W).astype(x.dtype) @ w_proj_fox[l]

        merged = jax.nn.sigmoid(gate_a) * y_a + jax.nn.sigmoid(gate_b) * y_b
        h = x + merged @ w_out[l]

        hn = rmsnorm(h, norm_mlp_g[l])
        x = h + jnp.square(jax.nn.relu(hn @ w_up[l])) @ w_down[l]
    return x
```

```python
import contextlib
import concourse.bass as bass
import concourse.mybir as mybir

F32 = mybir.dt.float32
BF16 = mybir.dt.bfloat16
AF = mybir.ActivationFunctionType
ALU = mybir.AluOpType
AX = mybir.AxisListType

ENGS = ("pe", "act", "dve", "pool", "sp")


class Op:
    __slots__ = ("eng", "fn", "deps", "dma", "idx", "sig", "signal")

    def __init__(self, eng, fn, dma):
        self.eng = eng
        self.fn = fn
        self.deps = []
        self.dma = dma
        self.sig = None
        self.signal = False


class Prog:
    def __init__(self, nc):
        self.nc = nc
        self.ops = []
        self.last_w = {}
        self.readers = {}
        self.final_waits = []

    def op(self, eng, fn, reads=(), writes=(), dma=None):
        o = Op(eng, fn, dma)
        o.idx = len(self.ops)
        pfr = [r for r in reads if r.startswith("pf")]
        if pfr:
            reads = [r for r in reads if not r.startswith("pf")]
            writes = list(writes) + [r for r in pfr if r not in writes]
        deps = set()
        for r in reads:
            w = self.last_w.get(r)
            if w is not None:
                deps.add(w)
        for wkey in writes:
            w = self.last_w.get(wkey)
            if w is not None:
                deps.add(w)
            for rd in self.readers.get(wkey, ()):
                deps.add(rd)
        for d in deps:
            if d is o:
                continue
            if d.dma is None and d.eng == eng:
                if eng == "pe":
                    continue
            o.deps.append(d)
        for r in reads:
            self.readers.setdefault(r, []).append(o)
        for wkey in writes:
            self.last_w[wkey] = o
            self.readers[wkey] = []
        self.ops.append(o)
        return o

    def barrier(self):
        last = {}
        for o in self.ops:
            if o.fn is None:
                continue
            k = ("dma", o.dma) if o.dma is not None else ("eng", o.eng)
            last[k] = o
        for e in ENGS:
            b = Op(e, None, None)
            b.idx = len(self.ops)
            b.deps = [d for k, d in last.items() if not (k == ("eng", e))]
            self.ops.append(b)
        self.last_w = {}
        self.readers = {}

    def emit(self, stack):
        nc = self.nc
        for o in self.ops:
            for d in o.deps:
                d.signal = True
        for o in self.final_waits:
            o.signal = True
        cnt = {}
        for o in self.ops:
            if o.dma is not None:
                k = ("dma", o.dma)
                cnt[k] = cnt.get(k, 0) + 16
                o.sig = (k, cnt[k])
            elif o.signal:
                k = ("eng", o.eng)
                cnt[k] = cnt.get(k, 0) + 1
                o.sig = (k, cnt[k])
        sems = {}
        for k in cnt:
            nm = "s_" + "_".join(str(x) for x in k)
            sems[k] = stack.enter_context(nc.semaphore(nm))
        per_eng = {e: [o for o in self.ops if o.eng == e] for e in ENGS}
        finals = list(self.final_waits)
        block = stack.enter_context(nc.Block())
        nwaits = [0]

        def run(eng_name, eobj):
            waited = {}
            for o in per_eng[eng_name]:
                need = {}
                for d in o.deps:
                    k, v = d.sig
                    if waited.get(k, 0) >= v:
                        continue
                    if need.get(k, 0) < v:
                        need[k] = v
                for k, v in need.items():
                    eobj.wait_ge(sems[k], v)
                    waited[k] = v
                    nwaits[0] += 1
                if o.fn is None:
                    continue
                ins = o.fn(eobj)
                if o.dma is not None:
                    ins.then_inc(sems[o.sig[0]], 16)
                elif o.signal:
                    ins.then_inc(sems[o.sig[0]], 1)
            if eng_name == "sp":
                need = {}
                for d in finals:
                    k, v = d.sig
                    if need.get(k, 0) < v:
                        need[k] = v
                for k, v in need.items():
                    if waited.get(k, 0) < v:
                        eobj.wait_ge(sems[k], v)

        @block.tensor
        def _(e):
            run("pe", e)

        @block.scalar
        def _(e):
            run("act", e)

        @block.vector
        def _(e):
            run("dve", e)

        @block.gpsimd
        def _(e):
            run("pool", e)

        @block.sync
        def _(e):
            run("sp", e)
        return nwaits[0]

import numpy as np
import os
from concourse.bass_utils import run_bass_kernel_spmd

D = 1024
NH = 8
DH = 128
DFF = 4096
EPS = 1e-6
ARENA = 105600


class StopBuild(Exception):
    pass


class Arena:
    def __init__(self, t):
        self.t = t
        self.off = 0
        self.peak = 0

    def bf(self, n):
        n2 = (n + 15) // 16 * 16
        o = self.off
        self.off += n2
        self.peak = max(self.peak, self.off)
        assert self.off <= ARENA, ("arena overflow", self.off)
        return self.t[:, o:o + n]

    def f32(self, n):
        n2 = (2 * n + 15) // 16 * 16
        o = self.off
        self.off += n2
        self.peak = max(self.peak, self.off)
        assert self.off <= ARENA, ("arena overflow", self.off)
        return self.t[:, o:o + 2 * n].bitcast(F32)

    def mark(self):
        return self.off

    def release(self, m):
        self.off = m


class K:
    def __init__(self, P):
        self.P = P

    def mm(self, out, lhsT, rhs, start, stop, r, w):
        return self.P.op("pe", lambda e: e.matmul(out, lhsT=lhsT, rhs=rhs, start=start, stop=stop), r, w)

    def tr(self, out, in_, ident, r, w):
        return self.P.op("pe", lambda e: e.transpose(out=out, in_=in_, identity=ident), r, w)

    def act(self, out, in_, func, r, w, scale=None, bias=None, accum=None):
        kw = {}
        if scale is not None:
            kw["scale"] = scale
        if bias is not None:
            kw["bias"] = bias
        if accum is not None:
            kw["accum_out"] = accum
        return self.P.op("act", lambda e: e.activation(out=out, in_=in_, func=func, **kw), r, w)

    def cp(self, eng, out, in_, r, w):
        if eng == "act":
            return self.P.op("act", lambda e: e.copy(out=out, in_=in_), r, w)
        return self.P.op(eng, lambda e: e.tensor_copy(out=out, in_=in_), r, w)

    def tt(self, eng, out, in0, in1, op, r, w):
        return self.P.op(eng, lambda e: e.tensor_tensor(out=out, in0=in0, in1=in1, op=op), r, w)

    def ts(self, eng, out, in0, s1, op0, r, w, s2=None, op1=None):
        if op1 is None:
            return self.P.op(eng, lambda e: e.tensor_scalar(out=out, in0=in0, scalar1=s1, scalar2=None, op0=op0), r, w)
        return self.P.op(eng, lambda e: e.tensor_scalar(out=out, in0=in0, scalar1=s1, scalar2=s2, op0=op0, op1=op1), r, w)

    def stt(self, out, in0, scalar, in1, op0, op1, r, w):
        return self.P.op("dve", lambda e: e.scalar_tensor_tensor(out=out, in0=in0, scalar=scalar, in1=in1, op0=op0, op1=op1), r, w)

    def memset(self, eng, ap, val, w):
        return self.P.op(eng, lambda e: e.memset(ap, val), (), w)

    def dma(self, eng, out, in_, r, w, key):
        return self.P.op(eng, lambda e: e.dma_start(out=out, in_=in_), r, w, dma=key)


def build(nc, T, NSEQ, dbg=False, stop=None):
    NT = T // 128
    BW = min(512, T)
    NB = T // BW
    TPB = BW // 128
    NT8 = NT * 8
    dram = lambda n, s, kind="ExternalInput": nc.dram_tensor(n, list(s), F32, kind=kind).ap()
    x_d = dram("x", [NSEQ, T, D])
    out_d = dram("out", [NSEQ, T, D], "ExternalOutput")
    wgdn_d = dram("w_gdn", [NH, 128, 8 * 512])
    wfox_d = dram("w_fox", [NH, 128, 8 * 384])
    wsm_d = dram("w_small", [128, 8 * 24])
    wgate_d = dram("w_gate", [8, 128, 8 * 256])
    wpab_d = dram("w_pab", [8, 128, 8 * 256])
    wout_d = dram("w_o", [128, 8 * 1024])
    wup_d = dram("w_up", [32, 128, 8 * 128])
    wdn_d = dram("w_down", [32, 128, 1024])
    convw_d = dram("convw", [128, 24 * 4])
    gmix_d = dram("gmix", [1, D])
    gmlp_d = dram("gmlp", [1, D])
    gnorm4_d = dram("gnorm4", [1, 512])
    alog_d = dram("alog_t", [1, NT8])
    dtb_d = dram("dtb_t", [1, NT8])
    fb_d = dram("fb_t", [1, NT8])
    fqg_d = dram("fqg", [128, 1])
    fkg_d = dram("fkg", [128, 1])
    dbg_d = {}
    if dbg:
        dbg_d["uT"] = dram("d_uT", [128, 8 * T], "ExternalOutput")
        dbg_d["oaT"] = dram("d_oaT", [128, 8 * T], "ExternalOutput")
        dbg_d["obT"] = dram("d_obT", [128, 8 * T], "ExternalOutput")
        dbg_d["tab"] = dram("d_tab", [128, 6 * NT8], "ExternalOutput")
        dbg_d["h"] = dram("d_h", [128, NT * D], "ExternalOutput")

    with contextlib.ExitStack() as st:
        P = Prog(nc)
        k = K(P)
        arena_t = st.enter_context(nc.sbuf_tensor("arena", [128, ARENA], BF16))
        A = Arena(arena_t)
        pf = [st.enter_context(nc.psum_tensor("pf%d" % i, [128, 512], F32)) for i in range(8)]
        pbf = pf[7][:, :].bitcast(BF16)

        def PF(b, c0=0, c1=512):
            return ["pf%d" % b]

        def pq(b, q, n=128):
            return pf[b][:, q * 128:q * 128 + n]

        def pbs(s):
            return pbf[:, s * 128:(s + 1) * 128]

        ident_bf = A.bf(128)
        ones_bf = A.bf(128)
        onesm_bf = A.bf(128)
        mUi_bf = A.bf(128)
        ident_f = A.f32(128)
        ones_f = A.f32(128)
        triU_f = A.f32(128)
        mUs_bd = A.f32(128)
        mLs_bd = A.f32(128)
        mL_off = A.f32(128)
        gmix_bc = A.f32(D)
        gmlp_bc = A.f32(D)
        gnorm4 = A.f32(512)
        alog_bc = A.f32(NT8)
        dtb_bc = A.f32(NT8)
        fb_bc = A.f32(NT8)
        cw = A.f32(96)
        fqg = A.f32(1)
        fkg = A.f32(1)
        eps_c = A.f32(1)
        one_c = A.f32(1)

        k.memset("pool", ones_bf, 1.0, ["ones_bf"])
        k.memset("pool", onesm_bf, 1.0 / 128, ["onesm_bf"])
        k.memset("pool", ones_f, 1.0, ["ones_f"])
        k.memset("pool", eps_c, EPS, ["eps_c"])
        k.memset("pool", one_c, 1.0, ["one_c"])

        def asel(out, in_, step, cm, cmp, r, w):
            P.op("pool", lambda e: e.affine_select(out=out, in_=in_, pattern=[[step, 128]], compare_op=cmp, fill=0.0,
                                                   base=0, channel_multiplier=cm), r, w)
        asel(ident_bf, ones_bf, 1, -1, ALU.is_equal, ["ones_bf"], ["ident_bf"])
        asel(ident_f, ones_f, 1, -1, ALU.is_equal, ["ones_f"], ["ident_f"])
        asel(mUi_bf, ones_bf, 1, -1, ALU.is_ge, ["ones_bf"], ["mUi_bf"])
        asel(triU_f, ones_f, 1, -1, ALU.is_ge, ["ones_f"], ["triU_f"])
        asel(mUs_bd, ones_f, 1, -1, ALU.is_gt, ["ones_f"], ["mUs_bd"])
        asel(mLs_bd, ones_f, -1, 1, ALU.is_gt, ["ones_f"], ["mLs_bd"])
        k.memset("pool", mUs_bd[0:64, 64:128], 0.0, ["mUs_bd"])
        k.memset("pool", mLs_bd[64:128, 0:64], 0.0, ["mLs_bd"])
        k.memset("pool", mL_off, 0.0, ["mL_off"])
        k.memset("pool", mL_off[64:128, 0:64], 1.0, ["mL_off"])
        k.dma("sp", gmix_bc, gmix_d[0:1, :].partition_broadcast(128), [], ["gmix_bc"], "cst1")
        k.dma("sp", gmlp_bc, gmlp_d[0:1, :].partition_broadcast(128), [], ["gmlp_bc"], "cst2")
        k.dma("sp", gnorm4, gnorm4_d[0:1, :].partition_broadcast(128), [], ["gnorm4"], "cst3")
        k.dma("sp", alog_bc, alog_d[0:1, :].partition_broadcast(128), [], ["alog_bc"], "cst4")
        k.dma("sp", dtb_bc, dtb_d[0:1, :].partition_broadcast(128), [], ["dtb_bc"], "cst5")
        k.dma("sp", fb_bc, fb_d[0:1, :].partition_broadcast(128), [], ["fb_bc"], "cst6")
        k.dma("sp", cw, convw_d[:, :], [], ["cw"], "cst7")
        k.dma("sp", fqg, fqg_d[:, :], [], ["fqg"], "cst8")
        k.dma("sp", fkg, fkg_d[:, :], [], ["fkg"], "cst9")

        a1 = A.mark()
        uT_flat = A.bf(8 * T)
        oaT_flat = A.bf(8 * T)
        obT_flat = A.bf(8 * T)
        a1_end = A.mark()
        uT = uT_flat.rearrange("p (c t) -> p c t", c=8)
        oaT = oaT_flat.rearrange("p (c t) -> p c t", c=8)
        obT = obT_flat.rearrange("p (c t) -> p c t", c=8)
        A.release(a1)
        h_flat = A.f32(NT * D)
        hnT_flat = A.bf(8 * T)
        assert A.mark() <= a1_end
        A.release(a1_end)
        h3 = h_flat.rearrange("p (i d) -> p i d", i=NT)
        hnT = hnT_flat.rearrange("p (c t) -> p c t", c=8)

        tabs = {}
        for nm in ["g", "G", "beta", "nbeta", "kds", "decl", "lf", "c", "tot", "cpre", "cend", "ty", "ta", "tb"]:
            tabs[nm] = A.f32(NT8)
        btab_flat = A.f32(8 * NT * NT)
        btab = btab_flat.rearrange("p (h j i) -> p h j i", h=8, j=NT)
        m0 = A.mark()

        def t3(ap):
            return ap.rearrange("p (i h) -> p i h", h=8)

        outs = []

        def chk(name):
            if stop == name:
                raise StopBuild()

        for s in range(NSEQ):
          try:
              A.release(m0)
              xt = [A.f32(D), A.f32(D)]
              junk = A.bf(D)
              ubf = [A.bf(D), A.bf(D)]
              ssc = [A.f32(1), A.f32(1)]
              for i in range(NT):
                  sl = i % 2
                  k.dma("sp", xt[sl], x_d[s, i * 128:(i + 1) * 128, :], [], ["xt%d" % sl], "xt%d" % sl)
                  k.act(junk, xt[sl], AF.Square, ["xt%d" % sl], ["junk", "ss%d" % sl], accum=ssc[sl])
                  k.act(ssc[sl], ssc[sl], AF.Ln, ["ss%d" % sl, "eps_c"], ["ss%d" % sl], scale=1.0 / D, bias=eps_c)
                  k.act(ssc[sl], ssc[sl], AF.Exp, ["ss%d" % sl], ["ss%d" % sl], scale=-0.5)
                  k.stt(ubf[sl], xt[sl], ssc[sl], gmix_bc, ALU.mult, ALU.mult, ["xt%d" % sl, "ss%d" % sl, "gmix_bc"], ["ubf%d" % sl])
                  for c in range(8):
                      k.tr(pbs(c), ubf[sl][:, c * 128:(c + 1) * 128], ident_bf, ["ubf%d" % sl, "ident_bf"], ["pf7"])
                  k.cp("act", uT[:, :, i * 128:(i + 1) * 128], pbf.rearrange("p (c t) -> p c t", c=8),
                       ["pf7"], ["uT%d" % i])
              uT_all = ["uT%d" % i for i in range(NT)]

              def uTb(b):
                  return ["uT%d" % i for i in range(b * TPB, (b + 1) * TPB)]

              if stop == "P1":
                  outs.append(k.dma("pool", dbg_d["uT"][:, :], uT_flat, uT_all, [], "dbgp_uT"))
                  break
              P.barrier()
              A.release(m0)
              wsm = A.bf(8 * 24)
              wsm3 = wsm.rearrange("p (c n) -> p c n", c=8)
              k.dma("pool", wsm, wsm_d[:, :], [], ["wsm"], "wsm")
              psS = pf[0][:, 0:NT * 24]
              for i in range(NT):
                  for c in range(8):
                      k.mm(pf[0][:, i * 24:(i + 1) * 24], uT[:, c, i * 128:(i + 1) * 128], wsm3[:, c, :], c == 0, c == 7,
                           ["uT%d" % i, "wsm"], PF(0))
              ps3 = psS.rearrange("p (i n) -> p i n", n=24)
              ga, gb, ff = ps3[:, :, 0:8], ps3[:, :, 8:16], ps3[:, :, 16:24]
              ty, ta, tb = tabs["ty"], tabs["ta"], tabs["tb"]
              k.tt("dve", t3(ty), ga, t3(dtb_bc), ALU.add, PF(0) + ["dtb_bc"], ["ty"])
              k.stt(ta, ty, -1.0, ty, ALU.mult, ALU.max, ["ty"], ["ta"])
              k.act(ta, ta, AF.Exp, ["ta"], ["ta"], scale=-1.0)
              k.act(ta, ta, AF.Ln, ["ta", "one_c"], ["ta"], bias=one_c)
              k.stt(ty, ty, 0.0, ta, ALU.max, ALU.add, ["ty", "ta"], ["ty"])
              k.act(tb, alog_bc, AF.Exp, ["alog_bc"], ["tb"])
              k.stt(tabs["g"], ty, -1.0, tb, ALU.mult, ALU.mult, ["ty", "tb"], ["g"])
              k.act(t3(tabs["beta"]), gb, AF.Sigmoid, PF(0), ["beta"])
              k.ts("dve", tabs["nbeta"], tabs["beta"], -1.0, ALU.mult, ["beta"], ["nbeta"])
              k.tt("dve", t3(ty), ff, t3(fb_bc), ALU.add, PF(0) + ["fb_bc"], ["ty"])
              k.stt(ta, ty, -1.0, ty, ALU.mult, ALU.max, ["ty"], ["ta"])
              k.act(ta, ta, AF.Exp, ["ta"], ["ta"], scale=-1.0)
              k.act(ta, ta, AF.Ln, ["ta", "one_c"], ["ta"], bias=one_c)
              k.stt(tabs["lf"], ty, 0.0, ta, ALU.min, ALU.subtract, ["ty", "ta"], ["lf"])
              k.mm(pf[1][:, 0:NT8], triU_f, tabs["g"], True, True, ["triU_f", "g"], ["pf1"])
              k.mm(pf[1][:, 128:128 + NT8], ones_f, tabs["g"], True, True, ["ones_f", "g"], ["pf1"])
              k.mm(pf[1][:, 256:256 + NT8], triU_f, tabs["lf"], True, True, ["triU_f", "lf"], ["pf1"])
              k.mm(pf[1][:, 384:384 + NT8], ones_f, tabs["lf"], True, True, ["ones_f", "lf"], ["pf1"])
              k.cp("act", tabs["G"], pf[1][:, 0:NT8], ["pf1"], ["G"])
              k.act(tabs["decl"], pf[1][:, 128:128 + NT8], AF.Exp, ["pf1"], ["decl"])
              k.tt("dve", ty, pf[1][:, 128:128 + NT8], tabs["G"], ALU.subtract, ["pf1", "G"], ["ty"])
              k.act(tabs["kds"], ty, AF.Exp, ["ty"], ["kds"])
              k.cp("act", tabs["tot"], pf[1][:, 384:384 + NT8], ["pf1"], ["tot"])
              k.memset("dve", tabs["cpre"][:, 0:8], 0.0, ["cpre"])
              for i in range(1, NT):
                  k.tt("dve", tabs["cpre"][:, i * 8:(i + 1) * 8], tabs["cpre"][:, (i - 1) * 8:i * 8],
                       tabs["tot"][:, (i - 1) * 8:i * 8], ALU.add, ["cpre", "tot"], ["cpre"])
              k.tt("dve", tabs["c"], pf[1][:, 256:256 + NT8], tabs["cpre"], ALU.add, ["pf1", "cpre"], ["c"])
              k.tt("dve", tabs["cend"], tabs["cpre"], tabs["tot"], ALU.add, ["cpre", "tot"], ["cend"])
              cend3 = t3(tabs["cend"])
              n_ = 0
              for h in range(NH):
                  for j in range(NT):
                      eng = "dve" if n_ % 2 == 0 else "pool"
                      n_ += 1
                      col = j * 8 + h
                      k.ts(eng, btab[:, h, j, :], cend3[:, :, h], tabs["c"][:, col:col + 1], ALU.subtract, ["cend", "c"], ["btab"])

              if stop == "SMALL":
                  outs.append(k.dma("pool", dbg_d["uT"][:, :], uT_flat, uT_all, [], "dbgp_uT"))
                  for ti, nm in enumerate(["g", "G", "beta", "kds", "decl", "c"]):
                      outs.append(k.dma("sp", dbg_d["tab"][:, ti * NT8:(ti + 1) * NT8], tabs[nm], [nm], [], "dbg"))
                  break
              P.barrier()
              A.release(m0)
              wg = [A.bf(8 * 512)]
              GRP = min(NT, int(os.environ.get("GDN_GRP", "4")))
              GW = GRP * 128
              NG = NT // GRP
              qT, kT, vT = A.bf(T), A.bf(T), A.bf(T)
              Kd_f, V_f, gz_f = A.bf(T), A.f32(T), A.f32(T)
              Kd = Kd_f.rearrange("p (i d) -> p i d", d=128)
              Vt = V_f.rearrange("p (i d) -> p i d", d=128)
              gz = gz_f.rearrange("p (i d) -> p i d", d=128)
              S_f = A.f32(128)
              S_bf, R_bf, vnew, on_bf, junk128 = [A.bf(128) for _ in range(5)]
              ssq = A.f32(1)
              mp = A.mark()
              sq = [A.bf(BW), A.bf(BW)]
              cb = [A.f32(BW + 3), A.f32(BW + 3)]
              acc = [A.f32(BW), A.f32(BW)]
              lnt = [A.f32(BW), A.f32(BW)]
              rs = [A.f32(BW), A.f32(BW)]
              tmpz = A.f32(BW)
              A.release(mp)
              grep_ = [A.f32(128), A.f32(128)]
              brep_ = [A.f32(128), A.f32(128)]
              E_, Y_, M1_, DuI, DuB, DlB, DlO, bbc = [A.f32(GW) for _ in range(8)]
              eG = A.bf(GW)
              Qb = [A.bf(GW), A.bf(GW)]
              Pb = [A.bf(GW), A.bf(GW)]
              Mb = [A.bf(GW), A.bf(GW)]
              Ao, Td, X1, IQb = [A.bf(GW) for _ in range(4)]
              AttnT = [A.bf(GW), A.bf(GW)]
              QeT = [A.bf(GW), A.bf(GW)]
              KeT = [A.bf(GW), A.bf(GW)]
              TpT = [A.bf(GW), A.bf(GW)]

              def fm_norm(src, src_r, dst, dst_w, ones_t, ones_r, mult, mult_r, pbank, par):
                  k.act(sq[par], src, AF.Square, src_r, ["sq%d" % par])
                  k.mm(pf[pbank][:, 0:BW], ones_t, sq[par], True, True, ["sq%d" % par] + ones_r, PF(pbank))
                  k.act(lnt[par], pf[pbank][:, 0:BW], AF.Ln, PF(pbank) + ["eps_c"], ["lnt%d" % par], bias=eps_c)
                  k.act(rs[par], lnt[par], AF.Exp, ["lnt%d" % par], ["rs%d" % par], scale=-0.5)
                  k.stt(dst, src, mult, rs[par], ALU.mult, ALU.mult, src_r + ["rs%d" % par] + mult_r, dst_w)

              pcnt = 0
              for h in range(NH):
                  ws = 0
                  wgs = wg[ws]
                  wg3 = wgs.rearrange("p (c n) -> p c n", c=8)
                  k.dma("pool", wgs, wgdn_d[h, :, :], [], ["wg%d" % ws], "wg%d" % ws)
                  for grp in range(3):
                      chunk = grp * 8 + h
                      dstT = (qT, kT, vT)[grp]
                      dnm = ("qT", "kT", "vT")[grp]
                      for b in range(NB):
                          pb = b % 2
                          pr = pcnt % 2
                          pcnt += 1
                          cbn, accn = "cb%d" % pr, "acc%d" % pr
                          cbc, accc = cb[pr], acc[pr]
                          for c in range(8):
                              k.mm(pf[pb][:, 0:BW], wg3[:, c, grp * 128:(grp + 1) * 128], uT[:, c, b * BW:(b + 1) * BW],
                                   c == 0, c == 7, ["wg%d" % ws] + uTb(b), PF(pb))
                          if b == 0:
                              k.memset("pool", cbc[:, 0:3], 0.0, [cbn])
                          else:
                              k.cp("act", cbc[:, 0:3], cb[1 - pr][:, BW:BW + 3], ["cb%d" % (1 - pr)], [cbn])
                          k.cp("act", cbc[:, 3:3 + BW], pf[pb][:, 0:BW], PF(pb), [cbn])
                          k.ts("dve", accc, cbc[:, 0:BW], cw[:, chunk * 4:chunk * 4 + 1], ALU.mult, [cbn, "cw"], [accn])
                          for tap in range(1, 4):
                              k.stt(accc, cbc[:, tap:tap + BW], cw[:, chunk * 4 + tap:chunk * 4 + tap + 1], accc, ALU.mult, ALU.add,
                                    [cbn, "cw", accn], [accn])
                          k.act(accc, accc, AF.Silu, [accn], [accn])
                          dsl = dstT[:, b * BW:(b + 1) * BW]
                          dw = ["%s%d" % (dnm, i) for i in range(b * TPB, (b + 1) * TPB)]
                          if grp < 2:
                              fm_norm(accc, [accn], dsl, dw, ones_bf, ["ones_bf"], (DH ** -0.5) if grp == 0 else 1.0, [], 2 + pr, pr)
                          else:
                              k.cp("pool", dsl, accc, [accn], dw)
                  chk("GDN_a")
                  for i in range(NT):
                      q4 = i % 4
                      for c in range(8):
                          k.mm(pq(3, q4), uT[:, c, i * 128:(i + 1) * 128], wg3[:, c, 384:512], c == 0, c == 7,
                               ["uT%d" % i, "wg%d" % ws], ["pf3"])
                      if q4 == TPB - 1 or i == NT - 1:
                          n4 = q4 + 1
                          i0 = i - q4
                          k.act(tmpz[:, 0:n4 * 128], pf[3][:, 0:n4 * 128], AF.Silu, ["pf3"], ["tmpz"])
                          k.tt("dve", gz_f[:, i0 * 128:(i0 + n4) * 128], tmpz[:, 0:n4 * 128], gnorm4[:, 0:n4 * 128], ALU.mult,
                               ["tmpz", "gnorm4"], ["gz%d" % ii for ii in range(i0, i0 + n4)])
                  chk("GDN_b")
                  for i in range(NT):
                      col = i * 8 + h
                      sk = (2 * i) % 8
                      k.tr(pbs(sk), kT[:, i * 128:(i + 1) * 128], ident_bf, ["kT%d" % i, "ident_bf"], ["pf7"])
                      k.act(Kd[:, i, :], pbs(sk), AF.Copy, ["pf7", "kds"], ["Kd%d" % i], scale=tabs["kds"][:, col:col + 1])
                      sv = (2 * i + 1) % 8
                      k.tr(pbs(sv), vT[:, i * 128:(i + 1) * 128], ident_bf, ["vT%d" % i, "ident_bf"], ["pf7"])
                      k.cp("act", Vt[:, i, :], pbs(sv), ["pf7"], ["V%d" % i])
                  chk("GDN_c")
                  P.barrier()
                  k.memset("pool", S_f, 0.0, ["S_f"])
                  k.memset("pool", S_bf, 0.0, ["S_bf"])

                  def v3(ap):
                      return ap.rearrange("p (g f) -> p g f", g=GRP)

                  def bcm(m):
                      return m.unsqueeze(1).to_broadcast([128, GRP, 128])

                  def colbc(tab, i0, h_):
                      return t3(tab)[:, i0:i0 + GRP, h_].unsqueeze(2).to_broadcast([128, GRP, 128])

                  def gs(t):
                      return slice(t * 128, (t + 1) * 128)

                  def prep(g, h=h):
                      i0 = g * GRP
                      gp = g % 2
                      b0v, b1v, b2v, b3v = [pf[b_][:, 0:GW] for b_ in range(4)]
                      for t in range(GRP):
                          col = (i0 + t) * 8 + h
                          r2 = t % 2
                          k.cp("dve", grep_[r2], tabs["g"][:, col:col + 1].to_broadcast([128, 128]), ["g"], ["grep%d" % r2])
                          k.cp("dve", brep_[r2], tabs["beta"][:, col:col + 1].to_broadcast([128, 128]), ["beta"], ["brep%d" % r2])
                          k.mm(pq(0, t), grep_[r2], triU_f, True, True, ["grep%d" % r2, "triU_f"], ["pf0"])
                          k.mm(pq(1, t), brep_[r2], ident_f, True, True, ["brep%d" % r2, "ident_f"], ["pf1"])
                      k.cp("act", bbc, b1v, ["pf1"], ["bbc"])
                      k.act(eG, b0v, AF.Exp, ["pf0"], ["eG"])
                      k.tt("dve", v3(E_), v3(b0v), colbc(tabs["G"], i0, h), ALU.subtract, ["pf0", "G"], ["E"])
                      k.ts("dve", Y_, E_, 0.0, ALU.max, ["E"], ["Y"])
                      k.ts("dve", E_, E_, 0.0, ALU.min, ["E"], ["E"])
                      k.act(Y_, Y_, AF.Exp, ["Y"], ["Y"], scale=-1.0)
                      k.act(E_, E_, AF.Exp, ["E"], ["E"])
                      k.tt("pool", v3(M1_), v3(bbc), bcm(mUs_bd), ALU.mult, ["bbc", "mUs_bd"], ["M1"])
                      k.tt("dve", v3(DuI), v3(E_), bcm(triU_f), ALU.mult, ["E", "triU_f"], ["DuI"])
                      k.tt("dve", DuB, E_, M1_, ALU.mult, ["E", "M1"], ["DuB"])
                      k.tt("dve", v3(DlB), v3(Y_), bcm(mLs_bd), ALU.mult, ["Y", "mLs_bd"], ["DlB"])
                      k.tt("pool", v3(DlB), v3(DlB), colbc(tabs["beta"], i0, h), ALU.mult, ["DlB", "beta"], ["DlB"])
                      k.tt("dve", v3(DlO), v3(Y_), bcm(mL_off), ALU.mult, ["Y", "mL_off"], ["DlO"])
                      k.tt("pool", v3(DlO), v3(DlO), colbc(tabs["beta"], i0, h), ALU.mult, ["DlO", "beta"], ["DlO"])
                      yield
                      for t in range(GRP):
                          i = i0 + t
                          kTi = kT[:, i * 128:(i + 1) * 128]
                          qTi = qT[:, i * 128:(i + 1) * 128]
                          k.mm(pq(2, t), kTi, kTi, True, True, ["kT%d" % i], ["pf2"])
                          k.mm(pq(3, t), kTi, qTi, True, True, ["kT%d" % i, "qT%d" % i], ["pf3"])
                      kr = ["kT%d" % (i0 + t) for t in range(GRP)]
                      qr = ["qT%d" % (i0 + t) for t in range(GRP)]
                      k.tt("dve", Qb[0], b2v, DlB, ALU.mult, ["pf2", "DlB"], ["Q0"])
                      k.tt("dve", Ao, b2v, DlO, ALU.mult, ["pf2", "DlO"], ["Ao"])
                      k.tt("dve", Pb[0], b2v, DuB, ALU.mult, ["pf2", "DuB"], ["P0"])
                      k.tt("dve", AttnT[gp], b3v, DuI, ALU.mult, ["pf3", "DuI"], ["AttnT%d" % gp])
                      k.tt("dve", v3(Mb[0]), bcm(ident_bf), v3(Pb[0]), ALU.subtract, ["ident_bf", "P0"], ["M0"])
                      k.tt("dve", QeT[gp], qT[:, i0 * 128:i0 * 128 + GW], eG, ALU.mult, qr + ["eG"], ["QeT%d" % gp])
                      k.tt("dve", KeT[gp], kT[:, i0 * 128:i0 * 128 + GW], eG, ALU.mult, kr + ["eG"], ["KeT%d" % gp])
                      cur = 0
                      for lvl in range(1, 6):
                          nxt = 1 - cur
                          for t in range(GRP):
                              if lvl < 5:
                                  k.mm(pq(0, t), Qb[cur][:, gs(t)], Pb[cur][:, gs(t)], True, True, ["Q%d" % cur, "P%d" % cur], ["pf0"])
                              k.mm(pq(1, t), Pb[cur][:, gs(t)], Qb[cur][:, gs(t)], True, True, ["Q%d" % cur, "P%d" % cur], ["pf1"])
                          if lvl < 5:
                              k.cp("act", Pb[nxt], b0v, ["pf0"], ["P%d" % nxt])
                          k.cp("dve", Qb[nxt], b1v, ["pf1"], ["Q%d" % nxt])
                          k.tt("dve", v3(IQb), v3(Qb[nxt]), bcm(ident_bf), ALU.add, ["Q%d" % nxt, "ident_bf"], ["IQb"])
                          for t in range(GRP):
                              k.mm(pq(2, t), IQb[:, gs(t)], Mb[cur][:, gs(t)], True, True, ["IQb", "M%d" % cur], ["pf2"])
                          k.cp("act", Mb[nxt], b2v, ["pf2"], ["M%d" % nxt])
                          if lvl == 5:
                              k.cp("dve", Y_, b2v, ["pf2"], ["Y"])
                          cur = nxt
                          if lvl == 2:
                              yield
                      yield
                      Ud = Mb[cur]
                      Udn = "M%d" % cur
                      for t in range(GRP):
                          k.tr(pbs(t), Ud[:, gs(t)], ident_bf, [Udn, "ident_bf"], ["pf7"])
                      k.cp("act", Td, pbf[:, 0:GW], ["pf7"], ["Td"])
                      for t in range(GRP):
                          k.mm(pq(3, t), Ao[:, gs(t)], Ud[:, gs(t)], True, True, ["Ao", Udn], ["pf3"])
                      k.cp("act", X1, b3v, ["pf3"], ["X1"])
                      for t in range(GRP):
                          k.mm(pq(0, t), Td[:, gs(t)], X1[:, gs(t)], True, True, ["Td", "X1"], ["pf0"])
                      k.tt("dve", Y_, Y_, b0v, ALU.subtract, ["Y", "pf0"], ["Y"])
                      k.tt("pool", v3(TpT[gp]), v3(Y_), colbc(tabs["beta"], i0, h), ALU.mult, ["Y", "beta"], ["TpT%d" % gp])
                      yield

                  def scan_step(i, h=h):
                      g = i // GRP
                      t = i % GRP
                      gp = g % 2
                      d2 = i % 2
                      col = i * 8 + h
                      k.mm(pq(4, 0), KeT[gp][:, gs(t)], S_bf, True, True, ["KeT%d" % gp, "S_bf"], ["pf4"])
                      k.tt("dve", R_bf, Vt[:, i, :], pq(4, 0), ALU.subtract, ["V%d" % i, "pf4"], ["R_bf"])
                      k.mm(pq(4, 1), TpT[gp][:, gs(t)], R_bf, True, True, ["TpT%d" % gp, "R_bf"], ["pf4"])
                      k.cp("act", vnew, pq(4, 1), ["pf4"], ["vnew"])
                      k.mm(pq(5 + d2, 0), QeT[gp][:, gs(t)], S_bf, True, False, ["QeT%d" % gp, "S_bf"], ["pf%d" % (5 + d2)])
                      k.mm(pq(5 + d2, 0), AttnT[gp][:, gs(t)], vnew, False, True, ["AttnT%d" % gp, "vnew"], ["pf%d" % (5 + d2)])
                      k.mm(pq(4, 3), Kd[:, i, :], vnew, True, True, ["Kd%d" % i, "vnew"], ["pf4"])
                      k.stt(S_f, S_f, tabs["decl"][:, col:col + 1], pq(4, 3), ALU.mult, ALU.add, ["S_f", "decl", "pf4"], ["S_f"])
                      k.cp("act", S_bf, S_f, ["S_f"], ["S_bf"])
                      k.act(junk128, pq(5 + d2, 0), AF.Square, ["pf%d" % (5 + d2)], ["junk128", "ssq"], accum=ssq)
                      k.act(ssq, ssq, AF.Ln, ["ssq", "eps_c"], ["ssq"], scale=1.0 / DH, bias=eps_c)
                      k.act(ssq, ssq, AF.Exp, ["ssq"], ["ssq"], scale=-0.5)
                      k.stt(on_bf, pq(5 + d2, 0), ssq, gz[:, i, :], ALU.mult, ALU.mult, ["pf%d" % (5 + d2), "ssq", "gz%d" % i], ["on_bf"])
                      so = 4 + (i % 4)
                      k.tr(pbs(so), on_bf, ident_bf, ["on_bf", "ident_bf"], ["pf7"])
                      k.cp("act", oaT[:, h, i * 128:(i + 1) * 128], pbs(so), ["pf7"], ["oaT%d_%d" % (h, i)])

                  for _ in prep(0):
                      pass
                  NST = 4
                  per = (NST + GRP - 1) // GRP
                  for g in range(NG):
                      gen = prep(g + 1) if g + 1 < NG else iter(())
                      for t in range(GRP):
                          scan_step(g * GRP + t)
                          for _ in range(per):
                              next(gen, None)
                      for _ in gen:
                          pass
                  P.barrier()
                  chk("GDN_h0")

              chk("GDN2")
              P.barrier()
              A.release(m0)
              wf = [A.bf(8 * 384), A.bf(8 * 384)]
              fqT, fkT, fV_f = A.bf(T), A.bf(T), A.bf(T)
              fV = fV_f.rearrange("p (i d) -> p i d", d=128)
              sq = [A.bf(BW), A.bf(BW)]
              lnt = [A.f32(BW), A.f32(BW)]
              rs = [A.f32(BW), A.f32(BW)]
              wff = A.f32(8 * 384)
              PT = [A.bf(BW), A.bf(BW)]
              rden = [A.f32(BW), A.f32(BW)]
              scale = DH ** -0.5
              for h in range(NH):
                  ws = h % 2
                  wfs = wf[ws]
                  wf3 = wfs.rearrange("p (c n) -> p c n", c=8)
                  k.dma("sp", wff, wfox_d[h, :, :], [], ["wff"], "wff")
                  for cc in range(3):
                      k.cp("pool", wfs[:, cc * 1024:(cc + 1) * 1024], wff[:, cc * 1024:(cc + 1) * 1024], ["wff"], ["wf%d" % ws])
                  for grp in range(2):
                      dstT = (fqT, fkT)[grp]
                      dnm = ("fqT", "fkT")[grp]
                      gcol = (fqg, fkg)[grp]
                      gnm = ("fqg", "fkg")[grp]
                      for b in range(NB):
                          pb = b % 2
                          for c in range(8):
                              k.mm(pf[pb][:, 0:BW], wf3[:, c, grp * 128:(grp + 1) * 128], uT[:, c, b * BW:(b + 1) * BW],
                                   c == 0, c == 7, ["wf%d" % ws] + uTb(b), PF(pb))
                          dw = ["%s%d" % (dnm, i) for i in range(b * TPB, (b + 1) * TPB)]
                          pr = pcnt % 2
                          pcnt += 1
                          fm_norm(pf[pb][:, 0:BW], PF(pb), dstT[:, b * BW:(b + 1) * BW], dw, onesm_bf, ["onesm_bf"],
                                  gcol, [gnm], (2, 7)[pr], pr)
                  for i in range(NT):
                      q4 = i % 4
                      for c in range(8):
                          k.mm(pq(5, q4), uT[:, c, i * 128:(i + 1) * 128], wf3[:, c, 256:384], c == 0, c == 7,
                               ["uT%d" % i, "wf%d" % ws], ["pf5"])
                      if q4 == TPB - 1 or i == NT - 1:
                          n4 = q4 + 1
                          i0 = i - q4
                          k.cp("act", fV_f[:, i0 * 128:(i0 + n4) * 128], pf[5][:, 0:n4 * 128], ["pf5"],
                               ["fV%d" % ii for ii in range(i0, i0 + n4)])
                  cnt = 0
                  for Ib in range(NB):
                      bo, bd = ((5, 6), (0, 1))[Ib % 2]
                      last_j = (Ib + 1) * TPB - 1
                      for j in range(last_j + 1):
                          r0 = max(0, j - Ib * TPB)
                          qoff = r0 * 128
                          sb_ = 3 + (cnt % 2)
                          ptb = cnt % 2
                          cnt += 1
                          qr = ["fqT%d" % (Ib * TPB + r) for r in range(r0, TPB)]
                          k.mm(pf[sb_][:, qoff:BW], fkT[:, j * 128:(j + 1) * 128], fqT[:, Ib * BW + qoff:(Ib + 1) * BW], True, True,
                               ["fkT%d" % j] + qr, PF(sb_))
                          for r in range(r0, TPB):
                              i = Ib * TPB + r
                              k.act(PT[ptb][:, r * 128:(r + 1) * 128], pf[sb_][:, r * 128:(r + 1) * 128], AF.Exp,
                                    PF(sb_) + ["btab"], ["PT%d" % ptb], scale=scale, bias=btab[:, h, j, i:i + 1])
                              if i == j:
                                  k.tt("dve", PT[ptb][:, r * 128:(r + 1) * 128], PT[ptb][:, r * 128:(r + 1) * 128], mUi_bf, ALU.mult,
                                       ["PT%d" % ptb, "mUi_bf"], ["PT%d" % ptb])
                          k.mm(pf[bo][:, qoff:BW], fV[:, j, :], PT[ptb][:, qoff:BW], j == 0, j == last_j,
                               ["fV%d" % j, "PT%d" % ptb], PF(bo))
                          k.mm(pf[bd][:, qoff:BW], ones_bf, PT[ptb][:, qoff:BW], j == 0, j == last_j,
                               ["ones_bf", "PT%d" % ptb], PF(bd))
                      rdn = rden[Ib % 2]
                      k.P.op("dve", lambda e, rdn=rdn, bd=bd: e.reciprocal(out=rdn, in_=pf[bd][:, 0:BW]), PF(bd), ["rden%d" % (Ib % 2)])
                      k.tt("dve", obT[:, h, Ib * BW:(Ib + 1) * BW], pf[bo][:, 0:BW], rdn, ALU.mult, PF(bo) + ["rden%d" % (Ib % 2)],
                           ["obT%d_%d" % (h, Ib)])

              if dbg and s == 0:
                  outs.append(k.dma("pool", dbg_d["uT"][:, :], uT_flat, uT_all, [], "dbgp_uT"))
                  outs.append(k.dma("pool", dbg_d["oaT"][:, :], oaT_flat, ["oaT%d_%d" % (h, i) for h in range(NH) for i in range(NT)], [], "dbgp_oaT"))
                  outs.append(k.dma("pool", dbg_d["obT"][:, :], obT_flat, ["obT%d_%d" % (h, b) for h in range(NH) for b in range(NB)], [], "dbgp_obT"))
                  for ti, nm in enumerate(["g", "G", "beta", "kds", "decl", "c"]):
                      outs.append(k.dma("sp", dbg_d["tab"][:, ti * NT8:(ti + 1) * NT8], tabs[nm], [nm], [], "dbg"))

              chk("FOX")
              P.barrier()
              A.release(m0)
              mT_flat = A.bf(8 * T)
              mT = mT_flat.rearrange("p (c t) -> p c t", c=8)
              m_mT = A.mark()
              wga = [A.bf(8 * 256), A.bf(8 * 256)]
              wpa = [A.bf(8 * 256), A.bf(8 * 256)]
              sA, sB, mA, mB = [A.f32(BW) for _ in range(4)]
              wgaf = A.f32(8 * 256)
              wpaf = A.f32(8 * 256)
              oa_all = {b: ["oaT%d_%d" % (h, i) for h in range(NH) for i in range(b * TPB, (b + 1) * TPB)] for b in range(NB)}
              ob_all = {b: ["obT%d_%d" % (h, b) for h in range(NH)] for b in range(NB)}
              for n in range(8):
                  ws = n % 2
                  k.dma("sp", wgaf, wgate_d[n, :, :], [], ["wgaf"], "wgaf")
                  k.dma("sp", wpaf, wpab_d[n, :, :], [], ["wpaf"], "wpaf")
                  for cc in range(2):
                      k.cp("pool", wga[ws][:, cc * 1024:(cc + 1) * 1024], wgaf[:, cc * 1024:(cc + 1) * 1024], ["wgaf"], ["wga%d" % ws])
                      k.cp("pool", wpa[ws][:, cc * 1024:(cc + 1) * 1024], wpaf[:, cc * 1024:(cc + 1) * 1024], ["wpaf"], ["wpa%d" % ws])
                  wga3 = wga[ws].rearrange("p (c n) -> p c n", c=8)
                  wpa3 = wpa[ws].rearrange("p (c n) -> p c n", c=8)
                  for b in range(NB):
                      bs = slice(b * BW, (b + 1) * BW)
                      for c in range(8):
                          k.mm(pf[0][:, 0:BW], wpa3[:, c, 0:128], oaT[:, c, bs], c == 0, c == 7, ["wpa%d" % ws] + oa_all[b], PF(0))
                      for c in range(8):
                          k.mm(pf[1][:, 0:BW], wga3[:, c, 0:128], uT[:, c, bs], c == 0, c == 7, ["wga%d" % ws] + uTb(b), PF(1))
                      for c in range(8):
                          k.mm(pf[2][:, 0:BW], wpa3[:, c, 128:256], obT[:, c, bs], c == 0, c == 7, ["wpa%d" % ws] + ob_all[b], PF(2))
                      for c in range(8):
                          k.mm(pf[3][:, 0:BW], wga3[:, c, 128:256], uT[:, c, bs], c == 0, c == 7, ["wga%d" % ws] + uTb(b), PF(3))
                      k.act(sA, pf[1][:, 0:BW], AF.Sigmoid, PF(1), ["sA"])
                      k.act(sB, pf[3][:, 0:BW], AF.Sigmoid, PF(3), ["sB"])
                      k.tt("dve", mA, pf[0][:, 0:BW], sA, ALU.mult, PF(0) + ["sA"], ["mA"])
                      k.tt("dve", mB, pf[2][:, 0:BW], sB, ALU.mult, PF(2) + ["sB"], ["mB"])
                      k.tt("dve", mT[:, n, bs], mA, mB, ALU.add, ["mA", "mB"], ["mT%d_%d" % (n, b)])

              P.barrier()
              A.release(m_mT)
              wo = A.bf(8 * 1024)
              wo3 = wo.rearrange("p (c n) -> p c n", c=8)
              xt = [A.f32(D), A.f32(D)]
              hnb = [A.bf(D), A.bf(D)]
              junk = A.bf(D)
              ssc = [A.f32(1), A.f32(1)]
              k.dma("pool", wo, wout_d[:, :], [], ["wo"], "wo")
              for i in range(NT):
                  sl = i % 2
                  k.dma("sp", xt[sl], x_d[s, i * 128:(i + 1) * 128, :], [], ["xt%d" % sl], "xt%d" % sl)
                  for half in range(2):
                      for c in range(8):
                          k.mm(pf[half][:, :], mT[:, c, i * 128:(i + 1) * 128], wo3[:, c, half * 512:(half + 1) * 512], c == 0, c == 7,
                               ["wo"] + ["mT%d_%d" % (n_, i // TPB) for n_ in range(8)], PF(half))
                      k.tt("dve", h3[:, i, half * 512:(half + 1) * 512], pf[half][:, :], xt[sl][:, half * 512:(half + 1) * 512], ALU.add,
                           PF(half) + ["xt%d" % sl], ["h%d_%d" % (i, half)])
                  hr = ["h%d_0" % i, "h%d_1" % i]
                  k.act(junk, h3[:, i, :], AF.Square, hr, ["junk", "ss%d" % sl], accum=ssc[sl])
                  k.act(ssc[sl], ssc[sl], AF.Ln, ["ss%d" % sl, "eps_c"], ["ss%d" % sl], scale=1.0 / D, bias=eps_c)
                  k.act(ssc[sl], ssc[sl], AF.Exp, ["ss%d" % sl], ["ss%d" % sl], scale=-0.5)
                  k.stt(hnb[sl], h3[:, i, :], ssc[sl], gmlp_bc, ALU.mult, ALU.mult, hr + ["ss%d" % sl, "gmlp_bc"], ["hnb%d" % sl])
                  for c in range(8):
                      k.tr(pbs(c), hnb[sl][:, c * 128:(c + 1) * 128], ident_bf, ["hnb%d" % sl, "ident_bf"], ["pf7"])
                  k.cp("act", hnT[:, :, i * 128:(i + 1) * 128], pbf.rearrange("p (c t) -> p c t", c=8),
                       ["pf7"], ["hnT%d" % i])
              if dbg and s == 0:
                  outs.append(k.dma("sp", dbg_d["h"][:, :], h_flat, ["h%d_%d" % (i, hf) for i in range(NT) for hf in range(2)], [], "dbg"))

              chk("POSTH")
              P.barrier()
              A.release(m0)
              aT_flat = A.bf(32 * BW)
              aT = aT_flat.rearrange("p (f t) -> p f t", f=32)
              wup = [A.bf(1024) for _ in range(4)]
              wdn = [A.bf(1024) for _ in range(4)]
              wstg = [A.f32(1024) for _ in range(3)]
              scnt = 0
              rtmp = [A.f32(BW), A.f32(BW)]
              otile = [A.f32(D), A.f32(D)]
              for b in range(NB):
                  bs = slice(b * BW, (b + 1) * BW)
                  hb = ["hnT%d" % i for i in range(b * TPB, (b + 1) * TPB)]
                  for f in range(32):
                      ws = f % 4
                      sg = scnt % 3
                      scnt += 1
                      k.dma("sp", wstg[sg], wup_d[f, :, :], [], ["wstg%d" % sg], "wstg%d" % sg)
                      k.cp("pool", wup[ws], wstg[sg], ["wstg%d" % sg], ["wup%d" % ws])
                      wu3 = wup[ws].rearrange("p (c n) -> p c n", c=8)
                      pb = f % 2
                      for c in range(8):
                          k.mm(pf[pb][:, 0:BW], wu3[:, c, :], hnT[:, c, bs], c == 0, c == 7, ["wup%d" % ws] + hb, PF(pb))
                      k.act(rtmp[pb], pf[pb][:, 0:BW], AF.Relu, PF(pb), ["rtmp%d" % pb])
                      k.tt("dve", aT[:, f, :], rtmp[pb], rtmp[pb], ALU.mult, ["rtmp%d" % pb], ["aT%d" % f])
                  for f in range(32):
                      ws = f % 4
                      sg = scnt % 3
                      scnt += 1
                      k.dma("sp", wstg[sg], wdn_d[f, :, :], [], ["wstg%d" % sg], "wstg%d" % sg)
                      k.cp("pool", wdn[ws], wstg[sg], ["wstg%d" % sg], ["wdn%d" % ws])
                      for t in range(TPB):
                          for half in range(2):
                              bk = t * 2 + half
                              k.mm(pf[bk][:, :], aT[:, f, t * 128:(t + 1) * 128], wdn[ws][:, half * 512:(half + 1) * 512], f == 0, f == 31,
                                   ["aT%d" % f, "wdn%d" % ws], PF(bk))
                  for t in range(TPB):
                      i = b * TPB + t
                      sl = i % 2
                      for half in range(2):
                          bk = t * 2 + half
                          k.tt("dve", otile[sl][:, half * 512:(half + 1) * 512], pf[bk][:, :], h3[:, i, half * 512:(half + 1) * 512], ALU.add,
                               PF(bk) + ["h%d_%d" % (i, half)], ["ot%d_%d" % (sl, half)])
                      outs.append(k.dma("sp", out_d[s, i * 128:(i + 1) * 128, :], otile[sl], ["ot%d_0" % sl, "ot%d_1" % sl], [], "out%d" % sl))
              P.barrier()

          except StopBuild:
            outs.append(k.dma("pool", dbg_d["uT"][:, :], uT_flat, [], [], "dbgp_stop"))
            break
        P.final_waits += outs
        nw = P.emit(st)
        print("ops", len(P.ops), "waits", nw, "arena peak", A.peak)
    return nc


def prep_weights(inp, T):
    NT = T // 128
    f = lambda a: np.ascontiguousarray(np.asarray(a, dtype=np.float32))
    w_in = np.asarray(inp["w_in"])[0]
    offs = np.cumsum([0, 1024, 1024, 1024, 1024, 8, 8, 1024, 1024, 1024, 8, 1024, 1024])
    gq, gk, gv, gz, ga, gb, fq, fk, fv, ff, gta, gtb = [w_in[:, offs[i]:offs[i + 1]] for i in range(12)]

    def pc(w):
        return w.reshape(8, 128, -1).transpose(1, 0, 2)
    w_gdn = np.stack([pc(np.concatenate([g_[:, h * 128:(h + 1) * 128] for g_ in (gq, gk, gv, gz)], 1)).reshape(128, -1) for h in range(8)])
    w_fox = np.stack([pc(np.concatenate([g_[:, h * 128:(h + 1) * 128] for g_ in (fq, fk, fv)], 1)).reshape(128, -1) for h in range(8)])
    w_small = pc(np.concatenate([ga, gb, ff], 1)).reshape(128, -1)
    w_gate = np.stack([pc(np.concatenate([gta[:, n * 128:(n + 1) * 128], gtb[:, n * 128:(n + 1) * 128]], 1)).reshape(128, -1) for n in range(8)])
    pa = np.asarray(inp["w_proj_gdn"])[0]
    pb = np.asarray(inp["w_proj_fox"])[0]
    w_pab = np.stack([pc(np.concatenate([pa[:, n * 128:(n + 1) * 128], pb[:, n * 128:(n + 1) * 128]], 1)).reshape(128, -1) for n in range(8)])
    w_o = pc(np.asarray(inp["w_out"])[0]).reshape(128, -1)
    wu = np.asarray(inp["w_up"])[0]
    w_up = np.stack([pc(wu[:, fch * 128:(fch + 1) * 128]).reshape(128, -1) for fch in range(32)])
    w_down = np.asarray(inp["w_down"])[0].reshape(32, 128, 1024)
    cwt = np.asarray(inp["gdn_conv_w"])[0]
    convw = cwt.reshape(4, 24, 128).transpose(2, 1, 0).reshape(128, 96)
    d = {
        "w_gdn": w_gdn, "w_fox": w_fox, "w_small": w_small, "w_gate": w_gate, "w_pab": w_pab, "w_o": w_o,
        "w_up": w_up, "w_down": w_down, "convw": convw,
        "gmix": np.asarray(inp["norm_mix_g"])[0][None, :], "gmlp": np.asarray(inp["norm_mlp_g"])[0][None, :],
        "gnorm4": np.tile(np.asarray(inp["gdn_norm_g"])[0], 4)[None, :],
        "alog_t": np.tile(np.asarray(inp["gdn_a_log"])[0], NT)[None, :],
        "dtb_t": np.tile(np.asarray(inp["gdn_dt_bias"])[0], NT)[None, :],
        "fb_t": np.tile(np.asarray(inp["fox_f_bias"])[0], NT)[None, :],
        "fqg": np.asarray(inp["fox_q_norm_g"])[0][:, None], "fkg": np.asarray(inp["fox_k_norm_g"])[0][:, None],
    }
    return {k_: f(v) for k_, v in d.items()}


_NC_CACHE = {}


def kernel(**inputs):
    x = np.asarray(inputs["x"], dtype=np.float32)
    Bt, T, _ = x.shape
    ncores = 8
    nseq = Bt // ncores
    wd = prep_weights(inputs, T)
    key = (T, nseq)
    if key not in _NC_CACHE:
        nc = bass.Bass("TRN2", target_bir_lowering=False)
        build(nc, T, nseq)
        _NC_CACHE[key] = nc
    nc = _NC_CACHE[key]
    in_maps = []
    for c in range(ncores):
        m = dict(wd)
        m["x"] = np.ascontiguousarray(x[c * nseq:(c + 1) * nseq])
        in_maps.append(m)
    res = run_bass_kernel_spmd(nc, in_maps, core_ids=list(range(ncores)))
    out = np.concatenate([res.results[c]["out"] for c in range(ncores)], axis=0)
    return out.astype(np.float32)
```

```python
import contextlib
import concourse.bass as bass
import concourse.mybir as mybir

F32 = mybir.dt.float32
BF16 = mybir.dt.bfloat16
AF = mybir.ActivationFunctionType
ALU = mybir.AluOpType
AX = mybir.AxisListType

ENGS = ("pe", "act", "dve", "pool", "sp")


class Op:
    __slots__ = ("eng", "fn", "deps", "dma", "idx", "sig", "signal")

    def __init__(self, eng, fn, dma):
        self.eng = eng
        self.fn = fn
        self.deps = []
        self.dma = dma
        self.sig = None
        self.signal = False


class Prog:
    def __init__(self, nc):
        self.nc = nc
        self.ops = []
        self.last_w = {}
        self.readers = {}
        self.final_waits = []

    def op(self, eng, fn, reads=(), writes=(), dma=None):
        o = Op(eng, fn, dma)
        o.idx = len(self.ops)
        pfr = [r for r in reads if r.startswith("pf")]
        if pfr:
            reads = [r for r in reads if not r.startswith("pf")]
            writes = list(writes) + [r for r in pfr if r not in writes]
        deps = set()
        for r in reads:
            w = self.last_w.get(r)
            if w is not None:
                deps.add(w)
        for wkey in writes:
            w = self.last_w.get(wkey)
            if w is not None:
                deps.add(w)
            for rd in self.readers.get(wkey, ()):
                deps.add(rd)
        for d in deps:
            if d is o:
                continue
            if d.dma is None and d.eng == eng:
                if eng == "pe":
                    continue
            o.deps.append(d)
        for r in reads:
            self.readers.setdefault(r, []).append(o)
        for wkey in writes:
            self.last_w[wkey] = o
            self.readers[wkey] = []
        self.ops.append(o)
        return o

    def barrier(self):
        last = {}
        for o in self.ops:
            if o.fn is None:
                continue
            k = ("dma", o.dma) if o.dma is not None else ("eng", o.eng)
            last[k] = o
        for e in ENGS:
            b = Op(e, None, None)
            b.idx = len(self.ops)
            b.deps = [d for k, d in last.items() if not (k == ("eng", e))]
            self.ops.append(b)
        self.last_w = {}
        self.readers = {}

    def emit(self, stack):
        nc = self.nc
        for o in self.ops:
            for d in o.deps:
                d.signal = True
        for o in self.final_waits:
            o.signal = True
        cnt = {}
        for o in self.ops:
            if o.dma is not None:
                k = ("dma", o.dma)
                cnt[k] = cnt.get(k, 0) + 16
                o.sig = (k, cnt[k])
            elif o.signal:
                k = ("eng", o.eng)
                cnt[k] = cnt.get(k, 0) + 1
                o.sig = (k, cnt[k])
        sems = {}
        for k in cnt:
            nm = "s_" + "_".join(str(x) for x in k)
            sems[k] = stack.enter_context(nc.semaphore(nm))
        per_eng = {e: [o for o in self.ops if o.eng == e] for e in ENGS}
        finals = list(self.final_waits)
        block = stack.enter_context(nc.Block())
        nwaits = [0]

        def run(eng_name, eobj):
            waited = {}
            for o in per_eng[eng_name]:
                need = {}
                for d in o.deps:
                    k, v = d.sig
                    if waited.get(k, 0) >= v:
                        continue
                    if need.get(k, 0) < v:
                        need[k] = v
                for k, v in need.items():
                    eobj.wait_ge(sems[k], v)
                    waited[k] = v
                    nwaits[0] += 1
                if o.fn is None:
                    continue
                ins = o.fn(eobj)
                if o.dma is not None:
                    ins.then_inc(sems[o.sig[0]], 16)
                elif o.signal:
                    ins.then_inc(sems[o.sig[0]], 1)
            if eng_name == "sp":
                need = {}
                for d in finals:
                    k, v = d.sig
                    if need.get(k, 0) < v:
                        need[k] = v
                for k, v in need.items():
                    if waited.get(k, 0) < v:
                        eobj.wait_ge(sems[k], v)

        @block.tensor
        def _(e):
            run("pe", e)

        @block.scalar
        def _(e):
            run("act", e)

        @block.vector
        def _(e):
            run("dve", e)

        @block.gpsimd
        def _(e):
            run("pool", e)

        @block.sync
        def _(e):
            run("sp", e)
        return nwaits[0]

import numpy as np
import os
from concourse.bass_utils import run_bass_kernel_spmd

D = 1024
NH = 8
DH = 128
DFF = 4096
EPS = 1e-6
ARENA = 105600


class StopBuild(Exception):
    pass


class Arena:
    def __init__(self, t):
        self.t = t
        self.off = 0
        self.peak = 0

    def bf(self, n):
        n2 = (n + 15) // 16 * 16
        o = self.off
        self.off += n2
        self.peak = max(self.peak, self.off)
        assert self.off <= ARENA, ("arena overflow", self.off)
        return self.t[:, o:o + n]

    def f32(self, n):
        n2 = (2 * n + 15) // 16 * 16
        o = self.off
        self.off += n2
        self.peak = max(self.peak, self.off)
        assert self.off <= ARENA, ("arena overflow", self.off)
        return self.t[:, o:o + 2 * n].bitcast(F32)

    def mark(self):
        return self.off

    def release(self, m):
        self.off = m


class K:
    def __init__(self, P):
        self.P = P

    def mm(self, out, lhsT, rhs, start, stop, r, w):
        return self.P.op("pe", lambda e: e.matmul(out, lhsT=lhsT, rhs=rhs, start=start, stop=stop), r, w)

    def tr(self, out, in_, ident, r, w):
        return self.P.op("pe", lambda e: e.transpose(out=out, in_=in_, identity=ident), r, w)

    def act(self, out, in_, func, r, w, scale=None, bias=None, accum=None):
        kw = {}
        if scale is not None:
            kw["scale"] = scale
        if bias is not None:
            kw["bias"] = bias
        if accum is not None:
            kw["accum_out"] = accum
        return self.P.op("act", lambda e: e.activation(out=out, in_=in_, func=func, **kw), r, w)

    def cp(self, eng, out, in_, r, w):
        if eng == "act":
            return self.P.op("act", lambda e: e.copy(out=out, in_=in_), r, w)
        return self.P.op(eng, lambda e: e.tensor_copy(out=out, in_=in_), r, w)

    def tt(self, eng, out, in0, in1, op, r, w):
        return self.P.op(eng, lambda e: e.tensor_tensor(out=out, in0=in0, in1=in1, op=op), r, w)

    def ts(self, eng, out, in0, s1, op0, r, w, s2=None, op1=None):
        if op1 is None:
            return self.P.op(eng, lambda e: e.tensor_scalar(out=out, in0=in0, scalar1=s1, scalar2=None, op0=op0), r, w)
        return self.P.op(eng, lambda e: e.tensor_scalar(out=out, in0=in0, scalar1=s1, scalar2=s2, op0=op0, op1=op1), r, w)

    def stt(self, out, in0, scalar, in1, op0, op1, r, w):
        return self.P.op("dve", lambda e: e.scalar_tensor_tensor(out=out, in0=in0, scalar=scalar, in1=in1, op0=op0, op1=op1), r, w)

    def memset(self, eng, ap, val, w):
        return self.P.op(eng, lambda e: e.memset(ap, val), (), w)

    def dma(self, eng, out, in_, r, w, key):
        return self.P.op(eng, lambda e: e.dma_start(out=out, in_=in_), r, w, dma=key)


def build(nc, T, NSEQ, dbg=False, stop=None):
    NT = T // 128
    BW = min(512, T)
    NB = T // BW
    TPB = BW // 128
    NT8 = NT * 8
    dram = lambda n, s, kind="ExternalInput": nc.dram_tensor(n, list(s), F32, kind=kind).ap()
    x_d = dram("x", [NSEQ, T, D])
    out_d = dram("out", [NSEQ, T, D], "ExternalOutput")
    wgdn_d = dram("w_gdn", [NH, 128, 8 * 512])
    wfox_d = dram("w_fox", [NH, 128, 8 * 384])
    wsm_d = dram("w_small", [128, 8 * 24])
    wgate_d = dram("w_gate", [8, 128, 8 * 256])
    wpab_d = dram("w_pab", [8, 128, 8 * 256])
    wout_d = dram("w_o", [128, 8 * 1024])
    wup_d = dram("w_up", [32, 128, 8 * 128])
    wdn_d = dram("w_down", [32, 128, 1024])
    convw_d = dram("convw", [128, 24 * 4])
    gmix_d = dram("gmix", [1, D])
    gmlp_d = dram("gmlp", [1, D])
    gnorm4_d = dram("gnorm4", [1, 512])
    alog_d = dram("alog_t", [1, NT8])
    dtb_d = dram("dtb_t", [1, NT8])
    fb_d = dram("fb_t", [1, NT8])
    fqg_d = dram("fqg", [128, 1])
    fkg_d = dram("fkg", [128, 1])
    dbg_d = {}
    if dbg:
        dbg_d["uT"] = dram("d_uT", [128, 8 * T], "ExternalOutput")
        dbg_d["oaT"] = dram("d_oaT", [128, 8 * T], "ExternalOutput")
        dbg_d["obT"] = dram("d_obT", [128, 8 * T], "ExternalOutput")
        dbg_d["tab"] = dram("d_tab", [128, 6 * NT8], "ExternalOutput")
        dbg_d["h"] = dram("d_h", [128, NT * D], "ExternalOutput")

    with contextlib.ExitStack() as st:
        P = Prog(nc)
        k = K(P)
        arena_t = st.enter_context(nc.sbuf_tensor("arena", [128, ARENA], BF16))
        A = Arena(arena_t)
        pf = [st.enter_context(nc.psum_tensor("pf%d" % i, [128, 512], F32)) for i in range(8)]
        pbf = pf[7][:, :].bitcast(BF16)

        def PF(b, c0=0, c1=512):
            return ["pf%d" % b]

        def pq(b, q, n=128):
            return pf[b][:, q * 128:q * 128 + n]

        def pbs(s):
            return pbf[:, s * 128:(s + 1) * 128]

        ident_bf = A.bf(128)
        ones_bf = A.bf(128)
        onesm_bf = A.bf(128)
        mUi_bf = A.bf(128)
        ident_f = A.f32(128)
        ones_f = A.f32(128)
        triU_f = A.f32(128)
        mUs_bd = A.f32(128)
        mLs_bd = A.f32(128)
        mL_off = A.f32(128)
        gmix_bc = A.f32(D)
        gmlp_bc = A.f32(D)
        gnorm4 = A.f32(512)
        alog_bc = A.f32(NT8)
        dtb_bc = A.f32(NT8)
        fb_bc = A.f32(NT8)
        cw = A.f32(96)
        fqg = A.f32(1)
        fkg = A.f32(1)
        eps_c = A.f32(1)
        one_c = A.f32(1)

        k.memset("pool", ones_bf, 1.0, ["ones_bf"])
        k.memset("pool", onesm_bf, 1.0 / 128, ["onesm_bf"])
        k.memset("pool", ones_f, 1.0, ["ones_f"])
        k.memset("pool", eps_c, EPS, ["eps_c"])
        k.memset("pool", one_c, 1.0, ["one_c"])

        def asel(out, in_, step, cm, cmp, r, w):
            P.op("pool", lambda e: e.affine_select(out=out, in_=in_, pattern=[[step, 128]], compare_op=cmp, fill=0.0,
                                                   base=0, channel_multiplier=cm), r, w)
        asel(ident_bf, ones_bf, 1, -1, ALU.is_equal, ["ones_bf"], ["ident_bf"])
        asel(ident_f, ones_f, 1, -1, ALU.is_equal, ["ones_f"], ["ident_f"])
        asel(mUi_bf, ones_bf, 1, -1, ALU.is_ge, ["ones_bf"], ["mUi_bf"])
        asel(triU_f, ones_f, 1, -1, ALU.is_ge, ["ones_f"], ["triU_f"])
        asel(mUs_bd, ones_f, 1, -1, ALU.is_gt, ["ones_f"], ["mUs_bd"])
        asel(mLs_bd, ones_f, -1, 1, ALU.is_gt, ["ones_f"], ["mLs_bd"])
        k.memset("pool", mUs_bd[0:64, 64:128], 0.0, ["mUs_bd"])
        k.memset("pool", mLs_bd[64:128, 0:64], 0.0, ["mLs_bd"])
        k.memset("pool", mL_off, 0.0, ["mL_off"])
        k.memset("pool", mL_off[64:128, 0:64], 1.0, ["mL_off"])
        k.dma("sp", gmix_bc, gmix_d[0:1, :].partition_broadcast(128), [], ["gmix_bc"], "cst1")
        k.dma("sp", gmlp_bc, gmlp_d[0:1, :].partition_broadcast(128), [], ["gmlp_bc"], "cst2")
        k.dma("sp", gnorm4, gnorm4_d[0:1, :].partition_broadcast(128), [], ["gnorm4"], "cst3")
        k.dma("sp", alog_bc, alog_d[0:1, :].partition_broadcast(128), [], ["alog_bc"], "cst4")
        k.dma("sp", dtb_bc, dtb_d[0:1, :].partition_broadcast(128), [], ["dtb_bc"], "cst5")
        k.dma("sp", fb_bc, fb_d[0:1, :].partition_broadcast(128), [], ["fb_bc"], "cst6")
        k.dma("sp", cw, convw_d[:, :], [], ["cw"], "cst7")
        k.dma("sp", fqg, fqg_d[:, :], [], ["fqg"], "cst8")
        k.dma("sp", fkg, fkg_d[:, :], [], ["fkg"], "cst9")

        a1 = A.mark()
        uT_flat = A.bf(8 * T)
        oaT_flat = A.bf(8 * T)
        obT_flat = A.bf(8 * T)
        a1_end = A.mark()
        uT = uT_flat.rearrange("p (c t) -> p c t", c=8)
        oaT = oaT_flat.rearrange("p (c t) -> p c t", c=8)
        obT = obT_flat.rearrange("p (c t) -> p c t", c=8)
        A.release(a1)
        h_flat = A.f32(NT * D)
        hnT_flat = A.bf(8 * T)
        assert A.mark() <= a1_end
        A.release(a1_end)
        h3 = h_flat.rearrange("p (i d) -> p i d", i=NT)
        hnT = hnT_flat.rearrange("p (c t) -> p c t", c=8)

        tabs = {}
        for nm in ["g", "G", "beta", "nbeta", "kds", "decl", "lf", "c", "tot", "cpre", "cend", "ty", "ta", "tb"]:
            tabs[nm] = A.f32(NT8)
        btab_flat = A.f32(8 * NT * NT)
        btab = btab_flat.rearrange("p (h j i) -> p h j i", h=8, j=NT)
        m0 = A.mark()

        def t3(ap):
            return ap.rearrange("p (i h) -> p i h", h=8)

        outs = []

        def chk(name):
            if stop == name:
                raise StopBuild()

        for s in range(NSEQ):
          try:
              A.release(m0)
              xt = [A.f32(D), A.f32(D)]
              junk = A.bf(D)
              ubf = [A.bf(D), A.bf(D)]
              ssc = [A.f32(1), A.f32(1)]
              for i in range(NT):
                  sl = i % 2
                  k.dma("sp", xt[sl], x_d[s, i * 128:(i + 1) * 128, :], [], ["xt%d" % sl], "xt%d" % sl)
                  k.act(junk, xt[sl], AF.Square, ["xt%d" % sl], ["junk", "ss%d" % sl], accum=ssc[sl])
                  k.act(ssc[sl], ssc[sl], AF.Ln, ["ss%d" % sl, "eps_c"], ["ss%d" % sl], scale=1.0 / D, bias=eps_c)
                  k.act(ssc[sl], ssc[sl], AF.Exp, ["ss%d" % sl], ["ss%d" % sl], scale=-0.5)
                  k.stt(ubf[sl], xt[sl], ssc[sl], gmix_bc, ALU.mult, ALU.mult, ["xt%d" % sl, "ss%d" % sl, "gmix_bc"], ["ubf%d" % sl])
                  for c in range(8):
                      k.tr(pbs(c), ubf[sl][:, c * 128:(c + 1) * 128], ident_bf, ["ubf%d" % sl, "ident_bf"], ["pf7"])
                  k.cp("act", uT[:, :, i * 128:(i + 1) * 128], pbf.rearrange("p (c t) -> p c t", c=8),
                       ["pf7"], ["uT%d" % i])
              uT_all = ["uT%d" % i for i in range(NT)]

              def uTb(b):
                  return ["uT%d" % i for i in range(b * TPB, (b + 1) * TPB)]

              if stop == "P1":
                  outs.append(k.dma("pool", dbg_d["uT"][:, :], uT_flat, uT_all, [], "dbgp_uT"))
                  break
              P.barrier()
              A.release(m0)
              wsm = A.bf(8 * 24)
              wsm3 = wsm.rearrange("p (c n) -> p c n", c=8)
              k.dma("pool", wsm, wsm_d[:, :], [], ["wsm"], "wsm")
              psS = pf[0][:, 0:NT * 24]
              for i in range(NT):
                  for c in range(8):
                      k.mm(pf[0][:, i * 24:(i + 1) * 24], uT[:, c, i * 128:(i + 1) * 128], wsm3[:, c, :], c == 0, c == 7,
                           ["uT%d" % i, "wsm"], PF(0))
              ps3 = psS.rearrange("p (i n) -> p i n", n=24)
              ga, gb, ff = ps3[:, :, 0:8], ps3[:, :, 8:16], ps3[:, :, 16:24]
              ty, ta, tb = tabs["ty"], tabs["ta"], tabs["tb"]
              k.tt("dve", t3(ty), ga, t3(dtb_bc), ALU.add, PF(0) + ["dtb_bc"], ["ty"])
              k.stt(ta, ty, -1.0, ty, ALU.mult, ALU.max, ["ty"], ["ta"])
              k.act(ta, ta, AF.Exp, ["ta"], ["ta"], scale=-1.0)
              k.act(ta, ta, AF.Ln, ["ta", "one_c"], ["ta"], bias=one_c)
              k.stt(ty, ty, 0.0, ta, ALU.max, ALU.add, ["ty", "ta"], ["ty"])
              k.act(tb, alog_bc, AF.Exp, ["alog_bc"], ["tb"])
              k.stt(tabs["g"], ty, -1.0, tb, ALU.mult, ALU.mult, ["ty", "tb"], ["g"])
              k.act(t3(tabs["beta"]), gb, AF.Sigmoid, PF(0), ["beta"])
              k.ts("dve", tabs["nbeta"], tabs["beta"], -1.0, ALU.mult, ["beta"], ["nbeta"])
              k.tt("dve", t3(ty), ff, t3(fb_bc), ALU.add, PF(0) + ["fb_bc"], ["ty"])
              k.stt(ta, ty, -1.0, ty, ALU.mult, ALU.max, ["ty"], ["ta"])
              k.act(ta, ta, AF.Exp, ["ta"], ["ta"], scale=-1.0)
              k.act(ta, ta, AF.Ln, ["ta", "one_c"], ["ta"], bias=one_c)
              k.stt(tabs["lf"], ty, 0.0, ta, ALU.min, ALU.subtract, ["ty", "ta"], ["lf"])
              k.mm(pf[1][:, 0:NT8], triU_f, tabs["g"], True, True, ["triU_f", "g"], ["pf1"])
              k.mm(pf[1][:, 128:128 + NT8], ones_f, tabs["g"], True, True, ["ones_f", "g"], ["pf1"])
              k.mm(pf[1][:, 256:256 + NT8], triU_f, tabs["lf"], True, True, ["triU_f", "lf"], ["pf1"])
              k.mm(pf[1][:, 384:384 + NT8], ones_f, tabs["lf"], True, True, ["ones_f", "lf"], ["pf1"])
              k.cp("act", tabs["G"], pf[1][:, 0:NT8], ["pf1"], ["G"])
              k.act(tabs["decl"], pf[1][:, 128:128 + NT8], AF.Exp, ["pf1"], ["decl"])
              k.tt("dve", ty, pf[1][:, 128:128 + NT8], tabs["G"], ALU.subtract, ["pf1", "G"], ["ty"])
              k.act(tabs["kds"], ty, AF.Exp, ["ty"], ["kds"])
              k.cp("act", tabs["tot"], pf[1][:, 384:384 + NT8], ["pf1"], ["tot"])
              k.memset("dve", tabs["cpre"][:, 0:8], 0.0, ["cpre"])
              for i in range(1, NT):
                  k.tt("dve", tabs["cpre"][:, i * 8:(i + 1) * 8], tabs["cpre"][:, (i - 1) * 8:i * 8],
                       tabs["tot"][:, (i - 1) * 8:i * 8], ALU.add, ["cpre", "tot"], ["cpre"])
              k.tt("dve", tabs["c"], pf[1][:, 256:256 + NT8], tabs["cpre"], ALU.add, ["pf1", "cpre"], ["c"])
              k.tt("dve", tabs["cend"], tabs["cpre"], tabs["tot"], ALU.add, ["cpre", "tot"], ["cend"])
              cend3 = t3(tabs["cend"])
              n_ = 0
              for h in range(NH):
                  for j in range(NT):
                      eng = "dve" if n_ % 2 == 0 else "pool"
                      n_ += 1
                      col = j * 8 + h
                      k.ts(eng, btab[:, h, j, :], cend3[:, :, h], tabs["c"][:, col:col + 1], ALU.subtract, ["cend", "c"], ["btab"])

              if stop == "SMALL":
                  outs.append(k.dma("pool", dbg_d["uT"][:, :], uT_flat, uT_all, [], "dbgp_uT"))
                  for ti, nm in enumerate(["g", "G", "beta", "kds", "decl", "c"]):
                      outs.append(k.dma("sp", dbg_d["tab"][:, ti * NT8:(ti + 1) * NT8], tabs[nm], [nm], [], "dbg"))
                  break
              P.barrier()
              A.release(m0)
              wg = [A.bf(8 * 512)]
              GRP = min(NT, int(os.environ.get("GDN_GRP", "4")))
              GW = GRP * 128
              NG = NT // GRP
              qT, kT, vT = A.bf(T), A.bf(T), A.bf(T)
              Kd_f, V_f, gz_f = A.bf(T), A.f32(T), A.f32(T)
              Kd = Kd_f.rearrange("p (i d) -> p i d", d=128)
              Vt = V_f.rearrange("p (i d) -> p i d", d=128)
              gz = gz_f.rearrange("p (i d) -> p i d", d=128)
              S_f = A.f32(128)
              S_bf, R_bf, vnew, on_bf, junk128 = [A.bf(128) for _ in range(5)]
              ssq = A.f32(1)
              mp = A.mark()
              sq = [A.bf(BW), A.bf(BW)]
              cb = [A.f32(BW + 3), A.f32(BW + 3)]
              acc = [A.f32(BW), A.f32(BW)]
              lnt = [A.f32(BW), A.f32(BW)]
              rs = [A.f32(BW), A.f32(BW)]
              tmpz = A.f32(BW)
              A.release(mp)
              grep_ = [A.f32(128), A.f32(128)]
              brep_ = [A.f32(128), A.f32(128)]
              E_, Y_, M1_, DuI, DuB, DlB, DlO, bbc = [A.f32(GW) for _ in range(8)]
              eG = A.bf(GW)
              Qb = [A.bf(GW), A.bf(GW)]
              Pb = [A.bf(GW), A.bf(GW)]
              Mb = [A.bf(GW), A.bf(GW)]
              Ao, Td, X1 = [A.bf(GW) for _ in range(3)]
              AttnT = [A.bf(GW), A.bf(GW)]
              QeT = [A.bf(GW), A.bf(GW)]
              KeT = [A.bf(GW), A.bf(GW)]
              TpT = [A.bf(GW), A.bf(GW)]

              def fm_norm(src, src_r, dst, dst_w, ones_t, ones_r, mult, mult_r, pbank, par):
                  k.act(sq[par], src, AF.Square, src_r, ["sq%d" % par])
                  k.mm(pf[pbank][:, 0:BW], ones_t, sq[par], True, True, ["sq%d" % par] + ones_r, PF(pbank))
                  k.act(lnt[par], pf[pbank][:, 0:BW], AF.Ln, PF(pbank) + ["eps_c"], ["lnt%d" % par], bias=eps_c)
                  k.act(rs[par], lnt[par], AF.Exp, ["lnt%d" % par], ["rs%d" % par], scale=-0.5)
                  k.stt(dst, src, mult, rs[par], ALU.mult, ALU.mult, src_r + ["rs%d" % par] + mult_r, dst_w)

              pcnt = 0
              for h in range(NH):
                  ws = 0
                  wgs = wg[ws]
                  wg3 = wgs.rearrange("p (c n) -> p c n", c=8)
                  k.dma("pool", wgs, wgdn_d[h, :, :], [], ["wg%d" % ws], "wg%d" % ws)
                  for grp in range(3):
                      chunk = grp * 8 + h
                      dstT = (qT, kT, vT)[grp]
                      dnm = ("qT", "kT", "vT")[grp]
                      for b in range(NB):
                          pb = b % 2
                          pr = pcnt % 2
                          pcnt += 1
                          cbn, accn = "cb%d" % pr, "acc%d" % pr
                          cbc, accc = cb[pr], acc[pr]
                          for c in range(8):
                              k.mm(pf[pb][:, 0:BW], wg3[:, c, grp * 128:(grp + 1) * 128], uT[:, c, b * BW:(b + 1) * BW],
                                   c == 0, c == 7, ["wg%d" % ws] + uTb(b), PF(pb))
                          if b == 0:
                              k.memset("pool", cbc[:, 0:3], 0.0, [cbn])
                          else:
                              k.cp("act", cbc[:, 0:3], cb[1 - pr][:, BW:BW + 3], ["cb%d" % (1 - pr)], [cbn])
                          k.cp("act", cbc[:, 3:3 + BW], pf[pb][:, 0:BW], PF(pb), [cbn])
                          k.ts("dve", accc, cbc[:, 0:BW], cw[:, chunk * 4:chunk * 4 + 1], ALU.mult, [cbn, "cw"], [accn])
                          for tap in range(1, 4):
                              k.stt(accc, cbc[:, tap:tap + BW], cw[:, chunk * 4 + tap:chunk * 4 + tap + 1], accc, ALU.mult, ALU.add,
                                    [cbn, "cw", accn], [accn])
                          k.act(accc, accc, AF.Silu, [accn], [accn])
                          dsl = dstT[:, b * BW:(b + 1) * BW]
                          dw = ["%s%d" % (dnm, i) for i in range(b * TPB, (b + 1) * TPB)]
                          if grp < 2:
                              fm_norm(accc, [accn], dsl, dw, ones_bf, ["ones_bf"], (DH ** -0.5) if grp == 0 else 1.0, [], 2 + pr, pr)
                          else:
                              k.cp("pool", dsl, accc, [accn], dw)
                  chk("GDN_a")
                  for i in range(NT):
                      q4 = i % 4
                      for c in range(8):
                          k.mm(pq(3, q4), uT[:, c, i * 128:(i + 1) * 128], wg3[:, c, 384:512], c == 0, c == 7,
                               ["uT%d" % i, "wg%d" % ws], ["pf3"])
                      if q4 == TPB - 1 or i == NT - 1:
                          n4 = q4 + 1
                          i0 = i - q4
                          k.act(tmpz[:, 0:n4 * 128], pf[3][:, 0:n4 * 128], AF.Silu, ["pf3"], ["tmpz"])
                          k.tt("dve", gz_f[:, i0 * 128:(i0 + n4) * 128], tmpz[:, 0:n4 * 128], gnorm4[:, 0:n4 * 128], ALU.mult,
                               ["tmpz", "gnorm4"], ["gz%d" % ii for ii in range(i0, i0 + n4)])
                  chk("GDN_b")
                  for i in range(NT):
                      col = i * 8 + h
                      sk = (2 * i) % 8
                      k.tr(pbs(sk), kT[:, i * 128:(i + 1) * 128], ident_bf, ["kT%d" % i, "ident_bf"], ["pf7"])
                      k.act(Kd[:, i, :], pbs(sk), AF.Copy, ["pf7", "kds"], ["Kd%d" % i], scale=tabs["kds"][:, col:col + 1])
                      sv = (2 * i + 1) % 8
                      k.tr(pbs(sv), vT[:, i * 128:(i + 1) * 128], ident_bf, ["vT%d" % i, "ident_bf"], ["pf7"])
                      k.cp("act", Vt[:, i, :], pbs(sv), ["pf7"], ["V%d" % i])
                  chk("GDN_c")
                  P.barrier()
                  k.memset("pool", S_f, 0.0, ["S_f"])
                  k.memset("pool", S_bf, 0.0, ["S_bf"])

                  def v3(ap):
                      return ap.rearrange("p (g f) -> p g f", g=GRP)

                  def bcm(m):
                      return m.unsqueeze(1).to_broadcast([128, GRP, 128])

                  def colbc(tab, i0, h_):
                      return t3(tab)[:, i0:i0 + GRP, h_].unsqueeze(2).to_broadcast([128, GRP, 128])

                  def gs(t):
                      return slice(t * 128, (t + 1) * 128)

                  def prep(g, h=h):
                      i0 = g * GRP
                      gp = g % 2
                      b0v, b1v, b2v, b3v = [pf[b_][:, 0:GW] for b_ in range(4)]
                      for t in range(GRP):
                          col = (i0 + t) * 8 + h
                          r2 = t % 2
                          k.cp("dve", grep_[r2], tabs["g"][:, col:col + 1].to_broadcast([128, 128]), ["g"], ["grep%d" % r2])
                          k.cp("dve", brep_[r2], tabs["beta"][:, col:col + 1].to_broadcast([128, 128]), ["beta"], ["brep%d" % r2])
                          k.mm(pq(0, t), grep_[r2], triU_f, True, True, ["grep%d" % r2, "triU_f"], ["pf0"])
                          k.mm(pq(1, t), brep_[r2], ident_f, True, True, ["brep%d" % r2, "ident_f"], ["pf1"])
                      k.cp("act", bbc, b1v, ["pf1"], ["bbc"])
                      k.act(eG, b0v, AF.Exp, ["pf0"], ["eG"])
                      k.tt("dve", v3(E_), v3(b0v), colbc(tabs["G"], i0, h), ALU.subtract, ["pf0", "G"], ["E"])
                      k.ts("dve", Y_, E_, 0.0, ALU.max, ["E"], ["Y"])
                      k.ts("dve", E_, E_, 0.0, ALU.min, ["E"], ["E"])
                      k.act(Y_, Y_, AF.Exp, ["Y"], ["Y"], scale=-1.0)
                      k.act(E_, E_, AF.Exp, ["E"], ["E"])
                      k.tt("pool", v3(M1_), v3(bbc), bcm(mUs_bd), ALU.mult, ["bbc", "mUs_bd"], ["M1"])
                      k.tt("dve", v3(DuI), v3(E_), bcm(triU_f), ALU.mult, ["E", "triU_f"], ["DuI"])
                      k.tt("dve", DuB, E_, M1_, ALU.mult, ["E", "M1"], ["DuB"])
                      k.tt("dve", v3(DlB), v3(Y_), bcm(mLs_bd), ALU.mult, ["Y", "mLs_bd"], ["DlB"])
                      k.tt("pool", v3(DlB), v3(DlB), colbc(tabs["beta"], i0, h), ALU.mult, ["DlB", "beta"], ["DlB"])
                      k.tt("dve", v3(DlO), v3(Y_), bcm(mL_off), ALU.mult, ["Y", "mL_off"], ["DlO"])
                      k.tt("pool", v3(DlO), v3(DlO), colbc(tabs["beta"], i0, h), ALU.mult, ["DlO", "beta"], ["DlO"])
                      yield
                      for t in range(GRP):
                          i = i0 + t
                          kTi = kT[:, i * 128:(i + 1) * 128]
                          qTi = qT[:, i * 128:(i + 1) * 128]
                          k.mm(pq(2, t), kTi, kTi, True, True, ["kT%d" % i], ["pf2"])
                          k.mm(pq(3, t), kTi, qTi, True, True, ["kT%d" % i, "qT%d" % i], ["pf3"])
                      kr = ["kT%d" % (i0 + t) for t in range(GRP)]
                      qr = ["qT%d" % (i0 + t) for t in range(GRP)]
                      k.tt("dve", Qb[0], b2v, DlB, ALU.mult, ["pf2", "DlB"], ["Q0"])
                      k.tt("dve", Ao, b2v, DlO, ALU.mult, ["pf2", "DlO"], ["Ao"])
                      k.tt("dve", Pb[0], b2v, DuB, ALU.mult, ["pf2", "DuB"], ["P0"])
                      k.tt("dve", AttnT[gp], b3v, DuI, ALU.mult, ["pf3", "DuI"], ["AttnT%d" % gp])
                      k.tt("dve", v3(Mb[0]), bcm(ident_bf), v3(Pb[0]), ALU.subtract, ["ident_bf", "P0"], ["M0"])
                      k.tt("dve", QeT[gp], qT[:, i0 * 128:i0 * 128 + GW], eG, ALU.mult, qr + ["eG"], ["QeT%d" % gp])
                      k.tt("dve", KeT[gp], kT[:, i0 * 128:i0 * 128 + GW], eG, ALU.mult, kr + ["eG"], ["KeT%d" % gp])
                      cur = 0
                      for lvl in range(1, 6):
                          nxt = 1 - cur
                          for t in range(GRP):
                              if lvl < 5:
                                  k.mm(pq(0, t), Qb[cur][:, gs(t)], Pb[cur][:, gs(t)], True, True, ["Q%d" % cur, "P%d" % cur], ["pf0"])
                              k.mm(pq(1, t), Pb[cur][:, gs(t)], Qb[cur][:, gs(t)], True, True, ["Q%d" % cur, "P%d" % cur], ["pf1"])
                          if lvl < 5:
                              k.cp("act", Pb[nxt], b0v, ["pf0"], ["P%d" % nxt])
                          k.cp("dve", Qb[nxt], b1v, ["pf1"], ["Q%d" % nxt])
                          for t in range(GRP):
                              k.mm(pq(2, t), ident_bf, Mb[cur][:, gs(t)], True, False, ["ident_bf", "M%d" % cur], ["pf2"])
                              k.mm(pq(2, t), Qb[nxt][:, gs(t)], Mb[cur][:, gs(t)], False, True, ["Q%d" % nxt, "M%d" % cur], ["pf2"])
                          k.cp("act", Mb[nxt], b2v, ["pf2"], ["M%d" % nxt])
                          if lvl == 5:
                              k.cp("dve", Y_, b2v, ["pf2"], ["Y"])
                          cur = nxt
                          if lvl == 2:
                              yield
                      yield
                      Ud = Mb[cur]
                      Udn = "M%d" % cur
                      for t in range(GRP):
                          k.tr(pbs(t), Ud[:, gs(t)], ident_bf, [Udn, "ident_bf"], ["pf7"])
                      k.cp("act", Td, pbf[:, 0:GW], ["pf7"], ["Td"])
                      for t in range(GRP):
                          k.mm(pq(3, t), Ao[:, gs(t)], Ud[:, gs(t)], True, True, ["Ao", Udn], ["pf3"])
                      k.cp("act", X1, b3v, ["pf3"], ["X1"])
                      for t in range(GRP):
                          k.mm(pq(0, t), Td[:, gs(t)], X1[:, gs(t)], True, True, ["Td", "X1"], ["pf0"])
                      k.tt("dve", Y_, Y_, b0v, ALU.subtract, ["Y", "pf0"], ["Y"])
                      k.tt("pool", v3(TpT[gp]), v3(Y_), colbc(tabs["beta"], i0, h), ALU.mult, ["Y", "beta"], ["TpT%d" % gp])
                      yield

                  def scan_step(i, h=h):
                      g = i // GRP
                      t = i % GRP
                      gp = g % 2
                      d2 = i % 2
                      col = i * 8 + h
                      k.mm(pq(4, 0), KeT[gp][:, gs(t)], S_bf, True, True, ["KeT%d" % gp, "S_bf"], ["pf4"])
                      k.tt("dve", R_bf, Vt[:, i, :], pq(4, 0), ALU.subtract, ["V%d" % i, "pf4"], ["R_bf"])
                      k.mm(pq(4, 1), TpT[gp][:, gs(t)], R_bf, True, True, ["TpT%d" % gp, "R_bf"], ["pf4"])
                      k.cp("act", vnew, pq(4, 1), ["pf4"], ["vnew"])
                      k.mm(pq(5 + d2, 0), QeT[gp][:, gs(t)], S_bf, True, False, ["QeT%d" % gp, "S_bf"], ["pf%d" % (5 + d2)])
                      k.mm(pq(5 + d2, 0), AttnT[gp][:, gs(t)], vnew, False, True, ["AttnT%d" % gp, "vnew"], ["pf%d" % (5 + d2)])
                      k.mm(pq(4, 3), Kd[:, i, :], vnew, True, True, ["Kd%d" % i, "vnew"], ["pf4"])
                      k.stt(S_f, S_f, tabs["decl"][:, col:col + 1], pq(4, 3), ALU.mult, ALU.add, ["S_f", "decl", "pf4"], ["S_f"])
                      k.cp("act", S_bf, S_f, ["S_f"], ["S_bf"])
                      k.act(junk128, pq(5 + d2, 0), AF.Square, ["pf%d" % (5 + d2)], ["junk128", "ssq"], accum=ssq)
                      k.act(ssq, ssq, AF.Ln, ["ssq", "eps_c"], ["ssq"], scale=1.0 / DH, bias=eps_c)
                      k.act(ssq, ssq, AF.Exp, ["ssq"], ["ssq"], scale=-0.5)
                      k.stt(on_bf, pq(5 + d2, 0), ssq, gz[:, i, :], ALU.mult, ALU.mult, ["pf%d" % (5 + d2), "ssq", "gz%d" % i], ["on_bf"])
                      so = 4 + (i % 4)
                      k.tr(pbs(so), on_bf, ident_bf, ["on_bf", "ident_bf"], ["pf7"])
                      k.cp("act", oaT[:, h, i * 128:(i + 1) * 128], pbs(so), ["pf7"], ["oaT%d_%d" % (h, i)])

                  for _ in prep(0):
                      pass
                  NST = 4
                  per = (NST + GRP - 1) // GRP
                  for g in range(NG):
                      gen = prep(g + 1) if g + 1 < NG else iter(())
                      for t in range(GRP):
                          scan_step(g * GRP + t)
                          for _ in range(per):
                              next(gen, None)
                      for _ in gen:
                          pass
                  P.barrier()
                  chk("GDN_h0")

              chk("GDN2")
              P.barrier()
              A.release(m0)
              wf = [A.bf(8 * 384), A.bf(8 * 384)]
              fqT, fkT, fV_f = A.bf(T), A.bf(T), A.bf(T)
              fV = fV_f.rearrange("p (i d) -> p i d", d=128)
              sq = [A.bf(BW), A.bf(BW)]
              lnt = [A.f32(BW), A.f32(BW)]
              rs = [A.f32(BW), A.f32(BW)]
              wff = A.f32(8 * 384)
              PT = [A.bf(BW), A.bf(BW)]
              rden = [A.f32(BW), A.f32(BW)]
              scale = DH ** -0.5
              for h in range(NH):
                  ws = h % 2
                  wfs = wf[ws]
                  wf3 = wfs.rearrange("p (c n) -> p c n", c=8)
                  k.dma("sp", wff, wfox_d[h, :, :], [], ["wff"], "wff")
                  for cc in range(3):
                      k.cp("pool", wfs[:, cc * 1024:(cc + 1) * 1024], wff[:, cc * 1024:(cc + 1) * 1024], ["wff"], ["wf%d" % ws])
                  for grp in range(2):
                      dstT = (fqT, fkT)[grp]
                      dnm = ("fqT", "fkT")[grp]
                      gcol = (fqg, fkg)[grp]
                      gnm = ("fqg", "fkg")[grp]
                      for b in range(NB):
                          pb = b % 2
                          for c in range(8):
                              k.mm(pf[pb][:, 0:BW], wf3[:, c, grp * 128:(grp + 1) * 128], uT[:, c, b * BW:(b + 1) * BW],
                                   c == 0, c == 7, ["wf%d" % ws] + uTb(b), PF(pb))
                          dw = ["%s%d" % (dnm, i) for i in range(b * TPB, (b + 1) * TPB)]
                          pr = pcnt % 2
                          pcnt += 1
                          fm_norm(pf[pb][:, 0:BW], PF(pb), dstT[:, b * BW:(b + 1) * BW], dw, onesm_bf, ["onesm_bf"],
                                  gcol, [gnm], (2, 7)[pr], pr)
                  for i in range(NT):
                      q4 = i % 4
                      for c in range(8):
                          k.mm(pq(5, q4), uT[:, c, i * 128:(i + 1) * 128], wf3[:, c, 256:384], c == 0, c == 7,
                               ["uT%d" % i, "wf%d" % ws], ["pf5"])
                      if q4 == TPB - 1 or i == NT - 1:
                          n4 = q4 + 1
                          i0 = i - q4
                          k.cp("act", fV_f[:, i0 * 128:(i0 + n4) * 128], pf[5][:, 0:n4 * 128], ["pf5"],
                               ["fV%d" % ii for ii in range(i0, i0 + n4)])
                  cnt = 0
                  for Ib in range(NB):
                      bo, bd = ((5, 6), (0, 1))[Ib % 2]
                      last_j = (Ib + 1) * TPB - 1
                      for j in range(last_j + 1):
                          r0 = max(0, j - Ib * TPB)
                          qoff = r0 * 128
                          sb_ = 3 + (cnt % 2)
                          ptb = cnt % 2
                          cnt += 1
                          qr = ["fqT%d" % (Ib * TPB + r) for r in range(r0, TPB)]
                          k.mm(pf[sb_][:, qoff:BW], fkT[:, j * 128:(j + 1) * 128], fqT[:, Ib * BW + qoff:(Ib + 1) * BW], True, True,
                               ["fkT%d" % j] + qr, PF(sb_))
                          for r in range(r0, TPB):
                              i = Ib * TPB + r
                              k.act(PT[ptb][:, r * 128:(r + 1) * 128], pf[sb_][:, r * 128:(r + 1) * 128], AF.Exp,
                                    PF(sb_) + ["btab"], ["PT%d" % ptb], scale=scale, bias=btab[:, h, j, i:i + 1])
                              if i == j:
                                  k.tt("dve", PT[ptb][:, r * 128:(r + 1) * 128], PT[ptb][:, r * 128:(r + 1) * 128], mUi_bf, ALU.mult,
                                       ["PT%d" % ptb, "mUi_bf"], ["PT%d" % ptb])
                          k.mm(pf[bo][:, qoff:BW], fV[:, j, :], PT[ptb][:, qoff:BW], j == 0, j == last_j,
                               ["fV%d" % j, "PT%d" % ptb], PF(bo))
                          k.mm(pf[bd][:, qoff:BW], ones_bf, PT[ptb][:, qoff:BW], j == 0, j == last_j,
                               ["ones_bf", "PT%d" % ptb], PF(bd))
                      rdn = rden[Ib % 2]
                      k.P.op("dve", lambda e, rdn=rdn, bd=bd: e.reciprocal(out=rdn, in_=pf[bd][:, 0:BW]), PF(bd), ["rden%d" % (Ib % 2)])
                      k.tt("dve", obT[:, h, Ib * BW:(Ib + 1) * BW], pf[bo][:, 0:BW], rdn, ALU.mult, PF(bo) + ["rden%d" % (Ib % 2)],
                           ["obT%d_%d" % (h, Ib)])

              if dbg and s == 0:
                  outs.append(k.dma("pool", dbg_d["uT"][:, :], uT_flat, uT_all, [], "dbgp_uT"))
                  outs.append(k.dma("pool", dbg_d["oaT"][:, :], oaT_flat, ["oaT%d_%d" % (h, i) for h in range(NH) for i in range(NT)], [], "dbgp_oaT"))
                  outs.append(k.dma("pool", dbg_d["obT"][:, :], obT_flat, ["obT%d_%d" % (h, b) for h in range(NH) for b in range(NB)], [], "dbgp_obT"))
                  for ti, nm in enumerate(["g", "G", "beta", "kds", "decl", "c"]):
                      outs.append(k.dma("sp", dbg_d["tab"][:, ti * NT8:(ti + 1) * NT8], tabs[nm], [nm], [], "dbg"))

              chk("FOX")
              P.barrier()
              A.release(m0)
              mT_flat = A.bf(8 * T)
              mT = mT_flat.rearrange("p (c t) -> p c t", c=8)
              m_mT = A.mark()
              wga = [A.bf(8 * 256), A.bf(8 * 256)]
              wpa = [A.bf(8 * 256), A.bf(8 * 256)]
              sA, sB, mA, mB = [A.f32(BW) for _ in range(4)]
              wgaf = A.f32(8 * 256)
              wpaf = A.f32(8 * 256)
              oa_all = {b: ["oaT%d_%d" % (h, i) for h in range(NH) for i in range(b * TPB, (b + 1) * TPB)] for b in range(NB)}
              ob_all = {b: ["obT%d_%d" % (h, b) for h in range(NH)] for b in range(NB)}
              for n in range(8):
                  ws = n % 2
                  k.dma("sp", wgaf, wgate_d[n, :, :], [], ["wgaf"], "wgaf")
                  k.dma("sp", wpaf, wpab_d[n, :, :], [], ["wpaf"], "wpaf")
                  for cc in range(2):
                      k.cp("pool", wga[ws][:, cc * 1024:(cc + 1) * 1024], wgaf[:, cc * 1024:(cc + 1) * 1024], ["wgaf"], ["wga%d" % ws])
                      k.cp("pool", wpa[ws][:, cc * 1024:(cc + 1) * 1024], wpaf[:, cc * 1024:(cc + 1) * 1024], ["wpaf"], ["wpa%d" % ws])
                  wga3 = wga[ws].rearrange("p (c n) -> p c n", c=8)
                  wpa3 = wpa[ws].rearrange("p (c n) -> p c n", c=8)
                  for b in range(NB):
                      bs = slice(b * BW, (b + 1) * BW)
                      o4 = 4 * ((n * NB + b) % 2)
                      for c in range(8):
                          k.mm(pf[o4 + 0][:, 0:BW], wpa3[:, c, 0:128], oaT[:, c, bs], c == 0, c == 7, ["wpa%d" % ws] + oa_all[b], PF(o4 + 0))
                      for c in range(8):
                          k.mm(pf[o4 + 1][:, 0:BW], wga3[:, c, 0:128], uT[:, c, bs], c == 0, c == 7, ["wga%d" % ws] + uTb(b), PF(o4 + 1))
                      for c in range(8):
                          k.mm(pf[o4 + 2][:, 0:BW], wpa3[:, c, 128:256], obT[:, c, bs], c == 0, c == 7, ["wpa%d" % ws] + ob_all[b], PF(o4 + 2))
                      for c in range(8):
                          k.mm(pf[o4 + 3][:, 0:BW], wga3[:, c, 128:256], uT[:, c, bs], c == 0, c == 7, ["wga%d" % ws] + uTb(b), PF(o4 + 3))
                      k.act(sA, pf[o4 + 1][:, 0:BW], AF.Sigmoid, PF(o4 + 1), ["sA"])
                      k.act(sB, pf[o4 + 3][:, 0:BW], AF.Sigmoid, PF(o4 + 3), ["sB"])
                      k.tt("dve", mA, pf[o4 + 0][:, 0:BW], sA, ALU.mult, PF(o4 + 0) + ["sA"], ["mA"])
                      k.tt("dve", mB, pf[o4 + 2][:, 0:BW], sB, ALU.mult, PF(o4 + 2) + ["sB"], ["mB"])
                      k.tt("dve", mT[:, n, bs], mA, mB, ALU.add, ["mA", "mB"], ["mT%d_%d" % (n, b)])

              P.barrier()
              A.release(m_mT)
              wo = A.bf(8 * 1024)
              wo3 = wo.rearrange("p (c n) -> p c n", c=8)
              xt = [A.f32(D), A.f32(D)]
              hnb = [A.bf(D), A.bf(D)]
              junk = A.bf(D)
              ssc = [A.f32(1), A.f32(1)]
              k.dma("pool", wo, wout_d[:, :], [], ["wo"], "wo")
              for i in range(NT):
                  sl = i % 2
                  k.dma("sp", xt[sl], x_d[s, i * 128:(i + 1) * 128, :], [], ["xt%d" % sl], "xt%d" % sl)
                  for half in range(2):
                      for c in range(8):
                          k.mm(pf[half][:, :], mT[:, c, i * 128:(i + 1) * 128], wo3[:, c, half * 512:(half + 1) * 512], c == 0, c == 7,
                               ["wo"] + ["mT%d_%d" % (n_, i // TPB) for n_ in range(8)], PF(half))
                      k.tt("dve", h3[:, i, half * 512:(half + 1) * 512], pf[half][:, :], xt[sl][:, half * 512:(half + 1) * 512], ALU.add,
                           PF(half) + ["xt%d" % sl], ["h%d_%d" % (i, half)])
                  hr = ["h%d_0" % i, "h%d_1" % i]
                  k.act(junk, h3[:, i, :], AF.Square, hr, ["junk", "ss%d" % sl], accum=ssc[sl])
                  k.act(ssc[sl], ssc[sl], AF.Ln, ["ss%d" % sl, "eps_c"], ["ss%d" % sl], scale=1.0 / D, bias=eps_c)
                  k.act(ssc[sl], ssc[sl], AF.Exp, ["ss%d" % sl], ["ss%d" % sl], scale=-0.5)
                  k.stt(hnb[sl], h3[:, i, :], ssc[sl], gmlp_bc, ALU.mult, ALU.mult, hr + ["ss%d" % sl, "gmlp_bc"], ["hnb%d" % sl])
                  for c in range(8):
                      k.tr(pbs(c), hnb[sl][:, c * 128:(c + 1) * 128], ident_bf, ["hnb%d" % sl, "ident_bf"], ["pf7"])
                  k.cp("act", hnT[:, :, i * 128:(i + 1) * 128], pbf.rearrange("p (c t) -> p c t", c=8),
                       ["pf7"], ["hnT%d" % i])
              if dbg and s == 0:
                  outs.append(k.dma("sp", dbg_d["h"][:, :], h_flat, ["h%d_%d" % (i, hf) for i in range(NT) for hf in range(2)], [], "dbg"))

              chk("POSTH")
              P.barrier()
              A.release(m0)
              aT_flat = A.bf(32 * BW)
              aT = aT_flat.rearrange("p (f t) -> p f t", f=32)
              wup = [A.bf(1024) for _ in range(4)]
              wdn = [A.bf(1024) for _ in range(4)]
              wstg = [A.f32(1024) for _ in range(3)]
              scnt = 0
              rtmp = [A.f32(BW), A.f32(BW)]
              otile = [A.f32(D), A.f32(D)]
              for b in range(NB):
                  bs = slice(b * BW, (b + 1) * BW)
                  hb = ["hnT%d" % i for i in range(b * TPB, (b + 1) * TPB)]
                  for f in range(32):
                      ws = f % 4
                      sg = scnt % 3
                      scnt += 1
                      k.dma("sp", wstg[sg], wup_d[f, :, :], [], ["wstg%d" % sg], "wstg%d" % sg)
                      k.cp("pool", wup[ws], wstg[sg], ["wstg%d" % sg], ["wup%d" % ws])
                      wu3 = wup[ws].rearrange("p (c n) -> p c n", c=8)
                      pb = f % 2
                      for c in range(8):
                          k.mm(pf[pb][:, 0:BW], wu3[:, c, :], hnT[:, c, bs], c == 0, c == 7, ["wup%d" % ws] + hb, PF(pb))
                      k.act(rtmp[pb], pf[pb][:, 0:BW], AF.Relu, PF(pb), ["rtmp%d" % pb])
                      k.tt("dve", aT[:, f, :], rtmp[pb], rtmp[pb], ALU.mult, ["rtmp%d" % pb], ["aT%d" % f])
                  for f in range(32):
                      ws = f % 4
                      sg = scnt % 3
                      scnt += 1
                      k.dma("sp", wstg[sg], wdn_d[f, :, :], [], ["wstg%d" % sg], "wstg%d" % sg)
                      k.cp("pool", wdn[ws], wstg[sg], ["wstg%d" % sg], ["wdn%d" % ws])
                      for t in range(TPB):
                          for half in range(2):
                              bk = t * 2 + half
                              k.mm(pf[bk][:, :], aT[:, f, t * 128:(t + 1) * 128], wdn[ws][:, half * 512:(half + 1) * 512], f == 0, f == 31,
                                   ["aT%d" % f, "wdn%d" % ws], PF(bk))
                  for t in range(TPB):
                      i = b * TPB + t
                      sl = i % 2
                      for half in range(2):
                          bk = t * 2 + half
                          k.tt("dve", otile[sl][:, half * 512:(half + 1) * 512], pf[bk][:, :], h3[:, i, half * 512:(half + 1) * 512], ALU.add,
                               PF(bk) + ["h%d_%d" % (i, half)], ["ot%d_%d" % (sl, half)])
                      outs.append(k.dma("sp", out_d[s, i * 128:(i + 1) * 128, :], otile[sl], ["ot%d_0" % sl, "ot%d_1" % sl], [], "out%d" % sl))
              P.barrier()

          except StopBuild:
            outs.append(k.dma("pool", dbg_d["uT"][:, :], uT_flat, [], [], "dbgp_stop"))
            break
        P.final_waits += outs
        nw = P.emit(st)
        print("ops", len(P.ops), "waits", nw, "arena peak", A.peak)
    return nc


def prep_weights(inp, T):
    NT = T // 128
    f = lambda a: np.ascontiguousarray(np.asarray(a, dtype=np.float32))
    w_in = np.asarray(inp["w_in"])[0]
    offs = np.cumsum([0, 1024, 1024, 1024, 1024, 8, 8, 1024, 1024, 1024, 8, 1024, 1024])
    gq, gk, gv, gz, ga, gb, fq, fk, fv, ff, gta, gtb = [w_in[:, offs[i]:offs[i + 1]] for i in range(12)]

    def pc(w):
        return w.reshape(8, 128, -1).transpose(1, 0, 2)
    w_gdn = np.stack([pc(np.concatenate([g_[:, h * 128:(h + 1) * 128] for g_ in (gq, gk, gv, gz)], 1)).reshape(128, -1) for h in range(8)])
    w_fox = np.stack([pc(np.concatenate([g_[:, h * 128:(h + 1) * 128] for g_ in (fq, fk, fv)], 1)).reshape(128, -1) for h in range(8)])
    w_small = pc(np.concatenate([ga, gb, ff], 1)).reshape(128, -1)
    w_gate = np.stack([pc(np.concatenate([gta[:, n * 128:(n + 1) * 128], gtb[:, n * 128:(n + 1) * 128]], 1)).reshape(128, -1) for n in range(8)])
    pa = np.asarray(inp["w_proj_gdn"])[0]
    pb = np.asarray(inp["w_proj_fox"])[0]
    w_pab = np.stack([pc(np.concatenate([pa[:, n * 128:(n + 1) * 128], pb[:, n * 128:(n + 1) * 128]], 1)).reshape(128, -1) for n in range(8)])
    w_o = pc(np.asarray(inp["w_out"])[0]).reshape(128, -1)
    wu = np.asarray(inp["w_up"])[0]
    w_up = np.stack([pc(wu[:, fch * 128:(fch + 1) * 128]).reshape(128, -1) for fch in range(32)])
    w_down = np.asarray(inp["w_down"])[0].reshape(32, 128, 1024)
    cwt = np.asarray(inp["gdn_conv_w"])[0]
    convw = cwt.reshape(4, 24, 128).transpose(2, 1, 0).reshape(128, 96)
    d = {
        "w_gdn": w_gdn, "w_fox": w_fox, "w_small": w_small, "w_gate": w_gate, "w_pab": w_pab, "w_o": w_o,
        "w_up": w_up, "w_down": w_down, "convw": convw,
        "gmix": np.asarray(inp["norm_mix_g"])[0][None, :], "gmlp": np.asarray(inp["norm_mlp_g"])[0][None, :],
        "gnorm4": np.tile(np.asarray(inp["gdn_norm_g"])[0], 4)[None, :],
        "alog_t": np.tile(np.asarray(inp["gdn_a_log"])[0], NT)[None, :],
        "dtb_t": np.tile(np.asarray(inp["gdn_dt_bias"])[0], NT)[None, :],
        "fb_t": np.tile(np.asarray(inp["fox_f_bias"])[0], NT)[None, :],
        "fqg": np.asarray(inp["fox_q_norm_g"])[0][:, None], "fkg": np.asarray(inp["fox_k_norm_g"])[0][:, None],
    }
    return {k_: f(v) for k_, v in d.items()}


_NC_CACHE = {}


def kernel(**inputs):
    x = np.asarray(inputs["x"], dtype=np.float32)
    Bt, T, _ = x.shape
    ncores = 8
    nseq = Bt // ncores
    wd = prep_weights(inputs, T)
    key = (T, nseq)
    if key not in _NC_CACHE:
        nc = bass.Bass("TRN2", target_bir_lowering=False)
        build(nc, T, nseq)
        _NC_CACHE[key] = nc
    nc = _NC_CACHE[key]
    in_maps = []
    for c in range(ncores):
        m = dict(wd)
        m["x"] = np.ascontiguousarray(x[c * nseq:(c + 1) * nseq])
        in_maps.append(m)
    res = run_bass_kernel_spmd(nc, in_maps, core_ids=list(range(ncores)))
    out = np.concatenate([res.results[c]["out"] for c in range(ncores)], axis=0)
    return out.astype(np.float32)
```

```python
import contextlib
import concourse.bass as bass
import concourse.mybir as mybir

F32 = mybir.dt.float32
BF16 = mybir.dt.bfloat16
AF = mybir.ActivationFunctionType
ALU = mybir.AluOpType
AX = mybir.AxisListType

ENGS = ("pe", "act", "dve", "pool", "sp")


class Op:
    __slots__ = ("eng", "fn", "deps", "dma", "idx", "sig", "signal")

    def __init__(self, eng, fn, dma):
        self.eng = eng
        self.fn = fn
        self.deps = []
        self.dma = dma
        self.sig = None
        self.signal = False


class Prog:
    def __init__(self, nc):
        self.nc = nc
        self.ops = []
        self.last_w = {}
        self.readers = {}
        self.final_waits = []

    def op(self, eng, fn, reads=(), writes=(), dma=None):
        o = Op(eng, fn, dma)
        o.idx = len(self.ops)
        pfr = [r for r in reads if r.startswith("pf")]
        if pfr:
            reads = [r for r in reads if not r.startswith("pf")]
            writes = list(writes) + [r for r in pfr if r not in writes]
        deps = set()
        for r in reads:
            w = self.last_w.get(r)
            if w is not None:
                deps.add(w)
        for wkey in writes:
            w = self.last_w.get(wkey)
            if w is not None:
                deps.add(w)
            for rd in self.readers.get(wkey, ()):
                deps.add(rd)
        for d in deps:
            if d is o:
                continue
            if d.dma is None and d.eng == eng:
                if eng == "pe":
                    continue
            o.deps.append(d)
        for r in reads:
            self.readers.setdefault(r, []).append(o)
        for wkey in writes:
            self.last_w[wkey] = o
            self.readers[wkey] = []
        self.ops.append(o)
        return o

    def barrier(self):
        last = {}
        for o in self.ops:
            if o.fn is None:
                continue
            k = ("dma", o.dma) if o.dma is not None else ("eng", o.eng)
            last[k] = o
        for e in ENGS:
            b = Op(e, None, None)
            b.idx = len(self.ops)
            b.deps = [d for k, d in last.items() if not (k == ("eng", e))]
            self.ops.append(b)
        self.last_w = {}
        self.readers = {}

    def emit(self, stack):
        nc = self.nc
        for o in self.ops:
            for d in o.deps:
                d.signal = True
        for o in self.final_waits:
            o.signal = True
        cnt = {}
        for o in self.ops:
            if o.dma is not None:
                k = ("dma", o.dma)
                cnt[k] = cnt.get(k, 0) + 16
                o.sig = (k, cnt[k])
            elif o.signal:
                k = ("eng", o.eng)
                cnt[k] = cnt.get(k, 0) + 1
                o.sig = (k, cnt[k])
        sems = {}
        for k in cnt:
            nm = "s_" + "_".join(str(x) for x in k)
            sems[k] = stack.enter_context(nc.semaphore(nm))
        per_eng = {e: [o for o in self.ops if o.eng == e] for e in ENGS}
        finals = list(self.final_waits)
        block = stack.enter_context(nc.Block())
        nwaits = [0]

        def run(eng_name, eobj):
            waited = {}
            for o in per_eng[eng_name]:
                need = {}
                for d in o.deps:
                    k, v = d.sig
                    if waited.get(k, 0) >= v:
                        continue
                    if need.get(k, 0) < v:
                        need[k] = v
                for k, v in need.items():
                    eobj.wait_ge(sems[k], v)
                    waited[k] = v
                    nwaits[0] += 1
                if o.fn is None:
                    continue
                ins = o.fn(eobj)
                if o.dma is not None:
                    ins.then_inc(sems[o.sig[0]], 16)
                elif o.signal:
                    ins.then_inc(sems[o.sig[0]], 1)
            if eng_name == "sp":
                need = {}
                for d in finals:
                    k, v = d.sig
                    if need.get(k, 0) < v:
                        need[k] = v
                for k, v in need.items():
                    if waited.get(k, 0) < v:
                        eobj.wait_ge(sems[k], v)

        @block.tensor
        def _(e):
            run("pe", e)

        @block.scalar
        def _(e):
            run("act", e)

        @block.vector
        def _(e):
            run("dve", e)

        @block.gpsimd
        def _(e):
            run("pool", e)

        @block.sync
        def _(e):
            run("sp", e)
        return nwaits[0]

import numpy as np
import os
from concourse.bass_utils import run_bass_kernel_spmd

D = 1024
NH = 8
DH = 128
DFF = 4096
EPS = 1e-6
ARENA = 105600


class StopBuild(Exception):
    pass


class Arena:
    def __init__(self, t):
        self.t = t
        self.off = 0
        self.peak = 0

    def bf(self, n):
        n2 = (n + 15) // 16 * 16
        o = self.off
        self.off += n2
        self.peak = max(self.peak, self.off)
        assert self.off <= ARENA, ("arena overflow", self.off)
        return self.t[:, o:o + n]

    def f32(self, n):
        n2 = (2 * n + 15) // 16 * 16
        o = self.off
        self.off += n2
        self.peak = max(self.peak, self.off)
        assert self.off <= ARENA, ("arena overflow", self.off)
        return self.t[:, o:o + 2 * n].bitcast(F32)

    def mark(self):
        return self.off

    def release(self, m):
        self.off = m


class K:
    def __init__(self, P):
        self.P = P

    def mm(self, out, lhsT, rhs, start, stop, r, w):
        return self.P.op("pe", lambda e: e.matmul(out, lhsT=lhsT, rhs=rhs, start=start, stop=stop), r, w)

    def tr(self, out, in_, ident, r, w):
        return self.P.op("pe", lambda e: e.transpose(out=out, in_=in_, identity=ident), r, w)

    def act(self, out, in_, func, r, w, scale=None, bias=None, accum=None):
        kw = {}
        if scale is not None:
            kw["scale"] = scale
        if bias is not None:
            kw["bias"] = bias
        if accum is not None:
            kw["accum_out"] = accum
        return self.P.op("act", lambda e: e.activation(out=out, in_=in_, func=func, **kw), r, w)

    def cp(self, eng, out, in_, r, w):
        if eng == "act":
            return self.P.op("act", lambda e: e.copy(out=out, in_=in_), r, w)
        return self.P.op(eng, lambda e: e.tensor_copy(out=out, in_=in_), r, w)

    def tt(self, eng, out, in0, in1, op, r, w):
        return self.P.op(eng, lambda e: e.tensor_tensor(out=out, in0=in0, in1=in1, op=op), r, w)

    def ts(self, eng, out, in0, s1, op0, r, w, s2=None, op1=None):
        if op1 is None:
            return self.P.op(eng, lambda e: e.tensor_scalar(out=out, in0=in0, scalar1=s1, scalar2=None, op0=op0), r, w)
        return self.P.op(eng, lambda e: e.tensor_scalar(out=out, in0=in0, scalar1=s1, scalar2=s2, op0=op0, op1=op1), r, w)

    def stt(self, out, in0, scalar, in1, op0, op1, r, w):
        return self.P.op("dve", lambda e: e.scalar_tensor_tensor(out=out, in0=in0, scalar=scalar, in1=in1, op0=op0, op1=op1), r, w)

    def memset(self, eng, ap, val, w):
        return self.P.op(eng, lambda e: e.memset(ap, val), (), w)

    def dma(self, eng, out, in_, r, w, key):
        return self.P.op(eng, lambda e: e.dma_start(out=out, in_=in_), r, w, dma=key)


def build(nc, T, NSEQ, dbg=False, stop=None):
    NT = T // 128
    BW = min(512, T)
    NB = T // BW
    TPB = BW // 128
    NT8 = NT * 8
    dram = lambda n, s, kind="ExternalInput": nc.dram_tensor(n, list(s), F32, kind=kind).ap()
    x_d = dram("x", [NSEQ, T, D])
    out_d = dram("out", [NSEQ, T, D], "ExternalOutput")
    wgdn_d = dram("w_gdn", [NH, 128, 8 * 512])
    wfox_d = dram("w_fox", [NH, 128, 8 * 384])
    wsm_d = dram("w_small", [128, 8 * 24])
    wgate_d = dram("w_gate", [8, 128, 8 * 256])
    wpab_d = dram("w_pab", [8, 128, 8 * 256])
    wout_d = dram("w_o", [128, 8 * 1024])
    wup_d = dram("w_up", [32, 128, 8 * 128])
    wdn_d = dram("w_down", [32, 128, 1024])
    convw_d = dram("convw", [128, 24 * 4])
    gmix_d = dram("gmix", [1, D])
    gmlp_d = dram("gmlp", [1, D])
    gnorm4_d = dram("gnorm4", [1, 512])
    alog_d = dram("alog_t", [1, NT8])
    dtb_d = dram("dtb_t", [1, NT8])
    fb_d = dram("fb_t", [1, NT8])
    fqg_d = dram("fqg", [128, 1])
    fkg_d = dram("fkg", [128, 1])
    dbg_d = {}
    if dbg:
        dbg_d["uT"] = dram("d_uT", [128, 8 * T], "ExternalOutput")
        dbg_d["oaT"] = dram("d_oaT", [128, 8 * T], "ExternalOutput")
        dbg_d["obT"] = dram("d_obT", [128, 8 * T], "ExternalOutput")
        dbg_d["tab"] = dram("d_tab", [128, 6 * NT8], "ExternalOutput")
        dbg_d["h"] = dram("d_h", [128, NT * D], "ExternalOutput")

    with contextlib.ExitStack() as st:
        P = Prog(nc)
        k = K(P)
        arena_t = st.enter_context(nc.sbuf_tensor("arena", [128, ARENA], BF16))
        A = Arena(arena_t)
        pf = [st.enter_context(nc.psum_tensor("pf%d" % i, [128, 512], F32)) for i in range(8)]
        pbf = pf[7][:, :].bitcast(BF16)

        def PF(b, c0=0, c1=512):
            return ["pf%d" % b]

        def pq(b, q, n=128):
            return pf[b][:, q * 128:q * 128 + n]

        def pbs(s):
            return pbf[:, s * 128:(s + 1) * 128]

        ident_bf = A.bf(128)
        ones_bf = A.bf(128)
        onesm_bf = A.bf(128)
        mUi_bf = A.bf(128)
        ident_f = A.f32(128)
        ones_f = A.f32(128)
        triU_f = A.f32(128)
        mUs_bd = A.f32(128)
        mLs_bd = A.f32(128)
        mL_off = A.f32(128)
        gmix_bc = A.f32(D)
        gmlp_bc = A.f32(D)
        gnorm4 = A.f32(512)
        alog_bc = A.f32(NT8)
        dtb_bc = A.f32(NT8)
        fb_bc = A.f32(NT8)
        cw = A.f32(96)
        fqg = A.f32(1)
        fkg = A.f32(1)
        eps_c = A.f32(1)
        one_c = A.f32(1)

        k.memset("pool", ones_bf, 1.0, ["ones_bf"])
        k.memset("pool", onesm_bf, 1.0 / 128, ["onesm_bf"])
        k.memset("pool", ones_f, 1.0, ["ones_f"])
        k.memset("pool", eps_c, EPS, ["eps_c"])
        k.memset("pool", one_c, 1.0, ["one_c"])

        def asel(out, in_, step, cm, cmp, r, w):
            P.op("pool", lambda e: e.affine_select(out=out, in_=in_, pattern=[[step, 128]], compare_op=cmp, fill=0.0,
                                                   base=0, channel_multiplier=cm), r, w)
        asel(ident_bf, ones_bf, 1, -1, ALU.is_equal, ["ones_bf"], ["ident_bf"])
        asel(ident_f, ones_f, 1, -1, ALU.is_equal, ["ones_f"], ["ident_f"])
        asel(mUi_bf, ones_bf, 1, -1, ALU.is_ge, ["ones_bf"], ["mUi_bf"])
        asel(triU_f, ones_f, 1, -1, ALU.is_ge, ["ones_f"], ["triU_f"])
        asel(mUs_bd, ones_f, 1, -1, ALU.is_gt, ["ones_f"], ["mUs_bd"])
        asel(mLs_bd, ones_f, -1, 1, ALU.is_gt, ["ones_f"], ["mLs_bd"])
        k.memset("pool", mUs_bd[0:64, 64:128], 0.0, ["mUs_bd"])
        k.memset("pool", mLs_bd[64:128, 0:64], 0.0, ["mLs_bd"])
        k.memset("pool", mL_off, 0.0, ["mL_off"])
        k.memset("pool", mL_off[64:128, 0:64], 1.0, ["mL_off"])
        k.dma("sp", gmix_bc, gmix_d[0:1, :].partition_broadcast(128), [], ["gmix_bc"], "cst1")
        k.dma("sp", gmlp_bc, gmlp_d[0:1, :].partition_broadcast(128), [], ["gmlp_bc"], "cst2")
        k.dma("sp", gnorm4, gnorm4_d[0:1, :].partition_broadcast(128), [], ["gnorm4"], "cst3")
        k.dma("sp", alog_bc, alog_d[0:1, :].partition_broadcast(128), [], ["alog_bc"], "cst4")
        k.dma("sp", dtb_bc, dtb_d[0:1, :].partition_broadcast(128), [], ["dtb_bc"], "cst5")
        k.dma("sp", fb_bc, fb_d[0:1, :].partition_broadcast(128), [], ["fb_bc"], "cst6")
        k.dma("sp", cw, convw_d[:, :], [], ["cw"], "cst7")
        k.dma("sp", fqg, fqg_d[:, :], [], ["fqg"], "cst8")
        k.dma("sp", fkg, fkg_d[:, :], [], ["fkg"], "cst9")

        a1 = A.mark()
        uT_flat = A.bf(8 * T)
        oaT_flat = A.bf(8 * T)
        obT_flat = A.bf(8 * T)
        a1_end = A.mark()
        uT = uT_flat.rearrange("p (c t) -> p c t", c=8)
        oaT = oaT_flat.rearrange("p (c t) -> p c t", c=8)
        obT = obT_flat.rearrange("p (c t) -> p c t", c=8)
        A.release(a1)
        h_flat = A.f32(NT * D)
        hnT_flat = A.bf(8 * T)
        assert A.mark() <= a1_end
        A.release(a1_end)
        h3 = h_flat.rearrange("p (i d) -> p i d", i=NT)
        hnT = hnT_flat.rearrange("p (c t) -> p c t", c=8)

        tabs = {}
        for nm in ["g", "G", "beta", "nbeta", "kds", "decl", "lf", "c", "tot", "cpre", "cend", "ty", "ta", "tb"]:
            tabs[nm] = A.f32(NT8)
        btab_flat = A.f32(8 * NT * NT)
        btab = btab_flat.rearrange("p (h j i) -> p h j i", h=8, j=NT)
        m0 = A.mark()

        def t3(ap):
            return ap.rearrange("p (i h) -> p i h", h=8)

        outs = []

        def chk(name):
            if stop == name:
                raise StopBuild()

        for s in range(NSEQ):
          try:
              A.release(m0)
              xt = [A.f32(D), A.f32(D)]
              junk = A.bf(D)
              ubf = [A.bf(D), A.bf(D)]
              ssc = [A.f32(1), A.f32(1)]
              for i in range(NT):
                  sl = i % 2
                  k.dma("sp", xt[sl], x_d[s, i * 128:(i + 1) * 128, :], [], ["xt%d" % sl], "xt%d" % sl)
                  k.act(junk, xt[sl], AF.Square, ["xt%d" % sl], ["junk", "ss%d" % sl], accum=ssc[sl])
                  k.act(ssc[sl], ssc[sl], AF.Ln, ["ss%d" % sl, "eps_c"], ["ss%d" % sl], scale=1.0 / D, bias=eps_c)
                  k.act(ssc[sl], ssc[sl], AF.Exp, ["ss%d" % sl], ["ss%d" % sl], scale=-0.5)
                  k.stt(ubf[sl], xt[sl], ssc[sl], gmix_bc, ALU.mult, ALU.mult, ["xt%d" % sl, "ss%d" % sl, "gmix_bc"], ["ubf%d" % sl])
                  for c in range(8):
                      k.tr(pbs(c), ubf[sl][:, c * 128:(c + 1) * 128], ident_bf, ["ubf%d" % sl, "ident_bf"], ["pf7"])
                  k.cp("act", uT[:, :, i * 128:(i + 1) * 128], pbf.rearrange("p (c t) -> p c t", c=8),
                       ["pf7"], ["uT%d" % i])
              uT_all = ["uT%d" % i for i in range(NT)]

              def uTb(b):
                  return ["uT%d" % i for i in range(b * TPB, (b + 1) * TPB)]

              if stop == "P1":
                  outs.append(k.dma("pool", dbg_d["uT"][:, :], uT_flat, uT_all, [], "dbgp_uT"))
                  break
              P.barrier()
              A.release(m0)
              wsm = A.bf(8 * 24)
              wsm3 = wsm.rearrange("p (c n) -> p c n", c=8)
              k.dma("pool", wsm, wsm_d[:, :], [], ["wsm"], "wsm")
              psS = pf[0][:, 0:NT * 24]
              for i in range(NT):
                  for c in range(8):
                      k.mm(pf[0][:, i * 24:(i + 1) * 24], uT[:, c, i * 128:(i + 1) * 128], wsm3[:, c, :], c == 0, c == 7,
                           ["uT%d" % i, "wsm"], PF(0))
              ps3 = psS.rearrange("p (i n) -> p i n", n=24)
              ga, gb, ff = ps3[:, :, 0:8], ps3[:, :, 8:16], ps3[:, :, 16:24]
              ty, ta, tb = tabs["ty"], tabs["ta"], tabs["tb"]
              k.tt("dve", t3(ty), ga, t3(dtb_bc), ALU.add, PF(0) + ["dtb_bc"], ["ty"])
              k.stt(ta, ty, -1.0, ty, ALU.mult, ALU.max, ["ty"], ["ta"])
              k.act(ta, ta, AF.Exp, ["ta"], ["ta"], scale=-1.0)
              k.act(ta, ta, AF.Ln, ["ta", "one_c"], ["ta"], bias=one_c)
              k.stt(ty, ty, 0.0, ta, ALU.max, ALU.add, ["ty", "ta"], ["ty"])
              k.act(tb, alog_bc, AF.Exp, ["alog_bc"], ["tb"])
              k.stt(tabs["g"], ty, -1.0, tb, ALU.mult, ALU.mult, ["ty", "tb"], ["g"])
              k.act(t3(tabs["beta"]), gb, AF.Sigmoid, PF(0), ["beta"])
              k.ts("dve", tabs["nbeta"], tabs["beta"], -1.0, ALU.mult, ["beta"], ["nbeta"])
              k.tt("dve", t3(ty), ff, t3(fb_bc), ALU.add, PF(0) + ["fb_bc"], ["ty"])
              k.stt(ta, ty, -1.0, ty, ALU.mult, ALU.max, ["ty"], ["ta"])
              k.act(ta, ta, AF.Exp, ["ta"], ["ta"], scale=-1.0)
              k.act(ta, ta, AF.Ln, ["ta", "one_c"], ["ta"], bias=one_c)
              k.stt(tabs["lf"], ty, 0.0, ta, ALU.min, ALU.subtract, ["ty", "ta"], ["lf"])
              k.mm(pf[1][:, 0:NT8], triU_f, tabs["g"], True, True, ["triU_f", "g"], ["pf1"])
              k.mm(pf[1][:, 128:128 + NT8], ones_f, tabs["g"], True, True, ["ones_f", "g"], ["pf1"])
              k.mm(pf[1][:, 256:256 + NT8], triU_f, tabs["lf"], True, True, ["triU_f", "lf"], ["pf1"])
              k.mm(pf[1][:, 384:384 + NT8], ones_f, tabs["lf"], True, True, ["ones_f", "lf"], ["pf1"])
              k.cp("act", tabs["G"], pf[1][:, 0:NT8], ["pf1"], ["G"])
              k.act(tabs["decl"], pf[1][:, 128:128 + NT8], AF.Exp, ["pf1"], ["decl"])
              k.tt("dve", ty, pf[1][:, 128:128 + NT8], tabs["G"], ALU.subtract, ["pf1", "G"], ["ty"])
              k.act(tabs["kds"], ty, AF.Exp, ["ty"], ["kds"])
              k.cp("act", tabs["tot"], pf[1][:, 384:384 + NT8], ["pf1"], ["tot"])
              k.memset("dve", tabs["cpre"][:, 0:8], 0.0, ["cpre"])
              for i in range(1, NT):
                  k.tt("dve", tabs["cpre"][:, i * 8:(i + 1) * 8], tabs["cpre"][:, (i - 1) * 8:i * 8],
                       tabs["tot"][:, (i - 1) * 8:i * 8], ALU.add, ["cpre", "tot"], ["cpre"])
              k.tt("dve", tabs["c"], pf[1][:, 256:256 + NT8], tabs["cpre"], ALU.add, ["pf1", "cpre"], ["c"])
              k.tt("dve", tabs["cend"], tabs["cpre"], tabs["tot"], ALU.add, ["cpre", "tot"], ["cend"])
              cend3 = t3(tabs["cend"])
              n_ = 0
              for h in range(NH):
                  for j in range(NT):
                      eng = "dve" if n_ % 2 == 0 else "pool"
                      n_ += 1
                      col = j * 8 + h
                      k.ts(eng, btab[:, h, j, :], cend3[:, :, h], tabs["c"][:, col:col + 1], ALU.subtract, ["cend", "c"], ["btab"])

              if stop == "SMALL":
                  outs.append(k.dma("pool", dbg_d["uT"][:, :], uT_flat, uT_all, [], "dbgp_uT"))
                  for ti, nm in enumerate(["g", "G", "beta", "kds", "decl", "c"]):
                      outs.append(k.dma("sp", dbg_d["tab"][:, ti * NT8:(ti + 1) * NT8], tabs[nm], [nm], [], "dbg"))
                  break
              P.barrier()
              A.release(m0)
              wg = [A.bf(8 * 512)]
              GRP = min(NT, int(os.environ.get("GDN_GRP", "4")))
              GW = GRP * 128
              NG = NT // GRP
              qT, kT, vT = A.bf(T), A.bf(T), A.bf(T)
              Kd_f, V_f, gz_f = A.bf(T), A.f32(T), A.f32(T)
              Kd = Kd_f.rearrange("p (i d) -> p i d", d=128)
              Vt = V_f.rearrange("p (i d) -> p i d", d=128)
              gz = gz_f.rearrange("p (i d) -> p i d", d=128)
              S_f = A.f32(128)
              S_bf, R_bf, vnew, on_bf, junk128 = [A.bf(128) for _ in range(5)]
              ssq = A.f32(1)
              mp = A.mark()
              sq = [A.bf(BW), A.bf(BW)]
              cb = [A.f32(BW + 3), A.f32(BW + 3)]
              acc = [A.f32(BW), A.f32(BW)]
              lnt = [A.f32(BW), A.f32(BW)]
              rs = [A.f32(BW), A.f32(BW)]
              tmpz = A.f32(BW)
              A.release(mp)
              grep_ = [A.f32(128), A.f32(128)]
              brep_ = [A.f32(128), A.f32(128)]
              E_, Y_, M1_, DuI, DuB, DlB, DlO, bbc = [A.f32(GW) for _ in range(8)]
              eG = A.bf(GW)
              Qb = [A.bf(GW), A.bf(GW)]
              Pb = [A.bf(GW), A.bf(GW)]
              Mb = [A.bf(GW), A.bf(GW)]
              Ao, Td, X1 = [A.bf(GW) for _ in range(3)]
              AttnT = [A.bf(GW), A.bf(GW)]
              QeT = [A.bf(GW), A.bf(GW)]
              KeT = [A.bf(GW), A.bf(GW)]
              TpT = [A.bf(GW), A.bf(GW)]

              def fm_norm(src, src_r, dst, dst_w, ones_t, ones_r, mult, mult_r, pbank, par):
                  k.act(sq[par], src, AF.Square, src_r, ["sq%d" % par])
                  k.mm(pf[pbank][:, 0:BW], ones_t, sq[par], True, True, ["sq%d" % par] + ones_r, PF(pbank))
                  k.act(lnt[par], pf[pbank][:, 0:BW], AF.Ln, PF(pbank) + ["eps_c"], ["lnt%d" % par], bias=eps_c)
                  k.act(rs[par], lnt[par], AF.Exp, ["lnt%d" % par], ["rs%d" % par], scale=-0.5)
                  k.stt(dst, src, mult, rs[par], ALU.mult, ALU.mult, src_r + ["rs%d" % par] + mult_r, dst_w)

              pcnt = 0
              for h in range(NH):
                  ws = 0
                  wgs = wg[ws]
                  wg3 = wgs.rearrange("p (c n) -> p c n", c=8)
                  k.dma("pool", wgs, wgdn_d[h, :, :], [], ["wg%d" % ws], "wg%d" % ws)
                  def zgate(ws=ws, wg3=wg3):
                      for i in range(NT):
                          q4 = i % 4
                          for c in range(8):
                              k.mm(pq(4, q4), uT[:, c, i * 128:(i + 1) * 128], wg3[:, c, 384:512], c == 0, c == 7,
                                   ["uT%d" % i, "wg%d" % ws], ["pf4"])
                          if q4 == TPB - 1 or i == NT - 1:
                              n4 = q4 + 1
                              i0 = i - q4
                              k.act(tmpz[:, 0:n4 * 128], pf[4][:, 0:n4 * 128], AF.Silu, ["pf4"], ["tmpz"])
                              k.tt("dve", gz_f[:, i0 * 128:(i0 + n4) * 128], tmpz[:, 0:n4 * 128], gnorm4[:, 0:n4 * 128], ALU.mult,
                                   ["tmpz", "gnorm4"], ["gz%d" % ii for ii in range(i0, i0 + n4)])
                          yield
                  zgen = zgate()
                  for grp in range(3):
                      chunk = grp * 8 + h
                      dstT = (qT, kT, vT)[grp]
                      dnm = ("qT", "kT", "vT")[grp]
                      for b in range(NB):
                          pb = b % 2
                          pr = pcnt % 2
                          pcnt += 1
                          cbn, accn = "cb%d" % pr, "acc%d" % pr
                          cbc, accc = cb[pr], acc[pr]
                          for c in range(8):
                              k.mm(pf[pb][:, 0:BW], wg3[:, c, grp * 128:(grp + 1) * 128], uT[:, c, b * BW:(b + 1) * BW],
                                   c == 0, c == 7, ["wg%d" % ws] + uTb(b), PF(pb))
                          if b == 0:
                              k.memset("pool", cbc[:, 0:3], 0.0, [cbn])
                          else:
                              k.cp("act", cbc[:, 0:3], cb[1 - pr][:, BW:BW + 3], ["cb%d" % (1 - pr)], [cbn])
                          k.cp("act", cbc[:, 3:3 + BW], pf[pb][:, 0:BW], PF(pb), [cbn])
                          k.ts("dve", accc, cbc[:, 0:BW], cw[:, chunk * 4:chunk * 4 + 1], ALU.mult, [cbn, "cw"], [accn])
                          for tap in range(1, 4):
                              k.stt(accc, cbc[:, tap:tap + BW], cw[:, chunk * 4 + tap:chunk * 4 + tap + 1], accc, ALU.mult, ALU.add,
                                    [cbn, "cw", accn], [accn])
                          k.act(accc, accc, AF.Silu, [accn], [accn])
                          dsl = dstT[:, b * BW:(b + 1) * BW]
                          dw = ["%s%d" % (dnm, i) for i in range(b * TPB, (b + 1) * TPB)]
                          if grp < 2:
                              fm_norm(accc, [accn], dsl, dw, ones_bf, ["ones_bf"], (DH ** -0.5) if grp == 0 else 1.0, [], 2 + pr, pr)
                          else:
                              k.cp("pool", dsl, accc, [accn], dw)
                          next(zgen, None)
                          next(zgen, None)
                  chk("GDN_a")
                  for _ in zgen:
                      pass
                  chk("GDN_b")
                  for i in range(NT):
                      col = i * 8 + h
                      sk = (2 * i) % 8
                      k.tr(pbs(sk), kT[:, i * 128:(i + 1) * 128], ident_bf, ["kT%d" % i, "ident_bf"], ["pf7"])
                      k.act(Kd[:, i, :], pbs(sk), AF.Copy, ["pf7", "kds"], ["Kd%d" % i], scale=tabs["kds"][:, col:col + 1])
                      sv = (2 * i + 1) % 8
                      k.tr(pbs(sv), vT[:, i * 128:(i + 1) * 128], ident_bf, ["vT%d" % i, "ident_bf"], ["pf7"])
                      k.cp("act", Vt[:, i, :], pbs(sv), ["pf7"], ["V%d" % i])
                  chk("GDN_c")
                  P.barrier()
                  k.memset("pool", S_f, 0.0, ["S_f"])
                  k.memset("pool", S_bf, 0.0, ["S_bf"])

                  def v3(ap):
                      return ap.rearrange("p (g f) -> p g f", g=GRP)

                  def bcm(m):
                      return m.unsqueeze(1).to_broadcast([128, GRP, 128])

                  def colbc(tab, i0, h_):
                      return t3(tab)[:, i0:i0 + GRP, h_].unsqueeze(2).to_broadcast([128, GRP, 128])

                  def gs(t):
                      return slice(t * 128, (t + 1) * 128)

                  def prep(g, h=h):
                      i0 = g * GRP
                      gp = g % 2
                      b0v, b1v, b2v, b3v = [pf[b_][:, 0:GW] for b_ in range(4)]
                      for t in range(GRP):
                          col = (i0 + t) * 8 + h
                          r2 = t % 2
                          k.cp("dve", grep_[r2], tabs["g"][:, col:col + 1].to_broadcast([128, 128]), ["g"], ["grep%d" % r2])
                          k.cp("dve", brep_[r2], tabs["beta"][:, col:col + 1].to_broadcast([128, 128]), ["beta"], ["brep%d" % r2])
                          k.mm(pq(0, t), grep_[r2], triU_f, True, True, ["grep%d" % r2, "triU_f"], ["pf0"])
                          k.mm(pq(1, t), brep_[r2], ident_f, True, True, ["brep%d" % r2, "ident_f"], ["pf1"])
                      k.cp("act", bbc, b1v, ["pf1"], ["bbc"])
                      k.act(eG, b0v, AF.Exp, ["pf0"], ["eG"])
                      k.tt("dve", v3(E_), v3(b0v), colbc(tabs["G"], i0, h), ALU.subtract, ["pf0", "G"], ["E"])
                      k.ts("dve", Y_, E_, 0.0, ALU.max, ["E"], ["Y"])
                      k.ts("dve", E_, E_, 0.0, ALU.min, ["E"], ["E"])
                      k.act(Y_, Y_, AF.Exp, ["Y"], ["Y"], scale=-1.0)
                      k.act(E_, E_, AF.Exp, ["E"], ["E"])
                      k.tt("pool", v3(M1_), v3(bbc), bcm(mUs_bd), ALU.mult, ["bbc", "mUs_bd"], ["M1"])
                      k.tt("dve", v3(DuI), v3(E_), bcm(triU_f), ALU.mult, ["E", "triU_f"], ["DuI"])
                      k.tt("dve", DuB, E_, M1_, ALU.mult, ["E", "M1"], ["DuB"])
                      k.tt("dve", v3(DlB), v3(Y_), bcm(mLs_bd), ALU.mult, ["Y", "mLs_bd"], ["DlB"])
                      k.tt("pool", v3(DlB), v3(DlB), colbc(tabs["beta"], i0, h), ALU.mult, ["DlB", "beta"], ["DlB"])
                      k.tt("dve", v3(DlO), v3(Y_), bcm(mL_off), ALU.mult, ["Y", "mL_off"], ["DlO"])
                      k.tt("pool", v3(DlO), v3(DlO), colbc(tabs["beta"], i0, h), ALU.mult, ["DlO", "beta"], ["DlO"])
                      yield
                      for t in range(GRP):
                          i = i0 + t
                          kTi = kT[:, i * 128:(i + 1) * 128]
                          qTi = qT[:, i * 128:(i + 1) * 128]
                          k.mm(pq(2, t), kTi, kTi, True, True, ["kT%d" % i], ["pf2"])
                          k.mm(pq(3, t), kTi, qTi, True, True, ["kT%d" % i, "qT%d" % i], ["pf3"])
                      kr = ["kT%d" % (i0 + t) for t in range(GRP)]
                      qr = ["qT%d" % (i0 + t) for t in range(GRP)]
                      k.tt("dve", Qb[0], b2v, DlB, ALU.mult, ["pf2", "DlB"], ["Q0"])
                      k.tt("dve", Ao, b2v, DlO, ALU.mult, ["pf2", "DlO"], ["Ao"])
                      k.tt("dve", Pb[0], b2v, DuB, ALU.mult, ["pf2", "DuB"], ["P0"])
                      k.tt("dve", AttnT[gp], b3v, DuI, ALU.mult, ["pf3", "DuI"], ["AttnT%d" % gp])
                      k.tt("dve", v3(Mb[0]), bcm(ident_bf), v3(Pb[0]), ALU.subtract, ["ident_bf", "P0"], ["M0"])
                      k.tt("dve", QeT[gp], qT[:, i0 * 128:i0 * 128 + GW], eG, ALU.mult, qr + ["eG"], ["QeT%d" % gp])
                      k.tt("dve", KeT[gp], kT[:, i0 * 128:i0 * 128 + GW], eG, ALU.mult, kr + ["eG"], ["KeT%d" % gp])
                      cur = 0
                      for lvl in range(1, 6):
                          nxt = 1 - cur
                          for t in range(GRP):
                              if lvl < 5:
                                  k.mm(pq(0, t), Qb[cur][:, gs(t)], Pb[cur][:, gs(t)], True, True, ["Q%d" % cur, "P%d" % cur], ["pf0"])
                              k.mm(pq(1, t), Pb[cur][:, gs(t)], Qb[cur][:, gs(t)], True, True, ["Q%d" % cur, "P%d" % cur], ["pf1"])
                          if lvl < 5:
                              k.cp("act", Pb[nxt], b0v, ["pf0"], ["P%d" % nxt])
                          k.cp("dve", Qb[nxt], b1v, ["pf1"], ["Q%d" % nxt])
                          for t in range(GRP):
                              k.mm(pq(2, t), ident_bf, Mb[cur][:, gs(t)], True, False, ["ident_bf", "M%d" % cur], ["pf2"])
                              k.mm(pq(2, t), Qb[nxt][:, gs(t)], Mb[cur][:, gs(t)], False, True, ["Q%d" % nxt, "M%d" % cur], ["pf2"])
                          k.cp("act", Mb[nxt], b2v, ["pf2"], ["M%d" % nxt])
                          if lvl == 5:
                              k.cp("dve", Y_, b2v, ["pf2"], ["Y"])
                          cur = nxt
                          if lvl == 2:
                              yield
                      yield
                      Ud = Mb[cur]
                      Udn = "M%d" % cur
                      for t in range(GRP):
                          k.tr(pbs(t), Ud[:, gs(t)], ident_bf, [Udn, "ident_bf"], ["pf7"])
                      k.cp("act", Td, pbf[:, 0:GW], ["pf7"], ["Td"])
                      for t in range(GRP):
                          k.mm(pq(3, t), Ao[:, gs(t)], Ud[:, gs(t)], True, True, ["Ao", Udn], ["pf3"])
                      k.cp("act", X1, b3v, ["pf3"], ["X1"])
                      for t in range(GRP):
                          k.mm(pq(0, t), Td[:, gs(t)], X1[:, gs(t)], True, True, ["Td", "X1"], ["pf0"])
                      k.tt("dve", Y_, Y_, b0v, ALU.subtract, ["Y", "pf0"], ["Y"])
                      k.tt("pool", v3(TpT[gp]), v3(Y_), colbc(tabs["beta"], i0, h), ALU.mult, ["Y", "beta"], ["TpT%d" % gp])
                      yield

                  def scan_step(i, h=h):
                      g = i // GRP
                      t = i % GRP
                      gp = g % 2
                      d2 = i % 2
                      col = i * 8 + h
                      k.mm(pq(4, 0), KeT[gp][:, gs(t)], S_bf, True, True, ["KeT%d" % gp, "S_bf"], ["pf4"])
                      k.tt("dve", R_bf, Vt[:, i, :], pq(4, 0), ALU.subtract, ["V%d" % i, "pf4"], ["R_bf"])
                      k.mm(pq(4, 1), TpT[gp][:, gs(t)], R_bf, True, True, ["TpT%d" % gp, "R_bf"], ["pf4"])
                      k.cp("act", vnew, pq(4, 1), ["pf4"], ["vnew"])
                      k.mm(pq(5 + d2, 0), QeT[gp][:, gs(t)], S_bf, True, False, ["QeT%d" % gp, "S_bf"], ["pf%d" % (5 + d2)])
                      k.mm(pq(5 + d2, 0), AttnT[gp][:, gs(t)], vnew, False, True, ["AttnT%d" % gp, "vnew"], ["pf%d" % (5 + d2)])
                      k.mm(pq(4, 3), Kd[:, i, :], vnew, True, True, ["Kd%d" % i, "vnew"], ["pf4"])
                      k.stt(S_f, S_f, tabs["decl"][:, col:col + 1], pq(4, 3), ALU.mult, ALU.add, ["S_f", "decl", "pf4"], ["S_f"])
                      k.cp("act", S_bf, S_f, ["S_f"], ["S_bf"])
                      k.act(junk128, pq(5 + d2, 0), AF.Square, ["pf%d" % (5 + d2)], ["junk128", "ssq"], accum=ssq)
                      k.act(ssq, ssq, AF.Ln, ["ssq", "eps_c"], ["ssq"], scale=1.0 / DH, bias=eps_c)
                      k.act(ssq, ssq, AF.Exp, ["ssq"], ["ssq"], scale=-0.5)
                      k.stt(on_bf, pq(5 + d2, 0), ssq, gz[:, i, :], ALU.mult, ALU.mult, ["pf%d" % (5 + d2), "ssq", "gz%d" % i], ["on_bf"])
                      so = 4 + (i % 4)
                      k.tr(pbs(so), on_bf, ident_bf, ["on_bf", "ident_bf"], ["pf7"])
                      k.cp("act", oaT[:, h, i * 128:(i + 1) * 128], pbs(so), ["pf7"], ["oaT%d_%d" % (h, i)])

                  for _ in prep(0):
                      pass
                  NST = 4
                  per = (NST + GRP - 1) // GRP
                  for g in range(NG):
                      gen = prep(g + 1) if g + 1 < NG else iter(())
                      for t in range(GRP):
                          scan_step(g * GRP + t)
                          for _ in range(per):
                              next(gen, None)
                      for _ in gen:
                          pass
                  P.barrier()
                  chk("GDN_h0")

              chk("GDN2")
              P.barrier()
              A.release(m0)
              wf = [A.bf(8 * 384), A.bf(8 * 384)]
              fqT, fkT, fV_f = A.bf(T), A.bf(T), A.bf(T)
              fV = fV_f.rearrange("p (i d) -> p i d", d=128)
              sq = [A.bf(BW), A.bf(BW)]
              lnt = [A.f32(BW), A.f32(BW)]
              rs = [A.f32(BW), A.f32(BW)]
              wff = A.f32(8 * 384)
              PT = [A.bf(BW), A.bf(BW)]
              rden = [A.f32(BW), A.f32(BW)]
              scale = DH ** -0.5
              for h in range(NH):
                  ws = h % 2
                  wfs = wf[ws]
                  wf3 = wfs.rearrange("p (c n) -> p c n", c=8)
                  k.dma("sp", wff, wfox_d[h, :, :], [], ["wff"], "wff")
                  for cc in range(3):
                      k.cp("pool", wfs[:, cc * 1024:(cc + 1) * 1024], wff[:, cc * 1024:(cc + 1) * 1024], ["wff"], ["wf%d" % ws])
                  def fvproj(ws=ws, wf3=wf3):
                      for i in range(NT):
                          q4 = i % 4
                          for c in range(8):
                              k.mm(pq(5, q4), uT[:, c, i * 128:(i + 1) * 128], wf3[:, c, 256:384], c == 0, c == 7,
                                   ["uT%d" % i, "wf%d" % ws], ["pf5"])
                          if q4 == TPB - 1 or i == NT - 1:
                              n4 = q4 + 1
                              i0 = i - q4
                              k.cp("act", fV_f[:, i0 * 128:(i0 + n4) * 128], pf[5][:, 0:n4 * 128], ["pf5"],
                                   ["fV%d" % ii for ii in range(i0, i0 + n4)])
                          yield
                  fvgen = fvproj()
                  for grp in range(2):
                      dstT = (fqT, fkT)[grp]
                      dnm = ("fqT", "fkT")[grp]
                      gcol = (fqg, fkg)[grp]
                      gnm = ("fqg", "fkg")[grp]
                      for b in range(NB):
                          pb = b % 2
                          for c in range(8):
                              k.mm(pf[pb][:, 0:BW], wf3[:, c, grp * 128:(grp + 1) * 128], uT[:, c, b * BW:(b + 1) * BW],
                                   c == 0, c == 7, ["wf%d" % ws] + uTb(b), PF(pb))
                          dw = ["%s%d" % (dnm, i) for i in range(b * TPB, (b + 1) * TPB)]
                          pr = pcnt % 2
                          pcnt += 1
                          fm_norm(pf[pb][:, 0:BW], PF(pb), dstT[:, b * BW:(b + 1) * BW], dw, onesm_bf, ["onesm_bf"],
                                  gcol, [gnm], (2, 7)[pr], pr)
                          next(fvgen, None)
                          next(fvgen, None)
                  for _ in fvgen:
                      pass
                  cnt = 0
                  for Ib in range(NB):
                      bo, bd = ((5, 6), (0, 1))[Ib % 2]
                      last_j = (Ib + 1) * TPB - 1
                      for j in range(last_j + 1):
                          r0 = max(0, j - Ib * TPB)
                          qoff = r0 * 128
                          sb_ = 3 + (cnt % 2)
                          ptb = cnt % 2
                          cnt += 1
                          qr = ["fqT%d" % (Ib * TPB + r) for r in range(r0, TPB)]
                          k.mm(pf[sb_][:, qoff:BW], fkT[:, j * 128:(j + 1) * 128], fqT[:, Ib * BW + qoff:(Ib + 1) * BW], True, True,
                               ["fkT%d" % j] + qr, PF(sb_))
                          for r in range(r0, TPB):
                              i = Ib * TPB + r
                              k.act(PT[ptb][:, r * 128:(r + 1) * 128], pf[sb_][:, r * 128:(r + 1) * 128], AF.Exp,
                                    PF(sb_) + ["btab"], ["PT%d" % ptb], scale=scale, bias=btab[:, h, j, i:i + 1])
                              if i == j:
                                  k.tt("dve", PT[ptb][:, r * 128:(r + 1) * 128], PT[ptb][:, r * 128:(r + 1) * 128], mUi_bf, ALU.mult,
                                       ["PT%d" % ptb, "mUi_bf"], ["PT%d" % ptb])
                          k.mm(pf[bo][:, qoff:BW], fV[:, j, :], PT[ptb][:, qoff:BW], j == 0, j == last_j,
                               ["fV%d" % j, "PT%d" % ptb], PF(bo))
                          k.mm(pf[bd][:, qoff:BW], ones_bf, PT[ptb][:, qoff:BW], j == 0, j == last_j,
                               ["ones_bf", "PT%d" % ptb], PF(bd))
                      rdn = rden[Ib % 2]
                      k.P.op("dve", lambda e, rdn=rdn, bd=bd: e.reciprocal(out=rdn, in_=pf[bd][:, 0:BW]), PF(bd), ["rden%d" % (Ib % 2)])
                      k.tt("dve", obT[:, h, Ib * BW:(Ib + 1) * BW], pf[bo][:, 0:BW], rdn, ALU.mult, PF(bo) + ["rden%d" % (Ib % 2)],
                           ["obT%d_%d" % (h, Ib)])

              if dbg and s == 0:
                  outs.append(k.dma("pool", dbg_d["uT"][:, :], uT_flat, uT_all, [], "dbgp_uT"))
                  outs.append(k.dma("pool", dbg_d["oaT"][:, :], oaT_flat, ["oaT%d_%d" % (h, i) for h in range(NH) for i in range(NT)], [], "dbgp_oaT"))
                  outs.append(k.dma("pool", dbg_d["obT"][:, :], obT_flat, ["obT%d_%d" % (h, b) for h in range(NH) for b in range(NB)], [], "dbgp_obT"))
                  for ti, nm in enumerate(["g", "G", "beta", "kds", "decl", "c"]):
                      outs.append(k.dma("sp", dbg_d["tab"][:, ti * NT8:(ti + 1) * NT8], tabs[nm], [nm], [], "dbg"))

              chk("FOX")
              P.barrier()
              A.release(m0)
              mT_flat = A.bf(8 * T)
              mT = mT_flat.rearrange("p (c t) -> p c t", c=8)
              m_mT = A.mark()
              wga = [A.bf(8 * 256), A.bf(8 * 256)]
              wpa = [A.bf(8 * 256), A.bf(8 * 256)]
              sA, sB, mA, mB = [A.f32(BW) for _ in range(4)]
              wgaf = A.f32(8 * 256)
              wpaf = A.f32(8 * 256)
              oa_all = {b: ["oaT%d_%d" % (h, i) for h in range(NH) for i in range(b * TPB, (b + 1) * TPB)] for b in range(NB)}
              ob_all = {b: ["obT%d_%d" % (h, b) for h in range(NH)] for b in range(NB)}
              for n in range(8):
                  ws = n % 2
                  k.dma("sp", wgaf, wgate_d[n, :, :], [], ["wgaf"], "wgaf")
                  k.dma("sp", wpaf, wpab_d[n, :, :], [], ["wpaf"], "wpaf")
                  for cc in range(2):
                      k.cp("pool", wga[ws][:, cc * 1024:(cc + 1) * 1024], wgaf[:, cc * 1024:(cc + 1) * 1024], ["wgaf"], ["wga%d" % ws])
                      k.cp("pool", wpa[ws][:, cc * 1024:(cc + 1) * 1024], wpaf[:, cc * 1024:(cc + 1) * 1024], ["wpaf"], ["wpa%d" % ws])
                  wga3 = wga[ws].rearrange("p (c n) -> p c n", c=8)
                  wpa3 = wpa[ws].rearrange("p (c n) -> p c n", c=8)
                  for b in range(NB):
                      bs = slice(b * BW, (b + 1) * BW)
                      o4 = 4 * ((n * NB + b) % 2)
                      for c in range(8):
                          k.mm(pf[o4 + 0][:, 0:BW], wpa3[:, c, 0:128], oaT[:, c, bs], c == 0, c == 7, ["wpa%d" % ws] + oa_all[b], PF(o4 + 0))
                      for c in range(8):
                          k.mm(pf[o4 + 1][:, 0:BW], wga3[:, c, 0:128], uT[:, c, bs], c == 0, c == 7, ["wga%d" % ws] + uTb(b), PF(o4 + 1))
                      for c in range(8):
                          k.mm(pf[o4 + 2][:, 0:BW], wpa3[:, c, 128:256], obT[:, c, bs], c == 0, c == 7, ["wpa%d" % ws] + ob_all[b], PF(o4 + 2))
                      for c in range(8):
                          k.mm(pf[o4 + 3][:, 0:BW], wga3[:, c, 128:256], uT[:, c, bs], c == 0, c == 7, ["wga%d" % ws] + uTb(b), PF(o4 + 3))
                      k.act(sA, pf[o4 + 1][:, 0:BW], AF.Sigmoid, PF(o4 + 1), ["sA"])
                      k.act(sB, pf[o4 + 3][:, 0:BW], AF.Sigmoid, PF(o4 + 3), ["sB"])
                      k.tt("dve", mA, pf[o4 + 0][:, 0:BW], sA, ALU.mult, PF(o4 + 0) + ["sA"], ["mA"])
                      k.tt("dve", mB, pf[o4 + 2][:, 0:BW], sB, ALU.mult, PF(o4 + 2) + ["sB"], ["mB"])
                      k.tt("dve", mT[:, n, bs], mA, mB, ALU.add, ["mA", "mB"], ["mT%d_%d" % (n, b)])

              P.barrier()
              A.release(m_mT)
              wo = A.bf(8 * 1024)
              wo3 = wo.rearrange("p (c n) -> p c n", c=8)
              xt = [A.f32(D), A.f32(D)]
              hnb = [A.bf(D), A.bf(D)]
              junk = A.bf(D)
              ssc = [A.f32(1), A.f32(1)]
              k.dma("pool", wo, wout_d[:, :], [], ["wo"], "wo")
              for i in range(NT):
                  sl = i % 2
                  k.dma("sp", xt[sl], x_d[s, i * 128:(i + 1) * 128, :], [], ["xt%d" % sl], "xt%d" % sl)
                  for half in range(2):
                      for c in range(8):
                          k.mm(pf[half][:, :], mT[:, c, i * 128:(i + 1) * 128], wo3[:, c, half * 512:(half + 1) * 512], c == 0, c == 7,
                               ["wo"] + ["mT%d_%d" % (n_, i // TPB) for n_ in range(8)], PF(half))
                      k.tt("dve", h3[:, i, half * 512:(half + 1) * 512], pf[half][:, :], xt[sl][:, half * 512:(half + 1) * 512], ALU.add,
                           PF(half) + ["xt%d" % sl], ["h%d_%d" % (i, half)])
                  hr = ["h%d_0" % i, "h%d_1" % i]
                  k.act(junk, h3[:, i, :], AF.Square, hr, ["junk", "ss%d" % sl], accum=ssc[sl])
                  k.act(ssc[sl], ssc[sl], AF.Ln, ["ss%d" % sl, "eps_c"], ["ss%d" % sl], scale=1.0 / D, bias=eps_c)
                  k.act(ssc[sl], ssc[sl], AF.Exp, ["ss%d" % sl], ["ss%d" % sl], scale=-0.5)
                  k.stt(hnb[sl], h3[:, i, :], ssc[sl], gmlp_bc, ALU.mult, ALU.mult, hr + ["ss%d" % sl, "gmlp_bc"], ["hnb%d" % sl])
                  for c in range(8):
                      k.tr(pbs(c), hnb[sl][:, c * 128:(c + 1) * 128], ident_bf, ["hnb%d" % sl, "ident_bf"], ["pf7"])
                  k.cp("act", hnT[:, :, i * 128:(i + 1) * 128], pbf.rearrange("p (c t) -> p c t", c=8),
                       ["pf7"], ["hnT%d" % i])
              if dbg and s == 0:
                  outs.append(k.dma("sp", dbg_d["h"][:, :], h_flat, ["h%d_%d" % (i, hf) for i in range(NT) for hf in range(2)], [], "dbg"))

              chk("POSTH")
              P.barrier()
              A.release(m0)
              aT_flat = A.bf(32 * BW)
              aT = aT_flat.rearrange("p (f t) -> p f t", f=32)
              wup = [A.bf(1024) for _ in range(4)]
              wdn = [A.bf(1024) for _ in range(4)]
              wstg = [A.f32(1024) for _ in range(3)]
              scnt = 0
              rtmp = [A.f32(BW), A.f32(BW)]
              otile = [A.f32(D), A.f32(D)]
              for b in range(NB):
                  bs = slice(b * BW, (b + 1) * BW)
                  hb = ["hnT%d" % i for i in range(b * TPB, (b + 1) * TPB)]
                  for f in range(32):
                      ws = f % 4
                      sg = scnt % 3
                      scnt += 1
                      k.dma("sp", wstg[sg], wup_d[f, :, :], [], ["wstg%d" % sg], "wstg%d" % sg)
                      k.cp("pool", wup[ws], wstg[sg], ["wstg%d" % sg], ["wup%d" % ws])
                      wu3 = wup[ws].rearrange("p (c n) -> p c n", c=8)
                      pb = f % 2
                      for c in range(8):
                          k.mm(pf[pb][:, 0:BW], wu3[:, c, :], hnT[:, c, bs], c == 0, c == 7, ["wup%d" % ws] + hb, PF(pb))
                      k.act(rtmp[pb], pf[pb][:, 0:BW], AF.Relu, PF(pb), ["rtmp%d" % pb])
                      k.tt("dve", aT[:, f, :], rtmp[pb], rtmp[pb], ALU.mult, ["rtmp%d" % pb], ["aT%d" % f])
                  for f in range(32):
                      ws = f % 4
                      sg = scnt % 3
                      scnt += 1
                      k.dma("sp", wstg[sg], wdn_d[f, :, :], [], ["wstg%d" % sg], "wstg%d" % sg)
                      k.cp("pool", wdn[ws], wstg[sg], ["wstg%d" % sg], ["wdn%d" % ws])
                      for t in range(TPB):
                          for half in range(2):
                              bk = t * 2 + half
                              k.mm(pf[bk][:, :], aT[:, f, t * 128:(t + 1) * 128], wdn[ws][:, half * 512:(half + 1) * 512], f == 0, f == 31,
                                   ["aT%d" % f, "wdn%d" % ws], PF(bk))
                  for t in range(TPB):
                      i = b * TPB + t
                      sl = i % 2
                      for half in range(2):
                          bk = t * 2 + half
                          k.tt("dve", otile[sl][:, half * 512:(half + 1) * 512], pf[bk][:, :], h3[:, i, half * 512:(half + 1) * 512], ALU.add,
                               PF(bk) + ["h%d_%d" % (i, half)], ["ot%d_%d" % (sl, half)])
                      outs.append(k.dma("sp", out_d[s, i * 128:(i + 1) * 128, :], otile[sl], ["ot%d_0" % sl, "ot%d_1" % sl], [], "out%d" % sl))
              P.barrier()

          except StopBuild:
            outs.append(k.dma("pool", dbg_d["uT"][:, :], uT_flat, [], [], "dbgp_stop"))
            break
        P.final_waits += outs
        nw = P.emit(st)
        print("ops", len(P.ops), "waits", nw, "arena peak", A.peak)
    return nc


def prep_weights(inp, T):
    NT = T // 128
    f = lambda a: np.ascontiguousarray(np.asarray(a, dtype=np.float32))
    w_in = np.asarray(inp["w_in"])[0]
    offs = np.cumsum([0, 1024, 1024, 1024, 1024, 8, 8, 1024, 1024, 1024, 8, 1024, 1024])
    gq, gk, gv, gz, ga, gb, fq, fk, fv, ff, gta, gtb = [w_in[:, offs[i]:offs[i + 1]] for i in range(12)]

    def pc(w):
        return w.reshape(8, 128, -1).transpose(1, 0, 2)
    w_gdn = np.stack([pc(np.concatenate([g_[:, h * 128:(h + 1) * 128] for g_ in (gq, gk, gv, gz)], 1)).reshape(128, -1) for h in range(8)])
    w_fox = np.stack([pc(np.concatenate([g_[:, h * 128:(h + 1) * 128] for g_ in (fq, fk, fv)], 1)).reshape(128, -1) for h in range(8)])
    w_small = pc(np.concatenate([ga, gb, ff], 1)).reshape(128, -1)
    w_gate = np.stack([pc(np.concatenate([gta[:, n * 128:(n + 1) * 128], gtb[:, n * 128:(n + 1) * 128]], 1)).reshape(128, -1) for n in range(8)])
    pa = np.asarray(inp["w_proj_gdn"])[0]
    pb = np.asarray(inp["w_proj_fox"])[0]
    w_pab = np.stack([pc(np.concatenate([pa[:, n * 128:(n + 1) * 128], pb[:, n * 128:(n + 1) * 128]], 1)).reshape(128, -1) for n in range(8)])
    w_o = pc(np.asarray(inp["w_out"])[0]).reshape(128, -1)
    wu = np.asarray(inp["w_up"])[0]
    w_up = np.stack([pc(wu[:, fch * 128:(fch + 1) * 128]).reshape(128, -1) for fch in range(32)])
    w_down = np.asarray(inp["w_down"])[0].reshape(32, 128, 1024)
    cwt = np.asarray(inp["gdn_conv_w"])[0]
    convw = cwt.reshape(4, 24, 128).transpose(2, 1, 0).reshape(128, 96)
    d = {
        "w_gdn": w_gdn, "w_fox": w_fox, "w_small": w_small, "w_gate": w_gate, "w_pab": w_pab, "w_o": w_o,
        "w_up": w_up, "w_down": w_down, "convw": convw,
        "gmix": np.asarray(inp["norm_mix_g"])[0][None, :], "gmlp": np.asarray(inp["norm_mlp_g"])[0][None, :],
        "gnorm4": np.tile(np.asarray(inp["gdn_norm_g"])[0], 4)[None, :],
        "alog_t": np.tile(np.asarray(inp["gdn_a_log"])[0], NT)[None, :],
        "dtb_t": np.tile(np.asarray(inp["gdn_dt_bias"])[0], NT)[None, :],
        "fb_t": np.tile(np.asarray(inp["fox_f_bias"])[0], NT)[None, :],
        "fqg": np.asarray(inp["fox_q_norm_g"])[0][:, None], "fkg": np.asarray(inp["fox_k_norm_g"])[0][:, None],
    }
    return {k_: f(v) for k_, v in d.items()}


_NC_CACHE = {}


def kernel(**inputs):
    x = np.asarray(inputs["x"], dtype=np.float32)
    Bt, T, _ = x.shape
    ncores = 8
    nseq = Bt // ncores
    wd = prep_weights(inputs, T)
    key = (T, nseq)
    if key not in _NC_CACHE:
        nc = bass.Bass("TRN2", target_bir_lowering=False)
        build(nc, T, nseq)
        _NC_CACHE[key] = nc
    nc = _NC_CACHE[key]
    in_maps = []
    for c in range(ncores):
        m = dict(wd)
        m["x"] = np.ascontiguousarray(x[c * nseq:(c + 1) * nseq])
        in_maps.append(m)
    res = run_bass_kernel_spmd(nc, in_maps, core_ids=list(range(ncores)))
    out = np.concatenate([res.results[c]["out"] for c in range(ncores)], axis=0)
    return out.astype(np.float32)
```

```python
import contextlib
import concourse.bass as bass
import concourse.mybir as mybir

F32 = mybir.dt.float32
BF16 = mybir.dt.bfloat16
AF = mybir.ActivationFunctionType
ALU = mybir.AluOpType
AX = mybir.AxisListType

ENGS = ("pe", "act", "dve", "pool", "sp")


class Op:
    __slots__ = ("eng", "fn", "deps", "dma", "idx", "sig", "signal")

    def __init__(self, eng, fn, dma):
        self.eng = eng
        self.fn = fn
        self.deps = []
        self.dma = dma
        self.sig = None
        self.signal = False


class Prog:
    def __init__(self, nc):
        self.nc = nc
        self.ops = []
        self.last_w = {}
        self.readers = {}
        self.final_waits = []

    def op(self, eng, fn, reads=(), writes=(), dma=None):
        o = Op(eng, fn, dma)
        o.idx = len(self.ops)
        pfr = [r for r in reads if r.startswith("pf")]
        if pfr:
            reads = [r for r in reads if not r.startswith("pf")]
            writes = list(writes) + [r for r in pfr if r not in writes]
        deps = set()
        for r in reads:
            w = self.last_w.get(r)
            if w is not None:
                deps.add(w)
        for wkey in writes:
            w = self.last_w.get(wkey)
            if w is not None:
                deps.add(w)
            for rd in self.readers.get(wkey, ()):
                deps.add(rd)
        for d in deps:
            if d is o:
                continue
            if d.dma is None and d.eng == eng:
                if eng == "pe":
                    continue
            o.deps.append(d)
        for r in reads:
            self.readers.setdefault(r, []).append(o)
        for wkey in writes:
            self.last_w[wkey] = o
            self.readers[wkey] = []
        self.ops.append(o)
        return o

    def barrier(self):
        last = {}
        for o in self.ops:
            if o.fn is None:
                continue
            k = ("dma", o.dma) if o.dma is not None else ("eng", o.eng)
            last[k] = o
        for e in ENGS:
            b = Op(e, None, None)
            b.idx = len(self.ops)
            b.deps = [d for k, d in last.items() if not (k == ("eng", e))]
            self.ops.append(b)
        self.last_w = {}
        self.readers = {}

    def emit(self, stack):
        nc = self.nc
        for o in self.ops:
            for d in o.deps:
                d.signal = True
        for o in self.final_waits:
            o.signal = True
        cnt = {}
        for o in self.ops:
            if o.dma is not None:
                k = ("dma", o.dma)
                cnt[k] = cnt.get(k, 0) + 16
                o.sig = (k, cnt[k])
            elif o.signal:
                k = ("eng", o.eng)
                cnt[k] = cnt.get(k, 0) + 1
                o.sig = (k, cnt[k])
        sems = {}
        for k in cnt:
            nm = "s_" + "_".join(str(x) for x in k)
            sems[k] = stack.enter_context(nc.semaphore(nm))
        per_eng = {e: [o for o in self.ops if o.eng == e] for e in ENGS}
        finals = list(self.final_waits)
        block = stack.enter_context(nc.Block())
        nwaits = [0]

        def run(eng_name, eobj):
            waited = {}
            for o in per_eng[eng_name]:
                need = {}
                for d in o.deps:
                    k, v = d.sig
                    if waited.get(k, 0) >= v:
                        continue
                    if need.get(k, 0) < v:
                        need[k] = v
                for k, v in need.items():
                    eobj.wait_ge(sems[k], v)
                    waited[k] = v
                    nwaits[0] += 1
                if o.fn is None:
                    continue
                ins = o.fn(eobj)
                if o.dma is not None:
                    ins.then_inc(sems[o.sig[0]], 16)
                elif o.signal:
                    ins.then_inc(sems[o.sig[0]], 1)
            if eng_name == "sp":
                need = {}
                for d in finals:
                    k, v = d.sig
                    if need.get(k, 0) < v:
                        need[k] = v
                for k, v in need.items():
                    if waited.get(k, 0) < v:
                        eobj.wait_ge(sems[k], v)

        @block.tensor
        def _(e):
            run("pe", e)

        @block.scalar
        def _(e):
            run("act", e)

        @block.vector
        def _(e):
            run("dve", e)

        @block.gpsimd
        def _(e):
            run("pool", e)

        @block.sync
        def _(e):
            run("sp", e)
        return nwaits[0]

import numpy as np
import os
from concourse.bass_utils import run_bass_kernel_spmd

D = 1024
NH = 8
DH = 128
DFF = 4096
EPS = 1e-6
ARENA = 105600


class StopBuild(Exception):
    pass


class Arena:
    def __init__(self, t):
        self.t = t
        self.off = 0
        self.peak = 0

    def bf(self, n):
        n2 = (n + 15) // 16 * 16
        o = self.off
        self.off += n2
        self.peak = max(self.peak, self.off)
        assert self.off <= ARENA, ("arena overflow", self.off)
        return self.t[:, o:o + n]

    def f32(self, n):
        n2 = (2 * n + 15) // 16 * 16
        o = self.off
        self.off += n2
        self.peak = max(self.peak, self.off)
        assert self.off <= ARENA, ("arena overflow", self.off)
        return self.t[:, o:o + 2 * n].bitcast(F32)

    def mark(self):
        return self.off

    def release(self, m):
        self.off = m


class K:
    def __init__(self, P):
        self.P = P

    def mm(self, out, lhsT, rhs, start, stop, r, w):
        return self.P.op("pe", lambda e: e.matmul(out, lhsT=lhsT, rhs=rhs, start=start, stop=stop), r, w)

    def tr(self, out, in_, ident, r, w):
        return self.P.op("pe", lambda e: e.transpose(out=out, in_=in_, identity=ident), r, w)

    def act(self, out, in_, func, r, w, scale=None, bias=None, accum=None):
        kw = {}
        if scale is not None:
            kw["scale"] = scale
        if bias is not None:
            kw["bias"] = bias
        if accum is not None:
            kw["accum_out"] = accum
        return self.P.op("act", lambda e: e.activation(out=out, in_=in_, func=func, **kw), r, w)

    def cp(self, eng, out, in_, r, w):
        if eng == "act":
            return self.P.op("act", lambda e: e.copy(out=out, in_=in_), r, w)
        return self.P.op(eng, lambda e: e.tensor_copy(out=out, in_=in_), r, w)

    def tt(self, eng, out, in0, in1, op, r, w):
        return self.P.op(eng, lambda e: e.tensor_tensor(out=out, in0=in0, in1=in1, op=op), r, w)

    def ts(self, eng, out, in0, s1, op0, r, w, s2=None, op1=None):
        if op1 is None:
            return self.P.op(eng, lambda e: e.tensor_scalar(out=out, in0=in0, scalar1=s1, scalar2=None, op0=op0), r, w)
        return self.P.op(eng, lambda e: e.tensor_scalar(out=out, in0=in0, scalar1=s1, scalar2=s2, op0=op0, op1=op1), r, w)

    def stt(self, out, in0, scalar, in1, op0, op1, r, w):
        return self.P.op("dve", lambda e: e.scalar_tensor_tensor(out=out, in0=in0, scalar=scalar, in1=in1, op0=op0, op1=op1), r, w)

    def memset(self, eng, ap, val, w):
        return self.P.op(eng, lambda e: e.memset(ap, val), (), w)

    def dma(self, eng, out, in_, r, w, key):
        return self.P.op(eng, lambda e: e.dma_start(out=out, in_=in_), r, w, dma=key)


def build(nc, T, NSEQ, dbg=False, stop=None):
    NT = T // 128
    BW = min(512, T)
    NB = T // BW
    TPB = BW // 128
    NT8 = NT * 8
    dram = lambda n, s, kind="ExternalInput": nc.dram_tensor(n, list(s), F32, kind=kind).ap()
    x_d = dram("x", [NSEQ, T, D])
    out_d = dram("out", [NSEQ, T, D], "ExternalOutput")
    wgdn_d = dram("w_gdn", [NH, 128, 8 * 512])
    wfox_d = dram("w_fox", [NH, 128, 8 * 384])
    wsm_d = dram("w_small", [128, 8 * 24])
    wgate_d = dram("w_gate", [8, 128, 8 * 256])
    wpab_d = dram("w_pab", [8, 128, 8 * 256])
    wout_d = dram("w_o", [128, 8 * 1024])
    wup_d = dram("w_up", [32, 128, 8 * 128])
    wdn_d = dram("w_down", [32, 128, 1024])
    convw_d = dram("convw", [128, 24 * 4])
    gmix_d = dram("gmix", [1, D])
    gmlp_d = dram("gmlp", [1, D])
    gnorm4_d = dram("gnorm4", [1, 512])
    alog_d = dram("alog_t", [1, NT8])
    dtb_d = dram("dtb_t", [1, NT8])
    fb_d = dram("fb_t", [1, NT8])
    fqg_d = dram("fqg", [128, 1])
    fkg_d = dram("fkg", [128, 1])
    dbg_d = {}
    if dbg:
        dbg_d["uT"] = dram("d_uT", [128, 8 * T], "ExternalOutput")
        dbg_d["oaT"] = dram("d_oaT", [128, 8 * T], "ExternalOutput")
        dbg_d["obT"] = dram("d_obT", [128, 8 * T], "ExternalOutput")
        dbg_d["tab"] = dram("d_tab", [128, 6 * NT8], "ExternalOutput")
        dbg_d["h"] = dram("d_h", [128, NT * D], "ExternalOutput")

    with contextlib.ExitStack() as st:
        P = Prog(nc)
        k = K(P)
        arena_t = st.enter_context(nc.sbuf_tensor("arena", [128, ARENA], BF16))
        A = Arena(arena_t)
        pf = [st.enter_context(nc.psum_tensor("pf%d" % i, [128, 512], F32)) for i in range(8)]
        pbf = pf[7][:, :].bitcast(BF16)

        def PF(b, c0=0, c1=512):
            return ["pf%d" % b]

        def pq(b, q, n=128):
            return pf[b][:, q * 128:q * 128 + n]

        def pbs(s):
            return pbf[:, s * 128:(s + 1) * 128]

        ident_bf = A.bf(128)
        ones_bf = A.bf(128)
        onesm_bf = A.bf(128)
        mUi_bf = A.bf(128)
        ident_f = A.f32(128)
        ones_f = A.f32(128)
        triU_f = A.f32(128)
        mUs_bd = A.f32(128)
        mLs_bd = A.f32(128)
        mL_off = A.f32(128)
        gmix_bc = A.f32(D)
        gmlp_bc = A.f32(D)
        gnorm4 = A.f32(512)
        alog_bc = A.f32(NT8)
        dtb_bc = A.f32(NT8)
        fb_bc = A.f32(NT8)
        cw = A.f32(96)
        fqg = A.f32(1)
        fkg = A.f32(1)
        eps_c = A.f32(1)
        one_c = A.f32(1)

        k.memset("pool", ones_bf, 1.0, ["ones_bf"])
        k.memset("pool", onesm_bf, 1.0 / 128, ["onesm_bf"])
        k.memset("pool", ones_f, 1.0, ["ones_f"])
        k.memset("pool", eps_c, EPS, ["eps_c"])
        k.memset("pool", one_c, 1.0, ["one_c"])

        def asel(out, in_, step, cm, cmp, r, w):
            P.op("pool", lambda e: e.affine_select(out=out, in_=in_, pattern=[[step, 128]], compare_op=cmp, fill=0.0,
                                                   base=0, channel_multiplier=cm), r, w)
        asel(ident_bf, ones_bf, 1, -1, ALU.is_equal, ["ones_bf"], ["ident_bf"])
        asel(ident_f, ones_f, 1, -1, ALU.is_equal, ["ones_f"], ["ident_f"])
        asel(mUi_bf, ones_bf, 1, -1, ALU.is_ge, ["ones_bf"], ["mUi_bf"])
        asel(triU_f, ones_f, 1, -1, ALU.is_ge, ["ones_f"], ["triU_f"])
        asel(mUs_bd, ones_f, 1, -1, ALU.is_gt, ["ones_f"], ["mUs_bd"])
        asel(mLs_bd, ones_f, -1, 1, ALU.is_gt, ["ones_f"], ["mLs_bd"])
        k.memset("pool", mUs_bd[0:64, 64:128], 0.0, ["mUs_bd"])
        k.memset("pool", mLs_bd[64:128, 0:64], 0.0, ["mLs_bd"])
        k.memset("pool", mL_off, 0.0, ["mL_off"])
        k.memset("pool", mL_off[64:128, 0:64], 1.0, ["mL_off"])
        k.dma("sp", gmix_bc, gmix_d[0:1, :].partition_broadcast(128), [], ["gmix_bc"], "cst1")
        k.dma("sp", gmlp_bc, gmlp_d[0:1, :].partition_broadcast(128), [], ["gmlp_bc"], "cst2")
        k.dma("sp", gnorm4, gnorm4_d[0:1, :].partition_broadcast(128), [], ["gnorm4"], "cst3")
        k.dma("sp", alog_bc, alog_d[0:1, :].partition_broadcast(128), [], ["alog_bc"], "cst4")
        k.dma("sp", dtb_bc, dtb_d[0:1, :].partition_broadcast(128), [], ["dtb_bc"], "cst5")
        k.dma("sp", fb_bc, fb_d[0:1, :].partition_broadcast(128), [], ["fb_bc"], "cst6")
        k.dma("sp", cw, convw_d[:, :], [], ["cw"], "cst7")
        k.dma("sp", fqg, fqg_d[:, :], [], ["fqg"], "cst8")
        k.dma("sp", fkg, fkg_d[:, :], [], ["fkg"], "cst9")

        a1 = A.mark()
        uT_flat = A.bf(8 * T)
        oaT_flat = A.bf(8 * T)
        obT_flat = A.bf(8 * T)
        a1_end = A.mark()
        uT = uT_flat.rearrange("p (c t) -> p c t", c=8)
        oaT = oaT_flat.rearrange("p (c t) -> p c t", c=8)
        obT = obT_flat.rearrange("p (c t) -> p c t", c=8)
        A.release(a1)
        h_flat = A.f32(NT * D)
        hnT_flat = A.bf(8 * T)
        assert A.mark() <= a1_end
        A.release(a1_end)
        h3 = h_flat.rearrange("p (i d) -> p i d", i=NT)
        hnT = hnT_flat.rearrange("p (c t) -> p c t", c=8)

        tabs = {}
        for nm in ["g", "G", "beta", "nbeta", "kds", "decl", "lf", "c", "tot", "cpre", "cend", "ty", "ta", "tb"]:
            tabs[nm] = A.f32(NT8)
        btab_flat = A.f32(8 * NT * NT)
        btab = btab_flat.rearrange("p (h j i) -> p h j i", h=8, j=NT)
        m0 = A.mark()

        def t3(ap):
            return ap.rearrange("p (i h) -> p i h", h=8)

        outs = []

        def chk(name):
            if stop == name:
                raise StopBuild()

        for s in range(NSEQ):
          try:
              A.release(m0)
              xt = [A.f32(D), A.f32(D)]
              junk = A.bf(D)
              ubf = [A.bf(D), A.bf(D)]
              ssc = [A.f32(1), A.f32(1)]
              for i in range(NT):
                  sl = i % 2
                  k.dma("sp", xt[sl], x_d[s, i * 128:(i + 1) * 128, :], [], ["xt%d" % sl], "xt%d" % sl)
                  k.act(junk, xt[sl], AF.Square, ["xt%d" % sl], ["junk", "ss%d" % sl], accum=ssc[sl])
                  k.act(ssc[sl], ssc[sl], AF.Ln, ["ss%d" % sl, "eps_c"], ["ss%d" % sl], scale=1.0 / D, bias=eps_c)
                  k.act(ssc[sl], ssc[sl], AF.Exp, ["ss%d" % sl], ["ss%d" % sl], scale=-0.5)
                  k.stt(ubf[sl], xt[sl], ssc[sl], gmix_bc, ALU.mult, ALU.mult, ["xt%d" % sl, "ss%d" % sl, "gmix_bc"], ["ubf%d" % sl])
                  for c in range(8):
                      k.tr(pbs(c), ubf[sl][:, c * 128:(c + 1) * 128], ident_bf, ["ubf%d" % sl, "ident_bf"], ["pf7"])
                  k.cp("act", uT[:, :, i * 128:(i + 1) * 128], pbf.rearrange("p (c t) -> p c t", c=8),
                       ["pf7"], ["uT%d" % i])
              uT_all = ["uT%d" % i for i in range(NT)]

              def uTb(b):
                  return ["uT%d" % i for i in range(b * TPB, (b + 1) * TPB)]

              if stop == "P1":
                  outs.append(k.dma("pool", dbg_d["uT"][:, :], uT_flat, uT_all, [], "dbgp_uT"))
                  break
              P.barrier()
              A.release(m0)
              wsm = A.bf(8 * 24)
              wsm3 = wsm.rearrange("p (c n) -> p c n", c=8)
              k.dma("pool", wsm, wsm_d[:, :], [], ["wsm"], "wsm")
              psS = pf[0][:, 0:NT * 24]
              for i in range(NT):
                  for c in range(8):
                      k.mm(pf[0][:, i * 24:(i + 1) * 24], uT[:, c, i * 128:(i + 1) * 128], wsm3[:, c, :], c == 0, c == 7,
                           ["uT%d" % i, "wsm"], PF(0))
              ps3 = psS.rearrange("p (i n) -> p i n", n=24)
              ga, gb, ff = ps3[:, :, 0:8], ps3[:, :, 8:16], ps3[:, :, 16:24]
              ty, ta, tb = tabs["ty"], tabs["ta"], tabs["tb"]
              k.tt("dve", t3(ty), ga, t3(dtb_bc), ALU.add, PF(0) + ["dtb_bc"], ["ty"])
              k.stt(ta, ty, -1.0, ty, ALU.mult, ALU.max, ["ty"], ["ta"])
              k.act(ta, ta, AF.Exp, ["ta"], ["ta"], scale=-1.0)
              k.act(ta, ta, AF.Ln, ["ta", "one_c"], ["ta"], bias=one_c)
              k.stt(ty, ty, 0.0, ta, ALU.max, ALU.add, ["ty", "ta"], ["ty"])
              k.act(tb, alog_bc, AF.Exp, ["alog_bc"], ["tb"])
              k.stt(tabs["g"], ty, -1.0, tb, ALU.mult, ALU.mult, ["ty", "tb"], ["g"])
              k.act(t3(tabs["beta"]), gb, AF.Sigmoid, PF(0), ["beta"])
              k.ts("dve", tabs["nbeta"], tabs["beta"], -1.0, ALU.mult, ["beta"], ["nbeta"])
              k.tt("dve", t3(ty), ff, t3(fb_bc), ALU.add, PF(0) + ["fb_bc"], ["ty"])
              k.stt(ta, ty, -1.0, ty, ALU.mult, ALU.max, ["ty"], ["ta"])
              k.act(ta, ta, AF.Exp, ["ta"], ["ta"], scale=-1.0)
              k.act(ta, ta, AF.Ln, ["ta", "one_c"], ["ta"], bias=one_c)
              k.stt(tabs["lf"], ty, 0.0, ta, ALU.min, ALU.subtract, ["ty", "ta"], ["lf"])
              k.mm(pf[1][:, 0:NT8], triU_f, tabs["g"], True, True, ["triU_f", "g"], ["pf1"])
              k.mm(pf[1][:, 128:128 + NT8], ones_f, tabs["g"], True, True, ["ones_f", "g"], ["pf1"])
              k.mm(pf[1][:, 256:256 + NT8], triU_f, tabs["lf"], True, True, ["triU_f", "lf"], ["pf1"])
              k.mm(pf[1][:, 384:384 + NT8], ones_f, tabs["lf"], True, True, ["ones_f", "lf"], ["pf1"])
              k.cp("act", tabs["G"], pf[1][:, 0:NT8], ["pf1"], ["G"])
              k.act(tabs["decl"], pf[1][:, 128:128 + NT8], AF.Exp, ["pf1"], ["decl"])
              k.tt("dve", ty, pf[1][:, 128:128 + NT8], tabs["G"], ALU.subtract, ["pf1", "G"], ["ty"])
              k.act(tabs["kds"], ty, AF.Exp, ["ty"], ["kds"])
              k.cp("act", tabs["tot"], pf[1][:, 384:384 + NT8], ["pf1"], ["tot"])
              k.memset("dve", tabs["cpre"][:, 0:8], 0.0, ["cpre"])
              for i in range(1, NT):
                  k.tt("dve", tabs["cpre"][:, i * 8:(i + 1) * 8], tabs["cpre"][:, (i - 1) * 8:i * 8],
                       tabs["tot"][:, (i - 1) * 8:i * 8], ALU.add, ["cpre", "tot"], ["cpre"])
              k.tt("dve", tabs["c"], pf[1][:, 256:256 + NT8], tabs["cpre"], ALU.add, ["pf1", "cpre"], ["c"])
              k.tt("dve", tabs["cend"], tabs["cpre"], tabs["tot"], ALU.add, ["cpre", "tot"], ["cend"])
              cend3 = t3(tabs["cend"])
              n_ = 0
              for h in range(NH):
                  for j in range(NT):
                      eng = "dve" if n_ % 2 == 0 else "pool"
                      n_ += 1
                      col = j * 8 + h
                      k.ts(eng, btab[:, h, j, :], cend3[:, :, h], tabs["c"][:, col:col + 1], ALU.subtract, ["cend", "c"], ["btab"])

              if stop == "SMALL":
                  outs.append(k.dma("pool", dbg_d["uT"][:, :], uT_flat, uT_all, [], "dbgp_uT"))
                  for ti, nm in enumerate(["g", "G", "beta", "kds", "decl", "c"]):
                      outs.append(k.dma("sp", dbg_d["tab"][:, ti * NT8:(ti + 1) * NT8], tabs[nm], [nm], [], "dbg"))
                  break
              P.barrier()
              A.release(m0)
              wg = [A.bf(8 * 512)]
              GRP = min(NT, int(os.environ.get("GDN_GRP", "4")))
              GW = GRP * 128
              NG = NT // GRP
              qT, kT, vT = A.bf(T), A.bf(T), A.bf(T)
              Kd_f, V_f, gz_f = A.bf(T), A.f32(T), A.f32(T)
              Kd = Kd_f.rearrange("p (i d) -> p i d", d=128)
              Vt = V_f.rearrange("p (i d) -> p i d", d=128)
              gz = gz_f.rearrange("p (i d) -> p i d", d=128)
              S_f = A.f32(128)
              S_bf, R_bf, vnew, on_bf, junk128 = [A.bf(128) for _ in range(5)]
              ssq = A.f32(1)
              mp = A.mark()
              sq = [A.bf(BW), A.bf(BW)]
              cb = [A.f32(BW + 3), A.f32(BW + 3)]
              acc = [A.f32(BW), A.f32(BW)]
              lnt = [A.f32(BW), A.f32(BW)]
              rs = [A.f32(BW), A.f32(BW)]
              tmpz = A.f32(BW)
              A.release(mp)
              grep_ = [A.f32(128), A.f32(128)]
              brep_ = [A.f32(128), A.f32(128)]
              E_, Y_, M1_, DuI, DuB, DlB, DlO, bbc = [A.f32(GW) for _ in range(8)]
              eG = A.bf(GW)
              Qb = [A.bf(GW), A.bf(GW)]
              Pb = [A.bf(GW), A.bf(GW)]
              Mb = [A.bf(GW), A.bf(GW)]
              Ao, Td, X1 = [A.bf(GW) for _ in range(3)]
              AttnT = [A.bf(GW), A.bf(GW)]
              QeT = [A.bf(GW), A.bf(GW)]
              KeT = [A.bf(GW), A.bf(GW)]
              TpT = [A.bf(GW), A.bf(GW)]

              def fm_norm(src, src_r, dst, dst_w, ones_t, ones_r, mult, mult_r, pbank, par):
                  k.act(sq[par], src, AF.Square, src_r, ["sq%d" % par])
                  k.mm(pf[pbank][:, 0:BW], ones_t, sq[par], True, True, ["sq%d" % par] + ones_r, PF(pbank))
                  k.act(lnt[par], pf[pbank][:, 0:BW], AF.Ln, PF(pbank) + ["eps_c"], ["lnt%d" % par], bias=eps_c)
                  k.act(rs[par], lnt[par], AF.Exp, ["lnt%d" % par], ["rs%d" % par], scale=-0.5)
                  k.stt(dst, src, mult, rs[par], ALU.mult, ALU.mult, src_r + ["rs%d" % par] + mult_r, dst_w)

              pcnt = 0
              for h in range(NH):
                  ws = 0
                  wgs = wg[ws]
                  wg3 = wgs.rearrange("p (c n) -> p c n", c=8)
                  k.dma("pool", wgs, wgdn_d[h, :, :], [], ["wg%d" % ws], "wg%d" % ws)
                  def zgate(ws=ws, wg3=wg3):
                      for i in range(NT):
                          q4 = i % 4
                          for c in range(8):
                              k.mm(pq(4, q4), uT[:, c, i * 128:(i + 1) * 128], wg3[:, c, 384:512], c == 0, c == 7,
                                   ["uT%d" % i, "wg%d" % ws], ["pf4"])
                          if q4 == TPB - 1 or i == NT - 1:
                              n4 = q4 + 1
                              i0 = i - q4
                              k.act(tmpz[:, 0:n4 * 128], pf[4][:, 0:n4 * 128], AF.Silu, ["pf4"], ["tmpz"])
                              k.tt("dve", gz_f[:, i0 * 128:(i0 + n4) * 128], tmpz[:, 0:n4 * 128], gnorm4[:, 0:n4 * 128], ALU.mult,
                                   ["tmpz", "gnorm4"], ["gz%d" % ii for ii in range(i0, i0 + n4)])
                          yield
                  zgen = zgate()
                  for grp in range(3):
                      chunk = grp * 8 + h
                      dstT = (qT, kT, vT)[grp]
                      dnm = ("qT", "kT", "vT")[grp]
                      for b in range(NB):
                          pb = b % 2
                          pr = pcnt % 2
                          pcnt += 1
                          cbn, accn = "cb%d" % pr, "acc%d" % pr
                          cbc, accc = cb[pr], acc[pr]
                          for c in range(8):
                              k.mm(pf[pb][:, 0:BW], wg3[:, c, grp * 128:(grp + 1) * 128], uT[:, c, b * BW:(b + 1) * BW],
                                   c == 0, c == 7, ["wg%d" % ws] + uTb(b), PF(pb))
                          if b == 0:
                              k.memset("pool", cbc[:, 0:3], 0.0, [cbn])
                          else:
                              k.cp("act", cbc[:, 0:3], cb[1 - pr][:, BW:BW + 3], ["cb%d" % (1 - pr)], [cbn])
                          k.cp("act", cbc[:, 3:3 + BW], pf[pb][:, 0:BW], PF(pb), [cbn])
                          k.ts("dve", accc, cbc[:, 0:BW], cw[:, chunk * 4:chunk * 4 + 1], ALU.mult, [cbn, "cw"], [accn])
                          for tap in range(1, 4):
                              k.stt(accc, cbc[:, tap:tap + BW], cw[:, chunk * 4 + tap:chunk * 4 + tap + 1], accc, ALU.mult, ALU.add,
                                    [cbn, "cw", accn], [accn])
                          k.act(accc, accc, AF.Silu, [accn], [accn])
                          dsl = dstT[:, b * BW:(b + 1) * BW]
                          dw = ["%s%d" % (dnm, i) for i in range(b * TPB, (b + 1) * TPB)]
                          if grp < 2:
                              fm_norm(accc, [accn], dsl, dw, ones_bf, ["ones_bf"], (DH ** -0.5) if grp == 0 else 1.0, [], 2 + pr, pr)
                          else:
                              k.cp("pool", dsl, accc, [accn], dw)
                          next(zgen, None)
                          next(zgen, None)
                  chk("GDN_a")
                  for _ in zgen:
                      pass
                  chk("GDN_b")
                  for i in range(NT):
                      col = i * 8 + h
                      sk = (2 * i) % 8
                      k.tr(pbs(sk), kT[:, i * 128:(i + 1) * 128], ident_bf, ["kT%d" % i, "ident_bf"], ["pf7"])
                      k.act(Kd[:, i, :], pbs(sk), AF.Copy, ["pf7", "kds"], ["Kd%d" % i], scale=tabs["kds"][:, col:col + 1])
                      sv = (2 * i + 1) % 8
                      k.tr(pbs(sv), vT[:, i * 128:(i + 1) * 128], ident_bf, ["vT%d" % i, "ident_bf"], ["pf7"])
                      k.cp("act", Vt[:, i, :], pbs(sv), ["pf7"], ["V%d" % i])
                  chk("GDN_c")
                  P.barrier()
                  k.memset("pool", S_f, 0.0, ["S_f"])
                  k.memset("pool", S_bf, 0.0, ["S_bf"])

                  def v3(ap):
                      return ap.rearrange("p (g f) -> p g f", g=GRP)

                  def bcm(m):
                      return m.unsqueeze(1).to_broadcast([128, GRP, 128])

                  def colbc(tab, i0, h_):
                      return t3(tab)[:, i0:i0 + GRP, h_].unsqueeze(2).to_broadcast([128, GRP, 128])

                  def gs(t):
                      return slice(t * 128, (t + 1) * 128)

                  def prep(g, h=h):
                      i0 = g * GRP
                      gp = g % 2
                      b0v, b1v, b2v, b3v = [pf[b_][:, 0:GW] for b_ in range(4)]
                      for t in range(GRP):
                          col = (i0 + t) * 8 + h
                          r2 = t % 2
                          k.cp("dve", grep_[r2], tabs["g"][:, col:col + 1].to_broadcast([128, 128]), ["g"], ["grep%d" % r2])
                          k.cp("dve", brep_[r2], tabs["beta"][:, col:col + 1].to_broadcast([128, 128]), ["beta"], ["brep%d" % r2])
                          k.mm(pq(0, t), grep_[r2], triU_f, True, True, ["grep%d" % r2, "triU_f"], ["pf0"])
                          k.mm(pq(1, t), brep_[r2], ident_f, True, True, ["brep%d" % r2, "ident_f"], ["pf1"])
                      k.cp("act", bbc, b1v, ["pf1"], ["bbc"])
                      k.act(eG, b0v, AF.Exp, ["pf0"], ["eG"])
                      k.tt("dve", v3(E_), v3(b0v), colbc(tabs["G"], i0, h), ALU.subtract, ["pf0", "G"], ["E"])
                      k.ts("dve", Y_, E_, 0.0, ALU.max, ["E"], ["Y"])
                      k.ts("dve", E_, E_, 0.0, ALU.min, ["E"], ["E"])
                      k.act(Y_, Y_, AF.Exp, ["Y"], ["Y"], scale=-1.0)
                      k.act(E_, E_, AF.Exp, ["E"], ["E"])
                      k.tt("pool", v3(M1_), v3(bbc), bcm(mUs_bd), ALU.mult, ["bbc", "mUs_bd"], ["M1"])
                      k.tt("dve", v3(DuI), v3(E_), bcm(triU_f), ALU.mult, ["E", "triU_f"], ["DuI"])
                      k.tt("dve", DuB, E_, M1_, ALU.mult, ["E", "M1"], ["DuB"])
                      k.tt("dve", v3(DlB), v3(Y_), bcm(mLs_bd), ALU.mult, ["Y", "mLs_bd"], ["DlB"])
                      k.tt("pool", v3(DlB), v3(DlB), colbc(tabs["beta"], i0, h), ALU.mult, ["DlB", "beta"], ["DlB"])
                      k.tt("dve", v3(DlO), v3(Y_), bcm(mL_off), ALU.mult, ["Y", "mL_off"], ["DlO"])
                      k.tt("pool", v3(DlO), v3(DlO), colbc(tabs["beta"], i0, h), ALU.mult, ["DlO", "beta"], ["DlO"])
                      yield
                      for t in range(GRP):
                          i = i0 + t
                          kTi = kT[:, i * 128:(i + 1) * 128]
                          qTi = qT[:, i * 128:(i + 1) * 128]
                          k.mm(pq(2, t), kTi, kTi, True, True, ["kT%d" % i], ["pf2"])
                          k.mm(pq(3, t), kTi, qTi, True, True, ["kT%d" % i, "qT%d" % i], ["pf3"])
                      kr = ["kT%d" % (i0 + t) for t in range(GRP)]
                      qr = ["qT%d" % (i0 + t) for t in range(GRP)]
                      k.tt("dve", Qb[0], b2v, DlB, ALU.mult, ["pf2", "DlB"], ["Q0"])
                      k.tt("dve", Ao, b2v, DlO, ALU.mult, ["pf2", "DlO"], ["Ao"])
                      k.tt("dve", Pb[0], b2v, DuB, ALU.mult, ["pf2", "DuB"], ["P0"])
                      k.tt("dve", AttnT[gp], b3v, DuI, ALU.mult, ["pf3", "DuI"], ["AttnT%d" % gp])
                      k.tt("dve", v3(Mb[0]), bcm(ident_bf), v3(Pb[0]), ALU.subtract, ["ident_bf", "P0"], ["M0"])
                      k.tt("dve", QeT[gp], qT[:, i0 * 128:i0 * 128 + GW], eG, ALU.mult, qr + ["eG"], ["QeT%d" % gp])
                      k.tt("dve", KeT[gp], kT[:, i0 * 128:i0 * 128 + GW], eG, ALU.mult, kr + ["eG"], ["KeT%d" % gp])
                      cur = 0
                      for lvl in range(1, 6):
                          nxt = 1 - cur
                          for t in range(GRP):
                              if lvl < 5:
                                  k.mm(pq(0, t), Qb[cur][:, gs(t)], Pb[cur][:, gs(t)], True, True, ["Q%d" % cur, "P%d" % cur], ["pf0"])
                              k.mm(pq(1, t), Pb[cur][:, gs(t)], Qb[cur][:, gs(t)], True, True, ["Q%d" % cur, "P%d" % cur], ["pf1"])
                          if lvl < 5:
                              k.cp("act", Pb[nxt], b0v, ["pf0"], ["P%d" % nxt])
                          k.cp("dve", Qb[nxt], b1v, ["pf1"], ["Q%d" % nxt])
                          for t in range(GRP):
                              k.mm(pq(2, t), ident_bf, Mb[cur][:, gs(t)], True, False, ["ident_bf", "M%d" % cur], ["pf2"])
                              k.mm(pq(2, t), Qb[nxt][:, gs(t)], Mb[cur][:, gs(t)], False, True, ["Q%d" % nxt, "M%d" % cur], ["pf2"])
                          k.cp("act", Mb[nxt], b2v, ["pf2"], ["M%d" % nxt])
                          if lvl == 5:
                              k.cp("dve", Y_, b2v, ["pf2"], ["Y"])
                          cur = nxt
                          if lvl == 2:
                              yield
                      yield
                      Ud = Mb[cur]
                      Udn = "M%d" % cur
                      for t in range(GRP):
                          k.tr(pbs(t), Ud[:, gs(t)], ident_bf, [Udn, "ident_bf"], ["pf7"])
                      k.cp("act", Td, pbf[:, 0:GW], ["pf7"], ["Td"])
                      for t in range(GRP):
                          k.mm(pq(3, t), Ao[:, gs(t)], Ud[:, gs(t)], True, True, ["Ao", Udn], ["pf3"])
                      k.cp("act", X1, b3v, ["pf3"], ["X1"])
                      for t in range(GRP):
                          k.mm(pq(0, t), Td[:, gs(t)], X1[:, gs(t)], True, True, ["Td", "X1"], ["pf0"])
                      k.tt("dve", Y_, Y_, b0v, ALU.subtract, ["Y", "pf0"], ["Y"])
                      k.tt("pool", v3(TpT[gp]), v3(Y_), colbc(tabs["beta"], i0, h), ALU.mult, ["Y", "beta"], ["TpT%d" % gp])
                      yield

                  def scan_step(i, h=h):
                      g = i // GRP
                      t = i % GRP
                      gp = g % 2
                      d2 = i % 2
                      col = i * 8 + h
                      k.mm(pq(4, 0), KeT[gp][:, gs(t)], S_bf, True, True, ["KeT%d" % gp, "S_bf"], ["pf4"])
                      k.tt("dve", R_bf, Vt[:, i, :], pq(4, 0), ALU.subtract, ["V%d" % i, "pf4"], ["R_bf"])
                      k.mm(pq(4, 1), TpT[gp][:, gs(t)], R_bf, True, True, ["TpT%d" % gp, "R_bf"], ["pf4"])
                      k.cp("act", vnew, pq(4, 1), ["pf4"], ["vnew"])
                      k.mm(pq(5 + d2, 0), QeT[gp][:, gs(t)], S_bf, True, False, ["QeT%d" % gp, "S_bf"], ["pf%d" % (5 + d2)])
                      k.mm(pq(5 + d2, 0), AttnT[gp][:, gs(t)], vnew, False, True, ["AttnT%d" % gp, "vnew"], ["pf%d" % (5 + d2)])
                      k.mm(pq(4, 3), Kd[:, i, :], vnew, True, True, ["Kd%d" % i, "vnew"], ["pf4"])
                      k.stt(S_f, S_f, tabs["decl"][:, col:col + 1], pq(4, 3), ALU.mult, ALU.add, ["S_f", "decl", "pf4"], ["S_f"])
                      k.cp("act", S_bf, S_f, ["S_f"], ["S_bf"])

                  def scan_post(i, h=h):
                      d2 = i % 2
                      k.act(junk128, pq(5 + d2, 0), AF.Square, ["pf%d" % (5 + d2)], ["junk128", "ssq"], accum=ssq)
                      k.act(ssq, ssq, AF.Ln, ["ssq", "eps_c"], ["ssq"], scale=1.0 / DH, bias=eps_c)
                      k.act(ssq, ssq, AF.Exp, ["ssq"], ["ssq"], scale=-0.5)
                      k.stt(on_bf, pq(5 + d2, 0), ssq, gz[:, i, :], ALU.mult, ALU.mult, ["pf%d" % (5 + d2), "ssq", "gz%d" % i], ["on_bf"])
                      so = 4 + (i % 4)
                      k.tr(pbs(so), on_bf, ident_bf, ["on_bf", "ident_bf"], ["pf7"])
                      k.cp("act", oaT[:, h, i * 128:(i + 1) * 128], pbs(so), ["pf7"], ["oaT%d_%d" % (h, i)])

                  for _ in prep(0):
                      pass
                  NST = 4
                  per = (NST + GRP - 1) // GRP
                  for g in range(NG):
                      gen = prep(g + 1) if g + 1 < NG else iter(())
                      for t in range(GRP):
                          scan_step(g * GRP + t)
                          if g * GRP + t > 0:
                              scan_post(g * GRP + t - 1)
                          for _ in range(per):
                              next(gen, None)
                      for _ in gen:
                          pass
                  scan_post(NT - 1)
                  P.barrier()
                  chk("GDN_h0")

              chk("GDN2")
              P.barrier()
              A.release(m0)
              wf = [A.bf(8 * 384), A.bf(8 * 384)]
              fqT, fkT, fV_f = A.bf(T), A.bf(T), A.bf(T)
              fV = fV_f.rearrange("p (i d) -> p i d", d=128)
              sq = [A.bf(BW), A.bf(BW)]
              lnt = [A.f32(BW), A.f32(BW)]
              rs = [A.f32(BW), A.f32(BW)]
              wff = A.f32(8 * 384)
              PT = [A.bf(BW), A.bf(BW)]
              rden = [A.f32(BW), A.f32(BW)]
              scale = DH ** -0.5
              for h in range(NH):
                  ws = h % 2
                  wfs = wf[ws]
                  wf3 = wfs.rearrange("p (c n) -> p c n", c=8)
                  k.dma("sp", wff, wfox_d[h, :, :], [], ["wff"], "wff")
                  for cc in range(3):
                      k.cp("pool", wfs[:, cc * 1024:(cc + 1) * 1024], wff[:, cc * 1024:(cc + 1) * 1024], ["wff"], ["wf%d" % ws])
                  def fvproj(ws=ws, wf3=wf3):
                      for i in range(NT):
                          q4 = i % 4
                          for c in range(8):
                              k.mm(pq(5, q4), uT[:, c, i * 128:(i + 1) * 128], wf3[:, c, 256:384], c == 0, c == 7,
                                   ["uT%d" % i, "wf%d" % ws], ["pf5"])
                          if q4 == TPB - 1 or i == NT - 1:
                              n4 = q4 + 1
                              i0 = i - q4
                              k.cp("act", fV_f[:, i0 * 128:(i0 + n4) * 128], pf[5][:, 0:n4 * 128], ["pf5"],
                                   ["fV%d" % ii for ii in range(i0, i0 + n4)])
                          yield
                  fvgen = fvproj()
                  for grp in range(2):
                      dstT = (fqT, fkT)[grp]
                      dnm = ("fqT", "fkT")[grp]
                      gcol = (fqg, fkg)[grp]
                      gnm = ("fqg", "fkg")[grp]
                      for b in range(NB):
                          pb = b % 2
                          for c in range(8):
                              k.mm(pf[pb][:, 0:BW], wf3[:, c, grp * 128:(grp + 1) * 128], uT[:, c, b * BW:(b + 1) * BW],
                                   c == 0, c == 7, ["wf%d" % ws] + uTb(b), PF(pb))
                          dw = ["%s%d" % (dnm, i) for i in range(b * TPB, (b + 1) * TPB)]
                          pr = pcnt % 2
                          pcnt += 1
                          fm_norm(pf[pb][:, 0:BW], PF(pb), dstT[:, b * BW:(b + 1) * BW], dw, onesm_bf, ["onesm_bf"],
                                  gcol, [gnm], (2, 7)[pr], pr)
                          next(fvgen, None)
                          next(fvgen, None)
                  for _ in fvgen:
                      pass
                  cnt = 0
                  for Ib in range(NB):
                      bo, bd = ((5, 6), (0, 1))[Ib % 2]
                      last_j = (Ib + 1) * TPB - 1
                      for j in range(last_j + 1):
                          r0 = max(0, j - Ib * TPB)
                          qoff = r0 * 128
                          sb_ = 3 + (cnt % 2)
                          ptb = cnt % 2
                          cnt += 1
                          qr = ["fqT%d" % (Ib * TPB + r) for r in range(r0, TPB)]
                          k.mm(pf[sb_][:, qoff:BW], fkT[:, j * 128:(j + 1) * 128], fqT[:, Ib * BW + qoff:(Ib + 1) * BW], True, True,
                               ["fkT%d" % j] + qr, PF(sb_))
                          for r in range(r0, TPB):
                              i = Ib * TPB + r
                              k.act(PT[ptb][:, r * 128:(r + 1) * 128], pf[sb_][:, r * 128:(r + 1) * 128], AF.Exp,
                                    PF(sb_) + ["btab"], ["PT%d" % ptb], scale=scale, bias=btab[:, h, j, i:i + 1])
                              if i == j:
                                  k.tt("dve", PT[ptb][:, r * 128:(r + 1) * 128], PT[ptb][:, r * 128:(r + 1) * 128], mUi_bf, ALU.mult,
                                       ["PT%d" % ptb, "mUi_bf"], ["PT%d" % ptb])
                          k.mm(pf[bo][:, qoff:BW], fV[:, j, :], PT[ptb][:, qoff:BW], j == 0, j == last_j,
                               ["fV%d" % j, "PT%d" % ptb], PF(bo))
                          k.mm(pf[bd][:, qoff:BW], ones_bf, PT[ptb][:, qoff:BW], j == 0, j == last_j,
                               ["ones_bf", "PT%d" % ptb], PF(bd))
                      rdn = rden[Ib % 2]
                      k.P.op("dve", lambda e, rdn=rdn, bd=bd: e.reciprocal(out=rdn, in_=pf[bd][:, 0:BW]), PF(bd), ["rden%d" % (Ib % 2)])
                      k.tt("dve", obT[:, h, Ib * BW:(Ib + 1) * BW], pf[bo][:, 0:BW], rdn, ALU.mult, PF(bo) + ["rden%d" % (Ib % 2)],
                           ["obT%d_%d" % (h, Ib)])

              if dbg and s == 0:
                  outs.append(k.dma("pool", dbg_d["uT"][:, :], uT_flat, uT_all, [], "dbgp_uT"))
                  outs.append(k.dma("pool", dbg_d["oaT"][:, :], oaT_flat, ["oaT%d_%d" % (h, i) for h in range(NH) for i in range(NT)], [], "dbgp_oaT"))
                  outs.append(k.dma("pool", dbg_d["obT"][:, :], obT_flat, ["obT%d_%d" % (h, b) for h in range(NH) for b in range(NB)], [], "dbgp_obT"))
                  for ti, nm in enumerate(["g", "G", "beta", "kds", "decl", "c"]):
                      outs.append(k.dma("sp", dbg_d["tab"][:, ti * NT8:(ti + 1) * NT8], tabs[nm], [nm], [], "dbg"))

              chk("FOX")
              P.barrier()
              A.release(m0)
              mT_flat = A.bf(8 * T)
              mT = mT_flat.rearrange("p (c t) -> p c t", c=8)
              m_mT = A.mark()
              wga = [A.bf(8 * 256), A.bf(8 * 256)]
              wpa = [A.bf(8 * 256), A.bf(8 * 256)]
              sA, sB, mA, mB = [A.f32(BW) for _ in range(4)]
              wgaf = A.f32(8 * 256)
              wpaf = A.f32(8 * 256)
              oa_all = {b: ["oaT%d_%d" % (h, i) for h in range(NH) for i in range(b * TPB, (b + 1) * TPB)] for b in range(NB)}
              ob_all = {b: ["obT%d_%d" % (h, b) for h in range(NH)] for b in range(NB)}
              for n in range(8):
                  ws = n % 2
                  k.dma("sp", wgaf, wgate_d[n, :, :], [], ["wgaf"], "wgaf")
                  k.dma("sp", wpaf, wpab_d[n, :, :], [], ["wpaf"], "wpaf")
                  for cc in range(2):
                      k.cp("pool", wga[ws][:, cc * 1024:(cc + 1) * 1024], wgaf[:, cc * 1024:(cc + 1) * 1024], ["wgaf"], ["wga%d" % ws])
                      k.cp("pool", wpa[ws][:, cc * 1024:(cc + 1) * 1024], wpaf[:, cc * 1024:(cc + 1) * 1024], ["wpaf"], ["wpa%d" % ws])
                  wga3 = wga[ws].rearrange("p (c n) -> p c n", c=8)
                  wpa3 = wpa[ws].rearrange("p (c n) -> p c n", c=8)
                  for b in range(NB):
                      bs = slice(b * BW, (b + 1) * BW)
                      o4 = 4 * ((n * NB + b) % 2)
                      for c in range(8):
                          k.mm(pf[o4 + 0][:, 0:BW], wpa3[:, c, 0:128], oaT[:, c, bs], c == 0, c == 7, ["wpa%d" % ws] + oa_all[b], PF(o4 + 0))
                      for c in range(8):
                          k.mm(pf[o4 + 1][:, 0:BW], wga3[:, c, 0:128], uT[:, c, bs], c == 0, c == 7, ["wga%d" % ws] + uTb(b), PF(o4 + 1))
                      for c in range(8):
                          k.mm(pf[o4 + 2][:, 0:BW], wpa3[:, c, 128:256], obT[:, c, bs], c == 0, c == 7, ["wpa%d" % ws] + ob_all[b], PF(o4 + 2))
                      for c in range(8):
                          k.mm(pf[o4 + 3][:, 0:BW], wga3[:, c, 128:256], uT[:, c, bs], c == 0, c == 7, ["wga%d" % ws] + uTb(b), PF(o4 + 3))
                      k.act(sA, pf[o4 + 1][:, 0:BW], AF.Sigmoid, PF(o4 + 1), ["sA"])
                      k.act(sB, pf[o4 + 3][:, 0:BW], AF.Sigmoid, PF(o4 + 3), ["sB"])
                      k.tt("dve", mA, pf[o4 + 0][:, 0:BW], sA, ALU.mult, PF(o4 + 0) + ["sA"], ["mA"])
                      k.tt("dve", mB, pf[o4 + 2][:, 0:BW], sB, ALU.mult, PF(o4 + 2) + ["sB"], ["mB"])
                      k.tt("dve", mT[:, n, bs], mA, mB, ALU.add, ["mA", "mB"], ["mT%d_%d" % (n, b)])

              P.barrier()
              A.release(m_mT)
              wo = A.bf(8 * 1024)
              wo3 = wo.rearrange("p (c n) -> p c n", c=8)
              xt = [A.f32(D), A.f32(D)]
              hnb = [A.bf(D), A.bf(D)]
              junk = A.bf(D)
              ssc = [A.f32(1), A.f32(1)]
              k.dma("pool", wo, wout_d[:, :], [], ["wo"], "wo")
              for i in range(NT):
                  sl = i % 2
                  k.dma("sp", xt[sl], x_d[s, i * 128:(i + 1) * 128, :], [], ["xt%d" % sl], "xt%d" % sl)
                  for half in range(2):
                      for c in range(8):
                          k.mm(pf[half][:, :], mT[:, c, i * 128:(i + 1) * 128], wo3[:, c, half * 512:(half + 1) * 512], c == 0, c == 7,
                               ["wo"] + ["mT%d_%d" % (n_, i // TPB) for n_ in range(8)], PF(half))
                      k.tt("dve", h3[:, i, half * 512:(half + 1) * 512], pf[half][:, :], xt[sl][:, half * 512:(half + 1) * 512], ALU.add,
                           PF(half) + ["xt%d" % sl], ["h%d_%d" % (i, half)])
                  hr = ["h%d_0" % i, "h%d_1" % i]
                  k.act(junk, h3[:, i, :], AF.Square, hr, ["junk", "ss%d" % sl], accum=ssc[sl])
                  k.act(ssc[sl], ssc[sl], AF.Ln, ["ss%d" % sl, "eps_c"], ["ss%d" % sl], scale=1.0 / D, bias=eps_c)
                  k.act(ssc[sl], ssc[sl], AF.Exp, ["ss%d" % sl], ["ss%d" % sl], scale=-0.5)
                  k.stt(hnb[sl], h3[:, i, :], ssc[sl], gmlp_bc, ALU.mult, ALU.mult, hr + ["ss%d" % sl, "gmlp_bc"], ["hnb%d" % sl])
                  for c in range(8):
                      k.tr(pbs(c), hnb[sl][:, c * 128:(c + 1) * 128], ident_bf, ["hnb%d" % sl, "ident_bf"], ["pf7"])
                  k.cp("act", hnT[:, :, i * 128:(i + 1) * 128], pbf.rearrange("p (c t) -> p c t", c=8),
                       ["pf7"], ["hnT%d" % i])
              if dbg and s == 0:
                  outs.append(k.dma("sp", dbg_d["h"][:, :], h_flat, ["h%d_%d" % (i, hf) for i in range(NT) for hf in range(2)], [], "dbg"))

              chk("POSTH")
              P.barrier()
              A.release(m0)
              aT_flat = A.bf(32 * BW)
              aT = aT_flat.rearrange("p (f t) -> p f t", f=32)
              wup = [A.bf(1024) for _ in range(4)]
              wdn = [A.bf(1024) for _ in range(4)]
              wstg = [A.f32(1024) for _ in range(3)]
              scnt = 0
              rtmp = [A.f32(BW), A.f32(BW)]
              otile = [A.f32(D), A.f32(D)]
              for b in range(NB):
                  bs = slice(b * BW, (b + 1) * BW)
                  hb = ["hnT%d" % i for i in range(b * TPB, (b + 1) * TPB)]
                  for f in range(32):
                      ws = f % 4
                      sg = scnt % 3
                      scnt += 1
                      k.dma("sp", wstg[sg], wup_d[f, :, :], [], ["wstg%d" % sg], "wstg%d" % sg)
                      k.cp("pool", wup[ws], wstg[sg], ["wstg%d" % sg], ["wup%d" % ws])
                      wu3 = wup[ws].rearrange("p (c n) -> p c n", c=8)
                      pb = f % 2
                      for c in range(8):
                          k.mm(pf[pb][:, 0:BW], wu3[:, c, :], hnT[:, c, bs], c == 0, c == 7, ["wup%d" % ws] + hb, PF(pb))
                      k.act(rtmp[pb], pf[pb][:, 0:BW], AF.Relu, PF(pb), ["rtmp%d" % pb])
                      k.tt("dve", aT[:, f, :], rtmp[pb], rtmp[pb], ALU.mult, ["rtmp%d" % pb], ["aT%d" % f])
                  for f in range(32):
                      ws = f % 4
                      sg = scnt % 3
                      scnt += 1
                      k.dma("sp", wstg[sg], wdn_d[f, :, :], [], ["wstg%d" % sg], "wstg%d" % sg)
                      k.cp("pool", wdn[ws], wstg[sg], ["wstg%d" % sg], ["wdn%d" % ws])
                      for t in range(TPB):
                          for half in range(2):
                              bk = t * 2 + half
                              k.mm(pf[bk][:, :], aT[:, f, t * 128:(t + 1) * 128], wdn[ws][:, half * 512:(half + 1) * 512], f == 0, f == 31,
                                   ["aT%d" % f, "wdn%d" % ws], PF(bk))
                  for t in range(TPB):
                      i = b * TPB + t
                      sl = i % 2
                      for half in range(2):
                          bk = t * 2 + half
                          k.tt("dve", otile[sl][:, half * 512:(half + 1) * 512], pf[bk][:, :], h3[:, i, half * 512:(half + 1) * 512], ALU.add,
                               PF(bk) + ["h%d_%d" % (i, half)], ["ot%d_%d" % (sl, half)])
                      outs.append(k.dma("sp", out_d[s, i * 128:(i + 1) * 128, :], otile[sl], ["ot%d_0" % sl, "ot%d_1" % sl], [], "out%d" % sl))
              P.barrier()

          except StopBuild:
            outs.append(k.dma("pool", dbg_d["uT"][:, :], uT_flat, [], [], "dbgp_stop"))
            break
        P.final_waits += outs
        nw = P.emit(st)
        print("ops", len(P.ops), "waits", nw, "arena peak", A.peak)
    return nc


def prep_weights(inp, T):
    NT = T // 128
    f = lambda a: np.ascontiguousarray(np.asarray(a, dtype=np.float32))
    w_in = np.asarray(inp["w_in"])[0]
    offs = np.cumsum([0, 1024, 1024, 1024, 1024, 8, 8, 1024, 1024, 1024, 8, 1024, 1024])
    gq, gk, gv, gz, ga, gb, fq, fk, fv, ff, gta, gtb = [w_in[:, offs[i]:offs[i + 1]] for i in range(12)]

    def pc(w):
        return w.reshape(8, 128, -1).transpose(1, 0, 2)
    w_gdn = np.stack([pc(np.concatenate([g_[:, h * 128:(h + 1) * 128] for g_ in (gq, gk, gv, gz)], 1)).reshape(128, -1) for h in range(8)])
    w_fox = np.stack([pc(np.concatenate([g_[:, h * 128:(h + 1) * 128] for g_ in (fq, fk, fv)], 1)).reshape(128, -1) for h in range(8)])
    w_small = pc(np.concatenate([ga, gb, ff], 1)).reshape(128, -1)
    w_gate = np.stack([pc(np.concatenate([gta[:, n * 128:(n + 1) * 128], gtb[:, n * 128:(n + 1) * 128]], 1)).reshape(128, -1) for n in range(8)])
    pa = np.asarray(inp["w_proj_gdn"])[0]
    pb = np.asarray(inp["w_proj_fox"])[0]
    w_pab = np.stack([pc(np.concatenate([pa[:, n * 128:(n + 1) * 128], pb[:, n * 128:(n + 1) * 128]], 1)).reshape(128, -1) for n in range(8)])
    w_o = pc(np.asarray(inp["w_out"])[0]).reshape(128, -1)
    wu = np.asarray(inp["w_up"])[0]
    w_up = np.stack([pc(wu[:, fch * 128:(fch + 1) * 128]).reshape(128, -1) for fch in range(32)])
    w_down = np.asarray(inp["w_down"])[0].reshape(32, 128, 1024)
    cwt = np.asarray(inp["gdn_conv_w"])[0]
    convw = cwt.reshape(4, 24, 128).transpose(2, 1, 0).reshape(128, 96)
    d = {
        "w_gdn": w_gdn, "w_fox": w_fox, "w_small": w_small, "w_gate": w_gate, "w_pab": w_pab, "w_o": w_o,
        "w_up": w_up, "w_down": w_down, "convw": convw,
        "gmix": np.asarray(inp["norm_mix_g"])[0][None, :], "gmlp": np.asarray(inp["norm_mlp_g"])[0][None, :],
        "gnorm4": np.tile(np.asarray(inp["gdn_norm_g"])[0], 4)[None, :],
        "alog_t": np.tile(np.asarray(inp["gdn_a_log"])[0], NT)[None, :],
        "dtb_t": np.tile(np.asarray(inp["gdn_dt_bias"])[0], NT)[None, :],
        "fb_t": np.tile(np.asarray(inp["fox_f_bias"])[0], NT)[None, :],
        "fqg": np.asarray(inp["fox_q_norm_g"])[0][:, None], "fkg": np.asarray(inp["fox_k_norm_g"])[0][:, None],
    }
    return {k_: f(v) for k_, v in d.items()}


_NC_CACHE = {}


def kernel(**inputs):
    x = np.asarray(inputs["x"], dtype=np.float32)
    Bt, T, _ = x.shape
    ncores = 8
    nseq = Bt // ncores
    wd = prep_weights(inputs, T)
    key = (T, nseq)
    if key not in _NC_CACHE:
        nc = bass.Bass("TRN2", target_bir_lowering=False)
        build(nc, T, nseq)
        _NC_CACHE[key] = nc
    nc = _NC_CACHE[key]
    in_maps = []
    for c in range(ncores):
        m = dict(wd)
        m["x"] = np.ascontiguousarray(x[c * nseq:(c + 1) * nseq])
        in_maps.append(m)
    res = run_bass_kernel_spmd(nc, in_maps, core_ids=list(range(ncores)))
    out = np.concatenate([res.results[c]["out"] for c in range(ncores)], axis=0)
    return out.astype(np.float32)
```

```python
import contextlib
import concourse.bass as bass
import concourse.mybir as mybir

F32 = mybir.dt.float32
BF16 = mybir.dt.bfloat16
AF = mybir.ActivationFunctionType
ALU = mybir.AluOpType
AX = mybir.AxisListType

ENGS = ("pe", "act", "dve", "pool", "sp")


class Op:
    __slots__ = ("eng", "fn", "deps", "dma", "idx", "sig", "signal")

    def __init__(self, eng, fn, dma):
        self.eng = eng
        self.fn = fn
        self.deps = []
        self.dma = dma
        self.sig = None
        self.signal = False


class Prog:
    def __init__(self, nc):
        self.nc = nc
        self.ops = []
        self.last_w = {}
        self.readers = {}
        self.final_waits = []

    def op(self, eng, fn, reads=(), writes=(), dma=None):
        o = Op(eng, fn, dma)
        o.idx = len(self.ops)
        pfr = [r for r in reads if r.startswith("pf")]
        if pfr:
            reads = [r for r in reads if not r.startswith("pf")]
            writes = list(writes) + [r for r in pfr if r not in writes]
        deps = set()
        for r in reads:
            w = self.last_w.get(r)
            if w is not None:
                deps.add(w)
        for wkey in writes:
            w = self.last_w.get(wkey)
            if w is not None:
                deps.add(w)
            for rd in self.readers.get(wkey, ()):
                deps.add(rd)
        for d in deps:
            if d is o:
                continue
            if d.dma is None and d.eng == eng:
                if eng == "pe":
                    continue
            o.deps.append(d)
        for r in reads:
            self.readers.setdefault(r, []).append(o)
        for wkey in writes:
            self.last_w[wkey] = o
            self.readers[wkey] = []
        self.ops.append(o)
        return o

    def barrier(self):
        last = {}
        for o in self.ops:
            if o.fn is None:
                continue
            k = ("dma", o.dma) if o.dma is not None else ("eng", o.eng)
            last[k] = o
        for e in ENGS:
            b = Op(e, None, None)
            b.idx = len(self.ops)
            b.deps = [d for k, d in last.items() if not (k == ("eng", e))]
            self.ops.append(b)
        self.last_w = {}
        self.readers = {}

    def emit(self, stack):
        nc = self.nc
        for o in self.ops:
            for d in o.deps:
                d.signal = True
        for o in self.final_waits:
            o.signal = True
        cnt = {}
        for o in self.ops:
            if o.dma is not None:
                k = ("dma", o.dma)
                cnt[k] = cnt.get(k, 0) + 16
                o.sig = (k, cnt[k])
            elif o.signal:
                k = ("eng", o.eng)
                cnt[k] = cnt.get(k, 0) + 1
                o.sig = (k, cnt[k])
        sems = {}
        for k in cnt:
            nm = "s_" + "_".join(str(x) for x in k)
            sems[k] = stack.enter_context(nc.semaphore(nm))
        per_eng = {e: [o for o in self.ops if o.eng == e] for e in ENGS}
        finals = list(self.final_waits)
        block = stack.enter_context(nc.Block())
        nwaits = [0]

        def run(eng_name, eobj):
            waited = {}
            for o in per_eng[eng_name]:
                need = {}
                for d in o.deps:
                    k, v = d.sig
                    if waited.get(k, 0) >= v:
                        continue
                    if need.get(k, 0) < v:
                        need[k] = v
                for k, v in need.items():
                    eobj.wait_ge(sems[k], v)
                    waited[k] = v
                    nwaits[0] += 1
                if o.fn is None:
                    continue
                ins = o.fn(eobj)
                if o.dma is not None:
                    ins.then_inc(sems[o.sig[0]], 16)
                elif o.signal:
                    ins.then_inc(sems[o.sig[0]], 1)
            if eng_name == "sp":
                need = {}
                for d in finals:
                    k, v = d.sig
                    if need.get(k, 0) < v:
                        need[k] = v
                for k, v in need.items():
                    if waited.get(k, 0) < v:
                        eobj.wait_ge(sems[k], v)

        @block.tensor
        def _(e):
            run("pe", e)

        @block.scalar
        def _(e):
            run("act", e)

        @block.vector
        def _(e):
            run("dve", e)

        @block.gpsimd
        def _(e):
            run("pool", e)

        @block.sync
        def _(e):
            run("sp", e)
        return nwaits[0]

import numpy as np
import os
from concourse.bass_utils import run_bass_kernel_spmd

D = 1024
NH = 8
DH = 128
DFF = 4096
EPS = 1e-6
ARENA = 105600


class StopBuild(Exception):
    pass


class Arena:
    def __init__(self, t):
        self.t = t
        self.off = 0
        self.peak = 0

    def bf(self, n):
        n2 = (n + 15) // 16 * 16
        o = self.off
        self.off += n2
        self.peak = max(self.peak, self.off)
        assert self.off <= ARENA, ("arena overflow", self.off)
        return self.t[:, o:o + n]

    def f32(self, n):
        n2 = (2 * n + 15) // 16 * 16
        o = self.off
        self.off += n2
        self.peak = max(self.peak, self.off)
        assert self.off <= ARENA, ("arena overflow", self.off)
        return self.t[:, o:o + 2 * n].bitcast(F32)

    def mark(self):
        return self.off

    def release(self, m):
        self.off = m


class K:
    def __init__(self, P):
        self.P = P

    def mm(self, out, lhsT, rhs, start, stop, r, w):
        return self.P.op("pe", lambda e: e.matmul(out, lhsT=lhsT, rhs=rhs, start=start, stop=stop), r, w)

    def tr(self, out, in_, ident, r, w):
        return self.P.op("pe", lambda e: e.transpose(out=out, in_=in_, identity=ident), r, w)

    def act(self, out, in_, func, r, w, scale=None, bias=None, accum=None):
        kw = {}
        if scale is not None:
            kw["scale"] = scale
        if bias is not None:
            kw["bias"] = bias
        if accum is not None:
            kw["accum_out"] = accum
        return self.P.op("act", lambda e: e.activation(out=out, in_=in_, func=func, **kw), r, w)

    def cp(self, eng, out, in_, r, w):
        if eng == "act":
            return self.P.op("act", lambda e: e.copy(out=out, in_=in_), r, w)
        return self.P.op(eng, lambda e: e.tensor_copy(out=out, in_=in_), r, w)

    def tt(self, eng, out, in0, in1, op, r, w):
        return self.P.op(eng, lambda e: e.tensor_tensor(out=out, in0=in0, in1=in1, op=op), r, w)

    def ts(self, eng, out, in0, s1, op0, r, w, s2=None, op1=None):
        if op1 is None:
            return self.P.op(eng, lambda e: e.tensor_scalar(out=out, in0=in0, scalar1=s1, scalar2=None, op0=op0), r, w)
        return self.P.op(eng, lambda e: e.tensor_scalar(out=out, in0=in0, scalar1=s1, scalar2=s2, op0=op0, op1=op1), r, w)

    def stt(self, out, in0, scalar, in1, op0, op1, r, w):
        return self.P.op("dve", lambda e: e.scalar_tensor_tensor(out=out, in0=in0, scalar=scalar, in1=in1, op0=op0, op1=op1), r, w)

    def memset(self, eng, ap, val, w):
        return self.P.op(eng, lambda e: e.memset(ap, val), (), w)

    def dma(self, eng, out, in_, r, w, key):
        return self.P.op(eng, lambda e: e.dma_start(out=out, in_=in_), r, w, dma=key)


def build(nc, T, NSEQ, dbg=False, stop=None):
    NT = T // 128
    BW = min(512, T)
    NB = T // BW
    TPB = BW // 128
    NT8 = NT * 8
    dram = lambda n, s, kind="ExternalInput": nc.dram_tensor(n, list(s), F32, kind=kind).ap()
    x_d = dram("x", [NSEQ, T, D])
    out_d = dram("out", [NSEQ, T, D], "ExternalOutput")
    wgdn_d = dram("w_gdn", [NH, 128, 8 * 512])
    wfox_d = dram("w_fox", [NH, 128, 8 * 384])
    wsm_d = dram("w_small", [128, 8 * 24])
    wgate_d = dram("w_gate", [8, 128, 8 * 256])
    wpab_d = dram("w_pab", [8, 128, 8 * 256])
    wout_d = dram("w_o", [128, 8 * 1024])
    wup_d = dram("w_up", [32, 128, 8 * 128])
    wdn_d = dram("w_down", [32, 128, 1024])
    convw_d = dram("convw", [128, 24 * 4])
    gmix_d = dram("gmix", [1, D])
    gmlp_d = dram("gmlp", [1, D])
    gnorm4_d = dram("gnorm4", [1, 512])
    alog_d = dram("alog_t", [1, NT8])
    dtb_d = dram("dtb_t", [1, NT8])
    fb_d = dram("fb_t", [1, NT8])
    fqg_d = dram("fqg", [128, 1])
    fkg_d = dram("fkg", [128, 1])
    dbg_d = {}
    if dbg:
        dbg_d["uT"] = dram("d_uT", [128, 8 * T], "ExternalOutput")
        dbg_d["oaT"] = dram("d_oaT", [128, 8 * T], "ExternalOutput")
        dbg_d["obT"] = dram("d_obT", [128, 8 * T], "ExternalOutput")
        dbg_d["tab"] = dram("d_tab", [128, 6 * NT8], "ExternalOutput")
        dbg_d["h"] = dram("d_h", [128, NT * D], "ExternalOutput")

    with contextlib.ExitStack() as st:
        P = Prog(nc)
        k = K(P)
        arena_t = st.enter_context(nc.sbuf_tensor("arena", [128, ARENA], BF16))
        A = Arena(arena_t)
        pf = [st.enter_context(nc.psum_tensor("pf%d" % i, [128, 512], F32)) for i in range(8)]
        pbf = pf[7][:, :].bitcast(BF16)

        def PF(b, c0=0, c1=512):
            return ["pf%d" % b]

        def pq(b, q, n=128):
            return pf[b][:, q * 128:q * 128 + n]

        def pbs(s):
            return pbf[:, s * 128:(s + 1) * 128]

        ident_bf = A.bf(128)
        ones_bf = A.bf(128)
        onesm_bf = A.bf(128)
        mUi_bf = A.bf(128)
        ident_f = A.f32(128)
        ones_f = A.f32(128)
        triU_f = A.f32(128)
        mUs_bd = A.f32(128)
        mLs_bd = A.f32(128)
        mL_off = A.f32(128)
        gmix_bc = A.f32(D)
        gmlp_bc = A.f32(D)
        gnorm4 = A.f32(512)
        alog_bc = A.f32(NT8)
        dtb_bc = A.f32(NT8)
        fb_bc = A.f32(NT8)
        cw = A.f32(96)
        fqg = A.f32(1)
        fkg = A.f32(1)
        eps_c = A.f32(1)
        one_c = A.f32(1)

        k.memset("pool", ones_bf, 1.0, ["ones_bf"])
        k.memset("pool", onesm_bf, 1.0 / 128, ["onesm_bf"])
        k.memset("pool", ones_f, 1.0, ["ones_f"])
        k.memset("pool", eps_c, EPS, ["eps_c"])
        k.memset("pool", one_c, 1.0, ["one_c"])

        def asel(out, in_, step, cm, cmp, r, w):
            P.op("pool", lambda e: e.affine_select(out=out, in_=in_, pattern=[[step, 128]], compare_op=cmp, fill=0.0,
                                                   base=0, channel_multiplier=cm), r, w)
        asel(ident_bf, ones_bf, 1, -1, ALU.is_equal, ["ones_bf"], ["ident_bf"])
        asel(ident_f, ones_f, 1, -1, ALU.is_equal, ["ones_f"], ["ident_f"])
        asel(mUi_bf, ones_bf, 1, -1, ALU.is_ge, ["ones_bf"], ["mUi_bf"])
        asel(triU_f, ones_f, 1, -1, ALU.is_ge, ["ones_f"], ["triU_f"])
        asel(mUs_bd, ones_f, 1, -1, ALU.is_gt, ["ones_f"], ["mUs_bd"])
        asel(mLs_bd, ones_f, -1, 1, ALU.is_gt, ["ones_f"], ["mLs_bd"])
        k.memset("pool", mUs_bd[0:64, 64:128], 0.0, ["mUs_bd"])
        k.memset("pool", mLs_bd[64:128, 0:64], 0.0, ["mLs_bd"])
        k.memset("pool", mL_off, 0.0, ["mL_off"])
        k.memset("pool", mL_off[64:128, 0:64], 1.0, ["mL_off"])
        k.dma("sp", gmix_bc, gmix_d[0:1, :].partition_broadcast(128), [], ["gmix_bc"], "cst1")
        k.dma("sp", gmlp_bc, gmlp_d[0:1, :].partition_broadcast(128), [], ["gmlp_bc"], "cst2")
        k.dma("sp", gnorm4, gnorm4_d[0:1, :].partition_broadcast(128), [], ["gnorm4"], "cst3")
        k.dma("sp", alog_bc, alog_d[0:1, :].partition_broadcast(128), [], ["alog_bc"], "cst4")
        k.dma("sp", dtb_bc, dtb_d[0:1, :].partition_broadcast(128), [], ["dtb_bc"], "cst5")
        k.dma("sp", fb_bc, fb_d[0:1, :].partition_broadcast(128), [], ["fb_bc"], "cst6")
        k.dma("sp", cw, convw_d[:, :], [], ["cw"], "cst7")
        k.dma("sp", fqg, fqg_d[:, :], [], ["fqg"], "cst8")
        k.dma("sp", fkg, fkg_d[:, :], [], ["fkg"], "cst9")

        a1 = A.mark()
        uT_flat = A.bf(8 * T)
        oaT_flat = A.bf(8 * T)
        obT_flat = A.bf(8 * T)
        a1_end = A.mark()
        uT = uT_flat.rearrange("p (c t) -> p c t", c=8)
        oaT = oaT_flat.rearrange("p (c t) -> p c t", c=8)
        obT = obT_flat.rearrange("p (c t) -> p c t", c=8)
        A.release(a1)
        h_flat = A.f32(NT * D)
        hnT_flat = A.bf(8 * T)
        assert A.mark() <= a1_end
        A.release(a1_end)
        h3 = h_flat.rearrange("p (i d) -> p i d", i=NT)
        hnT = hnT_flat.rearrange("p (c t) -> p c t", c=8)

        tabs = {}
        for nm in ["g", "G", "beta", "nbeta", "kds", "decl", "lf", "c", "tot", "cpre", "cend", "ty", "ta", "tb"]:
            tabs[nm] = A.f32(NT8)
        btab_flat = A.f32(8 * NT * NT)
        btab = btab_flat.rearrange("p (h j i) -> p h j i", h=8, j=NT)
        m0 = A.mark()

        def t3(ap):
            return ap.rearrange("p (i h) -> p i h", h=8)

        outs = []

        def chk(name):
            if stop == name:
                raise StopBuild()

        for s in range(NSEQ):
          try:
              A.release(m0)
              xt = [A.f32(D), A.f32(D)]
              junk = A.bf(D)
              ubf = [A.bf(D), A.bf(D)]
              ssc = [A.f32(1), A.f32(1)]
              for i in range(NT):
                  sl = i % 2
                  k.dma("sp", xt[sl], x_d[s, i * 128:(i + 1) * 128, :], [], ["xt%d" % sl], "xt%d" % sl)
                  k.act(junk, xt[sl], AF.Square, ["xt%d" % sl], ["junk", "ss%d" % sl], accum=ssc[sl])
                  k.act(ssc[sl], ssc[sl], AF.Ln, ["ss%d" % sl, "eps_c"], ["ss%d" % sl], scale=1.0 / D, bias=eps_c)
                  k.act(ssc[sl], ssc[sl], AF.Exp, ["ss%d" % sl], ["ss%d" % sl], scale=-0.5)
                  k.stt(ubf[sl], xt[sl], ssc[sl], gmix_bc, ALU.mult, ALU.mult, ["xt%d" % sl, "ss%d" % sl, "gmix_bc"], ["ubf%d" % sl])
                  for c in range(8):
                      k.tr(pbs(c), ubf[sl][:, c * 128:(c + 1) * 128], ident_bf, ["ubf%d" % sl, "ident_bf"], ["pf7"])
                  k.cp("act", uT[:, :, i * 128:(i + 1) * 128], pbf.rearrange("p (c t) -> p c t", c=8),
                       ["pf7"], ["uT%d" % i])
              uT_all = ["uT%d" % i for i in range(NT)]

              def uTb(b):
                  return ["uT%d" % i for i in range(b * TPB, (b + 1) * TPB)]

              if stop == "P1":
                  outs.append(k.dma("pool", dbg_d["uT"][:, :], uT_flat, uT_all, [], "dbgp_uT"))
                  break
              P.barrier()
              A.release(m0)
              wsm = A.bf(8 * 24)
              wsm3 = wsm.rearrange("p (c n) -> p c n", c=8)
              k.dma("pool", wsm, wsm_d[:, :], [], ["wsm"], "wsm")
              psS = pf[0][:, 0:NT * 24]
              for i in range(NT):
                  for c in range(8):
                      k.mm(pf[0][:, i * 24:(i + 1) * 24], uT[:, c, i * 128:(i + 1) * 128], wsm3[:, c, :], c == 0, c == 7,
                           ["uT%d" % i, "wsm"], PF(0))
              ps3 = psS.rearrange("p (i n) -> p i n", n=24)
              ga, gb, ff = ps3[:, :, 0:8], ps3[:, :, 8:16], ps3[:, :, 16:24]
              ty, ta, tb = tabs["ty"], tabs["ta"], tabs["tb"]
              k.tt("dve", t3(ty), ga, t3(dtb_bc), ALU.add, PF(0) + ["dtb_bc"], ["ty"])
              k.stt(ta, ty, -1.0, ty, ALU.mult, ALU.max, ["ty"], ["ta"])
              k.act(ta, ta, AF.Exp, ["ta"], ["ta"], scale=-1.0)
              k.act(ta, ta, AF.Ln, ["ta", "one_c"], ["ta"], bias=one_c)
              k.stt(ty, ty, 0.0, ta, ALU.max, ALU.add, ["ty", "ta"], ["ty"])
              k.act(tb, alog_bc, AF.Exp, ["alog_bc"], ["tb"])
              k.stt(tabs["g"], ty, -1.0, tb, ALU.mult, ALU.mult, ["ty", "tb"], ["g"])
              k.act(t3(tabs["beta"]), gb, AF.Sigmoid, PF(0), ["beta"])
              k.ts("dve", tabs["nbeta"], tabs["beta"], -1.0, ALU.mult, ["beta"], ["nbeta"])
              k.tt("dve", t3(ty), ff, t3(fb_bc), ALU.add, PF(0) + ["fb_bc"], ["ty"])
              k.stt(ta, ty, -1.0, ty, ALU.mult, ALU.max, ["ty"], ["ta"])
              k.act(ta, ta, AF.Exp, ["ta"], ["ta"], scale=-1.0)
              k.act(ta, ta, AF.Ln, ["ta", "one_c"], ["ta"], bias=one_c)
              k.stt(tabs["lf"], ty, 0.0, ta, ALU.min, ALU.subtract, ["ty", "ta"], ["lf"])
              k.mm(pf[1][:, 0:NT8], triU_f, tabs["g"], True, True, ["triU_f", "g"], ["pf1"])
              k.mm(pf[1][:, 128:128 + NT8], ones_f, tabs["g"], True, True, ["ones_f", "g"], ["pf1"])
              k.mm(pf[1][:, 256:256 + NT8], triU_f, tabs["lf"], True, True, ["triU_f", "lf"], ["pf1"])
              k.mm(pf[1][:, 384:384 + NT8], ones_f, tabs["lf"], True, True, ["ones_f", "lf"], ["pf1"])
              k.cp("act", tabs["G"], pf[1][:, 0:NT8], ["pf1"], ["G"])
              k.act(tabs["decl"], pf[1][:, 128:128 + NT8], AF.Exp, ["pf1"], ["decl"])
              k.tt("dve", ty, pf[1][:, 128:128 + NT8], tabs["G"], ALU.subtract, ["pf1", "G"], ["ty"])
              k.act(tabs["kds"], ty, AF.Exp, ["ty"], ["kds"])
              k.cp("act", tabs["tot"], pf[1][:, 384:384 + NT8], ["pf1"], ["tot"])
              k.memset("dve", tabs["cpre"][:, 0:8], 0.0, ["cpre"])
              for i in range(1, NT):
                  k.tt("dve", tabs["cpre"][:, i * 8:(i + 1) * 8], tabs["cpre"][:, (i - 1) * 8:i * 8],
                       tabs["tot"][:, (i - 1) * 8:i * 8], ALU.add, ["cpre", "tot"], ["cpre"])
              k.tt("dve", tabs["c"], pf[1][:, 256:256 + NT8], tabs["cpre"], ALU.add, ["pf1", "cpre"], ["c"])
              k.tt("dve", tabs["cend"], tabs["cpre"], tabs["tot"], ALU.add, ["cpre", "tot"], ["cend"])
              cend3 = t3(tabs["cend"])
              n_ = 0
              for h in range(NH):
                  for j in range(NT):
                      eng = "dve" if n_ % 2 == 0 else "pool"
                      n_ += 1
                      col = j * 8 + h
                      k.ts(eng, btab[:, h, j, :], cend3[:, :, h], tabs["c"][:, col:col + 1], ALU.subtract, ["cend", "c"], ["btab"])

              if stop == "SMALL":
                  outs.append(k.dma("pool", dbg_d["uT"][:, :], uT_flat, uT_all, [], "dbgp_uT"))
                  for ti, nm in enumerate(["g", "G", "beta", "kds", "decl", "c"]):
                      outs.append(k.dma("sp", dbg_d["tab"][:, ti * NT8:(ti + 1) * NT8], tabs[nm], [nm], [], "dbg"))
                  break
              P.barrier()
              A.release(m0)
              wg = [A.bf(8 * 512)]
              GRP = min(NT, int(os.environ.get("GDN_GRP", "4")))
              GW = GRP * 128
              NG = NT // GRP
              qT, kT, vT = A.bf(T), A.bf(T), A.bf(T)
              Kd_f, V_f, gz_f = A.bf(T), A.f32(T), A.f32(T)
              Kd = Kd_f.rearrange("p (i d) -> p i d", d=128)
              Vt = V_f.rearrange("p (i d) -> p i d", d=128)
              gz = gz_f.rearrange("p (i d) -> p i d", d=128)
              S_f = A.f32(128)
              S_bf, R_bf, vnew, on_bf, junk128 = [A.bf(128) for _ in range(5)]
              ssq = A.f32(1)
              mp = A.mark()
              sq = [A.bf(BW), A.bf(BW)]
              cb = [A.f32(BW + 3), A.f32(BW + 3)]
              acc = [A.f32(BW), A.f32(BW)]
              lnt = [A.f32(BW), A.f32(BW)]
              rs = [A.f32(BW), A.f32(BW)]
              tmpz = A.f32(BW)
              A.release(mp)
              grep_ = [A.f32(128), A.f32(128)]
              brep_ = [A.f32(128), A.f32(128)]
              E_, Y_, M1_, DuI, DuB, DlB, DlO, bbc = [A.f32(GW) for _ in range(8)]
              eG = A.bf(GW)
              Qb = [A.bf(GW), A.bf(GW)]
              Pb = [A.bf(GW), A.bf(GW)]
              Mb = [A.bf(GW), A.bf(GW)]
              Ao, Td, X1 = [A.bf(GW) for _ in range(3)]
              AttnT = [A.bf(GW), A.bf(GW)]
              QeT = [A.bf(GW), A.bf(GW)]
              KeT = [A.bf(GW), A.bf(GW)]
              TpT = [A.bf(GW), A.bf(GW)]

              def fm_norm(src, src_r, dst, dst_w, ones_t, ones_r, mult, mult_r, pbank, par):
                  k.act(sq[par], src, AF.Square, src_r, ["sq%d" % par])
                  k.mm(pf[pbank][:, 0:BW], ones_t, sq[par], True, True, ["sq%d" % par] + ones_r, PF(pbank))
                  k.act(lnt[par], pf[pbank][:, 0:BW], AF.Ln, PF(pbank) + ["eps_c"], ["lnt%d" % par], bias=eps_c)
                  k.act(rs[par], lnt[par], AF.Exp, ["lnt%d" % par], ["rs%d" % par], scale=-0.5)
                  k.stt(dst, src, mult, rs[par], ALU.mult, ALU.mult, src_r + ["rs%d" % par] + mult_r, dst_w)

              pcnt = 0
              for h in range(NH):
                  ws = 0
                  wgs = wg[ws]
                  wg3 = wgs.rearrange("p (c n) -> p c n", c=8)
                  if h == 0:
                      k.dma("pool", wgs, wgdn_d[h, :, :], [], ["wg%d" % ws], "wg%d" % ws)
                  def zgate(ws=ws, wg3=wg3):
                      for i in range(NT):
                          q4 = i % 4
                          for c in range(8):
                              k.mm(pq(4, q4), uT[:, c, i * 128:(i + 1) * 128], wg3[:, c, 384:512], c == 0, c == 7,
                                   ["uT%d" % i, "wg%d" % ws], ["pf4"])
                          if q4 == TPB - 1 or i == NT - 1:
                              n4 = q4 + 1
                              i0 = i - q4
                              k.act(tmpz[:, 0:n4 * 128], pf[4][:, 0:n4 * 128], AF.Silu, ["pf4"], ["tmpz"])
                              k.tt("dve", gz_f[:, i0 * 128:(i0 + n4) * 128], tmpz[:, 0:n4 * 128], gnorm4[:, 0:n4 * 128], ALU.mult,
                                   ["tmpz", "gnorm4"], ["gz%d" % ii for ii in range(i0, i0 + n4)])
                          yield
                  zgen = zgate()
                  for grp in range(3):
                      chunk = grp * 8 + h
                      dstT = (qT, kT, vT)[grp]
                      dnm = ("qT", "kT", "vT")[grp]
                      for b in range(NB):
                          pb = b % 2
                          pr = pcnt % 2
                          pcnt += 1
                          cbn, accn = "cb%d" % pr, "acc%d" % pr
                          cbc, accc = cb[pr], acc[pr]
                          for c in range(8):
                              k.mm(pf[pb][:, 0:BW], wg3[:, c, grp * 128:(grp + 1) * 128], uT[:, c, b * BW:(b + 1) * BW],
                                   c == 0, c == 7, ["wg%d" % ws] + uTb(b), PF(pb))
                          if b == 0:
                              k.memset("pool", cbc[:, 0:3], 0.0, [cbn])
                          else:
                              k.cp("act", cbc[:, 0:3], cb[1 - pr][:, BW:BW + 3], ["cb%d" % (1 - pr)], [cbn])
                          k.cp("act", cbc[:, 3:3 + BW], pf[pb][:, 0:BW], PF(pb), [cbn])
                          k.ts("dve", accc, cbc[:, 0:BW], cw[:, chunk * 4:chunk * 4 + 1], ALU.mult, [cbn, "cw"], [accn])
                          for tap in range(1, 4):
                              k.stt(accc, cbc[:, tap:tap + BW], cw[:, chunk * 4 + tap:chunk * 4 + tap + 1], accc, ALU.mult, ALU.add,
                                    [cbn, "cw", accn], [accn])
                          k.act(accc, accc, AF.Silu, [accn], [accn])
                          dsl = dstT[:, b * BW:(b + 1) * BW]
                          dw = ["%s%d" % (dnm, i) for i in range(b * TPB, (b + 1) * TPB)]
                          if grp < 2:
                              fm_norm(accc, [accn], dsl, dw, ones_bf, ["ones_bf"], (DH ** -0.5) if grp == 0 else 1.0, [], 2 + pr, pr)
                          else:
                              k.cp("pool", dsl, accc, [accn], dw)
                          next(zgen, None)
                          next(zgen, None)
                  chk("GDN_a")
                  for _ in zgen:
                      pass
                  chk("GDN_b")
                  for i in range(NT):
                      col = i * 8 + h
                      sk = (2 * i) % 8
                      k.tr(pbs(sk), kT[:, i * 128:(i + 1) * 128], ident_bf, ["kT%d" % i, "ident_bf"], ["pf7"])
                      k.act(Kd[:, i, :], pbs(sk), AF.Copy, ["pf7", "kds"], ["Kd%d" % i], scale=tabs["kds"][:, col:col + 1])
                      sv = (2 * i + 1) % 8
                      k.tr(pbs(sv), vT[:, i * 128:(i + 1) * 128], ident_bf, ["vT%d" % i, "ident_bf"], ["pf7"])
                      k.cp("act", Vt[:, i, :], pbs(sv), ["pf7"], ["V%d" % i])
                  chk("GDN_c")
                  P.barrier()
                  if h + 1 < NH:
                      k.dma("pool", wgs, wgdn_d[h + 1, :, :], [], ["wg%d" % ws], "wg%d" % ws)
                  k.memset("pool", S_f, 0.0, ["S_f"])
                  k.memset("pool", S_bf, 0.0, ["S_bf"])

                  def v3(ap):
                      return ap.rearrange("p (g f) -> p g f", g=GRP)

                  def bcm(m):
                      return m.unsqueeze(1).to_broadcast([128, GRP, 128])

                  def colbc(tab, i0, h_):
                      return t3(tab)[:, i0:i0 + GRP, h_].unsqueeze(2).to_broadcast([128, GRP, 128])

                  def gs(t):
                      return slice(t * 128, (t + 1) * 128)

                  def prep(g, h=h):
                      i0 = g * GRP
                      gp = g % 2
                      b0v, b1v, b2v, b3v = [pf[b_][:, 0:GW] for b_ in range(4)]
                      for t in range(GRP):
                          col = (i0 + t) * 8 + h
                          r2 = t % 2
                          k.cp("dve", grep_[r2], tabs["g"][:, col:col + 1].to_broadcast([128, 128]), ["g"], ["grep%d" % r2])
                          k.cp("dve", brep_[r2], tabs["beta"][:, col:col + 1].to_broadcast([128, 128]), ["beta"], ["brep%d" % r2])
                          k.mm(pq(0, t), grep_[r2], triU_f, True, True, ["grep%d" % r2, "triU_f"], ["pf0"])
                          k.mm(pq(1, t), brep_[r2], ident_f, True, True, ["brep%d" % r2, "ident_f"], ["pf1"])
                      k.cp("act", bbc, b1v, ["pf1"], ["bbc"])
                      k.act(eG, b0v, AF.Exp, ["pf0"], ["eG"])
                      k.tt("dve", v3(E_), v3(b0v), colbc(tabs["G"], i0, h), ALU.subtract, ["pf0", "G"], ["E"])
                      k.ts("dve", Y_, E_, 0.0, ALU.max, ["E"], ["Y"])
                      k.ts("dve", E_, E_, 0.0, ALU.min, ["E"], ["E"])
                      k.act(Y_, Y_, AF.Exp, ["Y"], ["Y"], scale=-1.0)
                      k.act(E_, E_, AF.Exp, ["E"], ["E"])
                      k.tt("pool", v3(M1_), v3(bbc), bcm(mUs_bd), ALU.mult, ["bbc", "mUs_bd"], ["M1"])
                      k.tt("dve", v3(DuI), v3(E_), bcm(triU_f), ALU.mult, ["E", "triU_f"], ["DuI"])
                      k.tt("dve", DuB, E_, M1_, ALU.mult, ["E", "M1"], ["DuB"])
                      k.tt("dve", v3(DlB), v3(Y_), bcm(mLs_bd), ALU.mult, ["Y", "mLs_bd"], ["DlB"])
                      k.tt("pool", v3(DlB), v3(DlB), colbc(tabs["beta"], i0, h), ALU.mult, ["DlB", "beta"], ["DlB"])
                      k.tt("dve", v3(DlO), v3(Y_), bcm(mL_off), ALU.mult, ["Y", "mL_off"], ["DlO"])
                      k.tt("pool", v3(DlO), v3(DlO), colbc(tabs["beta"], i0, h), ALU.mult, ["DlO", "beta"], ["DlO"])
                      yield
                      for t in range(GRP):
                          i = i0 + t
                          kTi = kT[:, i * 128:(i + 1) * 128]
                          qTi = qT[:, i * 128:(i + 1) * 128]
                          k.mm(pq(2, t), kTi, kTi, True, True, ["kT%d" % i], ["pf2"])
                          k.mm(pq(3, t), kTi, qTi, True, True, ["kT%d" % i, "qT%d" % i], ["pf3"])
                      kr = ["kT%d" % (i0 + t) for t in range(GRP)]
                      qr = ["qT%d" % (i0 + t) for t in range(GRP)]
                      k.tt("dve", Qb[0], b2v, DlB, ALU.mult, ["pf2", "DlB"], ["Q0"])
                      k.tt("dve", Ao, b2v, DlO, ALU.mult, ["pf2", "DlO"], ["Ao"])
                      k.tt("dve", Pb[0], b2v, DuB, ALU.mult, ["pf2", "DuB"], ["P0"])
                      k.tt("dve", AttnT[gp], b3v, DuI, ALU.mult, ["pf3", "DuI"], ["AttnT%d" % gp])
                      k.tt("dve", v3(Mb[0]), bcm(ident_bf), v3(Pb[0]), ALU.subtract, ["ident_bf", "P0"], ["M0"])
                      k.tt("dve", QeT[gp], qT[:, i0 * 128:i0 * 128 + GW], eG, ALU.mult, qr + ["eG"], ["QeT%d" % gp])
                      k.tt("dve", KeT[gp], kT[:, i0 * 128:i0 * 128 + GW], eG, ALU.mult, kr + ["eG"], ["KeT%d" % gp])
                      cur = 0
                      for lvl in range(1, 6):
                          nxt = 1 - cur
                          for t in range(GRP):
                              if lvl < 5:
                                  k.mm(pq(0, t), Qb[cur][:, gs(t)], Pb[cur][:, gs(t)], True, True, ["Q%d" % cur, "P%d" % cur], ["pf0"])
                              k.mm(pq(1, t), Pb[cur][:, gs(t)], Qb[cur][:, gs(t)], True, True, ["Q%d" % cur, "P%d" % cur], ["pf1"])
                          if lvl < 5:
                              k.cp("act", Pb[nxt], b0v, ["pf0"], ["P%d" % nxt])
                          k.cp("dve", Qb[nxt], b1v, ["pf1"], ["Q%d" % nxt])
                          for t in range(GRP):
                              k.mm(pq(2, t), ident_bf, Mb[cur][:, gs(t)], True, False, ["ident_bf", "M%d" % cur], ["pf2"])
                              k.mm(pq(2, t), Qb[nxt][:, gs(t)], Mb[cur][:, gs(t)], False, True, ["Q%d" % nxt, "M%d" % cur], ["pf2"])
                          k.cp("act", Mb[nxt], b2v, ["pf2"], ["M%d" % nxt])
                          if lvl == 5:
                              k.cp("dve", Y_, b2v, ["pf2"], ["Y"])
                          cur = nxt
                          if lvl == 2:
                              yield
                      yield
                      Ud = Mb[cur]
                      Udn = "M%d" % cur
                      for t in range(GRP):
                          k.tr(pbs(t), Ud[:, gs(t)], ident_bf, [Udn, "ident_bf"], ["pf7"])
                      k.cp("act", Td, pbf[:, 0:GW], ["pf7"], ["Td"])
                      for t in range(GRP):
                          k.mm(pq(3, t), Ao[:, gs(t)], Ud[:, gs(t)], True, True, ["Ao", Udn], ["pf3"])
                      k.cp("act", X1, b3v, ["pf3"], ["X1"])
                      for t in range(GRP):
                          k.mm(pq(0, t), Td[:, gs(t)], X1[:, gs(t)], True, True, ["Td", "X1"], ["pf0"])
                      k.tt("dve", Y_, Y_, b0v, ALU.subtract, ["Y", "pf0"], ["Y"])
                      k.tt("pool", v3(TpT[gp]), v3(Y_), colbc(tabs["beta"], i0, h), ALU.mult, ["Y", "beta"], ["TpT%d" % gp])
                      yield

                  def scan_step(i, h=h):
                      g = i // GRP
                      t = i % GRP
                      gp = g % 2
                      d2 = i % 2
                      col = i * 8 + h
                      k.mm(pq(4, 0), KeT[gp][:, gs(t)], S_bf, True, True, ["KeT%d" % gp, "S_bf"], ["pf4"])
                      k.tt("dve", R_bf, Vt[:, i, :], pq(4, 0), ALU.subtract, ["V%d" % i, "pf4"], ["R_bf"])
                      k.mm(pq(4, 1), TpT[gp][:, gs(t)], R_bf, True, True, ["TpT%d" % gp, "R_bf"], ["pf4"])
                      k.cp("act", vnew, pq(4, 1), ["pf4"], ["vnew"])
                      k.mm(pq(5 + d2, 0), QeT[gp][:, gs(t)], S_bf, True, False, ["QeT%d" % gp, "S_bf"], ["pf%d" % (5 + d2)])
                      k.mm(pq(5 + d2, 0), AttnT[gp][:, gs(t)], vnew, False, True, ["AttnT%d" % gp, "vnew"], ["pf%d" % (5 + d2)])
                      k.mm(pq(4, 3), Kd[:, i, :], vnew, True, True, ["Kd%d" % i, "vnew"], ["pf4"])
                      k.stt(S_f, S_f, tabs["decl"][:, col:col + 1], pq(4, 3), ALU.mult, ALU.add, ["S_f", "decl", "pf4"], ["S_f"])
                      k.cp("act", S_bf, S_f, ["S_f"], ["S_bf"])
                      k.act(junk128, pq(5 + d2, 0), AF.Square, ["pf%d" % (5 + d2)], ["junk128", "ssq"], accum=ssq)
                      k.act(ssq, ssq, AF.Ln, ["ssq", "eps_c"], ["ssq"], scale=1.0 / DH, bias=eps_c)
                      k.act(ssq, ssq, AF.Exp, ["ssq"], ["ssq"], scale=-0.5)
                      k.stt(on_bf, pq(5 + d2, 0), ssq, gz[:, i, :], ALU.mult, ALU.mult, ["pf%d" % (5 + d2), "ssq", "gz%d" % i], ["on_bf"])
                      so = 4 + (i % 4)
                      k.tr(pbs(so), on_bf, ident_bf, ["on_bf", "ident_bf"], ["pf7"])
                      k.cp("act", oaT[:, h, i * 128:(i + 1) * 128], pbs(so), ["pf7"], ["oaT%d_%d" % (h, i)])

                  for _ in prep(0):
                      pass
                  NST = 4
                  per = (NST + GRP - 1) // GRP
                  for g in range(NG):
                      gen = prep(g + 1) if g + 1 < NG else iter(())
                      for t in range(GRP):
                          scan_step(g * GRP + t)
                          for _ in range(per):
                              next(gen, None)
                      for _ in gen:
                          pass
                  P.barrier()
                  chk("GDN_h0")

              chk("GDN2")
              P.barrier()
              A.release(m0)
              wf = [A.bf(8 * 384), A.bf(8 * 384)]
              fqT, fkT, fV_f = A.bf(T), A.bf(T), A.bf(T)
              fV = fV_f.rearrange("p (i d) -> p i d", d=128)
              sq = [A.bf(BW), A.bf(BW)]
              lnt = [A.f32(BW), A.f32(BW)]
              rs = [A.f32(BW), A.f32(BW)]
              wff = A.f32(8 * 384)
              PT = [A.bf(BW), A.bf(BW)]
              rden = [A.f32(BW), A.f32(BW)]
              scale = DH ** -0.5
              for h in range(NH):
                  ws = h % 2
                  wfs = wf[ws]
                  wf3 = wfs.rearrange("p (c n) -> p c n", c=8)
                  k.dma("sp", wff, wfox_d[h, :, :], [], ["wff"], "wff")
                  for cc in range(3):
                      k.cp("pool", wfs[:, cc * 1024:(cc + 1) * 1024], wff[:, cc * 1024:(cc + 1) * 1024], ["wff"], ["wf%d" % ws])
                  def fvproj(ws=ws, wf3=wf3):
                      for i in range(NT):
                          q4 = i % 4
                          for c in range(8):
                              k.mm(pq(5, q4), uT[:, c, i * 128:(i + 1) * 128], wf3[:, c, 256:384], c == 0, c == 7,
                                   ["uT%d" % i, "wf%d" % ws], ["pf5"])
                          if q4 == TPB - 1 or i == NT - 1:
                              n4 = q4 + 1
                              i0 = i - q4
                              k.cp("act", fV_f[:, i0 * 128:(i0 + n4) * 128], pf[5][:, 0:n4 * 128], ["pf5"],
                                   ["fV%d" % ii for ii in range(i0, i0 + n4)])
                          yield
                  fvgen = fvproj()
                  for grp in range(2):
                      dstT = (fqT, fkT)[grp]
                      dnm = ("fqT", "fkT")[grp]
                      gcol = (fqg, fkg)[grp]
                      gnm = ("fqg", "fkg")[grp]
                      for b in range(NB):
                          pb = b % 2
                          for c in range(8):
                              k.mm(pf[pb][:, 0:BW], wf3[:, c, grp * 128:(grp + 1) * 128], uT[:, c, b * BW:(b + 1) * BW],
                                   c == 0, c == 7, ["wf%d" % ws] + uTb(b), PF(pb))
                          dw = ["%s%d" % (dnm, i) for i in range(b * TPB, (b + 1) * TPB)]
                          pr = pcnt % 2
                          pcnt += 1
                          fm_norm(pf[pb][:, 0:BW], PF(pb), dstT[:, b * BW:(b + 1) * BW], dw, onesm_bf, ["onesm_bf"],
                                  gcol, [gnm], (2, 7)[pr], pr)
                          next(fvgen, None)
                          next(fvgen, None)
                  for _ in fvgen:
                      pass
                  cnt = 0
                  for Ib in range(NB):
                      bo, bd = ((5, 6), (0, 1))[Ib % 2]
                      last_j = (Ib + 1) * TPB - 1
                      for j in range(last_j + 1):
                          r0 = max(0, j - Ib * TPB)
                          qoff = r0 * 128
                          sb_ = 3 + (cnt % 2)
                          ptb = cnt % 2
                          cnt += 1
                          qr = ["fqT%d" % (Ib * TPB + r) for r in range(r0, TPB)]
                          k.mm(pf[sb_][:, qoff:BW], fkT[:, j * 128:(j + 1) * 128], fqT[:, Ib * BW + qoff:(Ib + 1) * BW], True, True,
                               ["fkT%d" % j] + qr, PF(sb_))
                          for r in range(r0, TPB):
                              i = Ib * TPB + r
                              k.act(PT[ptb][:, r * 128:(r + 1) * 128], pf[sb_][:, r * 128:(r + 1) * 128], AF.Exp,
                                    PF(sb_) + ["btab"], ["PT%d" % ptb], scale=scale, bias=btab[:, h, j, i:i + 1])
                              if i == j:
                                  k.tt("dve", PT[ptb][:, r * 128:(r + 1) * 128], PT[ptb][:, r * 128:(r + 1) * 128], mUi_bf, ALU.mult,
                                       ["PT%d" % ptb, "mUi_bf"], ["PT%d" % ptb])
                          k.mm(pf[bo][:, qoff:BW], fV[:, j, :], PT[ptb][:, qoff:BW], j == 0, j == last_j,
                               ["fV%d" % j, "PT%d" % ptb], PF(bo))
                          k.mm(pf[bd][:, qoff:BW], ones_bf, PT[ptb][:, qoff:BW], j == 0, j == last_j,
                               ["ones_bf", "PT%d" % ptb], PF(bd))
                      rdn = rden[Ib % 2]
                      k.P.op("dve", lambda e, rdn=rdn, bd=bd: e.reciprocal(out=rdn, in_=pf[bd][:, 0:BW]), PF(bd), ["rden%d" % (Ib % 2)])
                      k.tt("dve", obT[:, h, Ib * BW:(Ib + 1) * BW], pf[bo][:, 0:BW], rdn, ALU.mult, PF(bo) + ["rden%d" % (Ib % 2)],
                           ["obT%d_%d" % (h, Ib)])

              if dbg and s == 0:
                  outs.append(k.dma("pool", dbg_d["uT"][:, :], uT_flat, uT_all, [], "dbgp_uT"))
                  outs.append(k.dma("pool", dbg_d["oaT"][:, :], oaT_flat, ["oaT%d_%d" % (h, i) for h in range(NH) for i in range(NT)], [], "dbgp_oaT"))
                  outs.append(k.dma("pool", dbg_d["obT"][:, :], obT_flat, ["obT%d_%d" % (h, b) for h in range(NH) for b in range(NB)], [], "dbgp_obT"))
                  for ti, nm in enumerate(["g", "G", "beta", "kds", "decl", "c"]):
                      outs.append(k.dma("sp", dbg_d["tab"][:, ti * NT8:(ti + 1) * NT8], tabs[nm], [nm], [], "dbg"))

              chk("FOX")
              P.barrier()
              A.release(m0)
              mT_flat = A.bf(8 * T)
              mT = mT_flat.rearrange("p (c t) -> p c t", c=8)
              m_mT = A.mark()
              wga = [A.bf(8 * 256), A.bf(8 * 256)]
              wpa = [A.bf(8 * 256), A.bf(8 * 256)]
              sA, sB, mA, mB = [A.f32(BW) for _ in range(4)]
              wgaf = A.f32(8 * 256)
              wpaf = A.f32(8 * 256)
              oa_all = {b: ["oaT%d_%d" % (h, i) for h in range(NH) for i in range(b * TPB, (b + 1) * TPB)] for b in range(NB)}
              ob_all = {b: ["obT%d_%d" % (h, b) for h in range(NH)] for b in range(NB)}
              for n in range(8):
                  ws = n % 2
                  k.dma("sp", wgaf, wgate_d[n, :, :], [], ["wgaf"], "wgaf")
                  k.dma("sp", wpaf, wpab_d[n, :, :], [], ["wpaf"], "wpaf")
                  for cc in range(2):
                      k.cp("pool", wga[ws][:, cc * 1024:(cc + 1) * 1024], wgaf[:, cc * 1024:(cc + 1) * 1024], ["wgaf"], ["wga%d" % ws])
                      k.cp("pool", wpa[ws][:, cc * 1024:(cc + 1) * 1024], wpaf[:, cc * 1024:(cc + 1) * 1024], ["wpaf"], ["wpa%d" % ws])
                  wga3 = wga[ws].rearrange("p (c n) -> p c n", c=8)
                  wpa3 = wpa[ws].rearrange("p (c n) -> p c n", c=8)
                  for b in range(NB):
                      bs = slice(b * BW, (b + 1) * BW)
                      o4 = 4 * ((n * NB + b) % 2)
                      for c in range(8):
                          k.mm(pf[o4 + 0][:, 0:BW], wpa3[:, c, 0:128], oaT[:, c, bs], c == 0, c == 7, ["wpa%d" % ws] + oa_all[b], PF(o4 + 0))
                      for c in range(8):
                          k.mm(pf[o4 + 1][:, 0:BW], wga3[:, c, 0:128], uT[:, c, bs], c == 0, c == 7, ["wga%d" % ws] + uTb(b), PF(o4 + 1))
                      for c in range(8):
                          k.mm(pf[o4 + 2][:, 0:BW], wpa3[:, c, 128:256], obT[:, c, bs], c == 0, c == 7, ["wpa%d" % ws] + ob_all[b], PF(o4 + 2))
                      for c in range(8):
                          k.mm(pf[o4 + 3][:, 0:BW], wga3[:, c, 128:256], uT[:, c, bs], c == 0, c == 7, ["wga%d" % ws] + uTb(b), PF(o4 + 3))
                      k.act(sA, pf[o4 + 1][:, 0:BW], AF.Sigmoid, PF(o4 + 1), ["sA"])
                      k.act(sB, pf[o4 + 3][:, 0:BW], AF.Sigmoid, PF(o4 + 3), ["sB"])
                      k.tt("dve", mA, pf[o4 + 0][:, 0:BW], sA, ALU.mult, PF(o4 + 0) + ["sA"], ["mA"])
                      k.tt("dve", mB, pf[o4 + 2][:, 0:BW], sB, ALU.mult, PF(o4 + 2) + ["sB"], ["mB"])
                      k.tt("dve", mT[:, n, bs], mA, mB, ALU.add, ["mA", "mB"], ["mT%d_%d" % (n, b)])

              P.barrier()
              A.release(m_mT)
              wo = A.bf(8 * 1024)
              wo3 = wo.rearrange("p (c n) -> p c n", c=8)
              xt = [A.f32(D), A.f32(D)]
              hnb = [A.bf(D), A.bf(D)]
              junk = A.bf(D)
              ssc = [A.f32(1), A.f32(1)]
              k.dma("pool", wo, wout_d[:, :], [], ["wo"], "wo")
              for i in range(NT):
                  sl = i % 2
                  k.dma("sp", xt[sl], x_d[s, i * 128:(i + 1) * 128, :], [], ["xt%d" % sl], "xt%d" % sl)
                  for half in range(2):
                      for c in range(8):
                          k.mm(pf[half][:, :], mT[:, c, i * 128:(i + 1) * 128], wo3[:, c, half * 512:(half + 1) * 512], c == 0, c == 7,
                               ["wo"] + ["mT%d_%d" % (n_, i // TPB) for n_ in range(8)], PF(half))
                      k.tt("dve", h3[:, i, half * 512:(half + 1) * 512], pf[half][:, :], xt[sl][:, half * 512:(half + 1) * 512], ALU.add,
                           PF(half) + ["xt%d" % sl], ["h%d_%d" % (i, half)])
                  hr = ["h%d_0" % i, "h%d_1" % i]
                  k.act(junk, h3[:, i, :], AF.Square, hr, ["junk", "ss%d" % sl], accum=ssc[sl])
                  k.act(ssc[sl], ssc[sl], AF.Ln, ["ss%d" % sl, "eps_c"], ["ss%d" % sl], scale=1.0 / D, bias=eps_c)
                  k.act(ssc[sl], ssc[sl], AF.Exp, ["ss%d" % sl], ["ss%d" % sl], scale=-0.5)
                  k.stt(hnb[sl], h3[:, i, :], ssc[sl], gmlp_bc, ALU.mult, ALU.mult, hr + ["ss%d" % sl, "gmlp_bc"], ["hnb%d" % sl])
                  for c in range(8):
                      k.tr(pbs(c), hnb[sl][:, c * 128:(c + 1) * 128], ident_bf, ["hnb%d" % sl, "ident_bf"], ["pf7"])
                  k.cp("act", hnT[:, :, i * 128:(i + 1) * 128], pbf.rearrange("p (c t) -> p c t", c=8),
                       ["pf7"], ["hnT%d" % i])
              if dbg and s == 0:
                  outs.append(k.dma("sp", dbg_d["h"][:, :], h_flat, ["h%d_%d" % (i, hf) for i in range(NT) for hf in range(2)], [], "dbg"))

              chk("POSTH")
              P.barrier()
              A.release(m0)
              aT_flat = A.bf(32 * BW)
              aT = aT_flat.rearrange("p (f t) -> p f t", f=32)
              wup = [A.bf(1024) for _ in range(4)]
              wdn = [A.bf(1024) for _ in range(4)]
              wstg = [A.f32(1024) for _ in range(3)]
              scnt = 0
              rtmp = [A.f32(BW), A.f32(BW)]
              otile = [A.f32(D), A.f32(D)]
              for b in range(NB):
                  bs = slice(b * BW, (b + 1) * BW)
                  hb = ["hnT%d" % i for i in range(b * TPB, (b + 1) * TPB)]
                  for f in range(32):
                      ws = f % 4
                      sg = scnt % 3
                      scnt += 1
                      k.dma("sp", wstg[sg], wup_d[f, :, :], [], ["wstg%d" % sg], "wstg%d" % sg)
                      k.cp("pool", wup[ws], wstg[sg], ["wstg%d" % sg], ["wup%d" % ws])
                      wu3 = wup[ws].rearrange("p (c n) -> p c n", c=8)
                      pb = f % 2
                      for c in range(8):
                          k.mm(pf[pb][:, 0:BW], wu3[:, c, :], hnT[:, c, bs], c == 0, c == 7, ["wup%d" % ws] + hb, PF(pb))
                      k.act(rtmp[pb], pf[pb][:, 0:BW], AF.Relu, PF(pb), ["rtmp%d" % pb])
                      k.tt("dve", aT[:, f, :], rtmp[pb], rtmp[pb], ALU.mult, ["rtmp%d" % pb], ["aT%d" % f])
                  for f in range(32):
                      ws = f % 4
                      sg = scnt % 3
                      scnt += 1
                      k.dma("sp", wstg[sg], wdn_d[f, :, :], [], ["wstg%d" % sg], "wstg%d" % sg)
                      k.cp("pool", wdn[ws], wstg[sg], ["wstg%d" % sg], ["wdn%d" % ws])
                      for t in range(TPB):
                          for half in range(2):
                              bk = t * 2 + half
                              k.mm(pf[bk][:, :], aT[:, f, t * 128:(t + 1) * 128], wdn[ws][:, half * 512:(half + 1) * 512], f == 0, f == 31,
                                   ["aT%d" % f, "wdn%d" % ws], PF(bk))
                  for t in range(TPB):
                      i = b * TPB + t
                      sl = i % 2
                      for half in range(2):
                          bk = t * 2 + half
                          k.tt("dve", otile[sl][:, half * 512:(half + 1) * 512], pf[bk][:, :], h3[:, i, half * 512:(half + 1) * 512], ALU.add,
                               PF(bk) + ["h%d_%d" % (i, half)], ["ot%d_%d" % (sl, half)])
                      outs.append(k.dma("sp", out_d[s, i * 128:(i + 1) * 128, :], otile[sl], ["ot%d_0" % sl, "ot%d_1" % sl], [], "out%d" % sl))
              P.barrier()

          except StopBuild:
            outs.append(k.dma("pool", dbg_d["uT"][:, :], uT_flat, [], [], "dbgp_stop"))
            break
        P.final_waits += outs
        nw = P.emit(st)
        print("ops", len(P.ops), "waits", nw, "arena peak", A.peak)
    return nc


def prep_weights(inp, T):
    NT = T // 128
    f = lambda a: np.ascontiguousarray(np.asarray(a, dtype=np.float32))
    w_in = np.asarray(inp["w_in"])[0]
    offs = np.cumsum([0, 1024, 1024, 1024, 1024, 8, 8, 1024, 1024, 1024, 8, 1024, 1024])
    gq, gk, gv, gz, ga, gb, fq, fk, fv, ff, gta, gtb = [w_in[:, offs[i]:offs[i + 1]] for i in range(12)]

    def pc(w):
        return w.reshape(8, 128, -1).transpose(1, 0, 2)
    w_gdn = np.stack([pc(np.concatenate([g_[:, h * 128:(h + 1) * 128] for g_ in (gq, gk, gv, gz)], 1)).reshape(128, -1) for h in range(8)])
    w_fox = np.stack([pc(np.concatenate([g_[:, h * 128:(h + 1) * 128] for g_ in (fq, fk, fv)], 1)).reshape(128, -1) for h in range(8)])
    w_small = pc(np.concatenate([ga, gb, ff], 1)).reshape(128, -1)
    w_gate = np.stack([pc(np.concatenate([gta[:, n * 128:(n + 1) * 128], gtb[:, n * 128:(n + 1) * 128]], 1)).reshape(128, -1) for n in range(8)])
    pa = np.asarray(inp["w_proj_gdn"])[0]
    pb = np.asarray(inp["w_proj_fox"])[0]
    w_pab = np.stack([pc(np.concatenate([pa[:, n * 128:(n + 1) * 128], pb[:, n * 128:(n + 1) * 128]], 1)).reshape(128, -1) for n in range(8)])
    w_o = pc(np.asarray(inp["w_out"])[0]).reshape(128, -1)
    wu = np.asarray(inp["w_up"])[0]
    w_up = np.stack([pc(wu[:, fch * 128:(fch + 1) * 128]).reshape(128, -1) for fch in range(32)])
    w_down = np.asarray(inp["w_down"])[0].reshape(32, 128, 1024)
    cwt = np.asarray(inp["gdn_conv_w"])[0]
    convw = cwt.reshape(4, 24, 128).transpose(2, 1, 0).reshape(128, 96)
    d = {
        "w_gdn": w_gdn, "w_fox": w_fox, "w_small": w_small, "w_gate": w_gate, "w_pab": w_pab, "w_o": w_o,
        "w_up": w_up, "w_down": w_down, "convw": convw,
        "gmix": np.asarray(inp["norm_mix_g"])[0][None, :], "gmlp": np.asarray(inp["norm_mlp_g"])[0][None, :],
        "gnorm4": np.tile(np.asarray(inp["gdn_norm_g"])[0], 4)[None, :],
        "alog_t": np.tile(np.asarray(inp["gdn_a_log"])[0], NT)[None, :],
        "dtb_t": np.tile(np.asarray(inp["gdn_dt_bias"])[0], NT)[None, :],
        "fb_t": np.tile(np.asarray(inp["fox_f_bias"])[0], NT)[None, :],
        "fqg": np.asarray(inp["fox_q_norm_g"])[0][:, None], "fkg": np.asarray(inp["fox_k_norm_g"])[0][:, None],
    }
    return {k_: f(v) for k_, v in d.items()}


_NC_CACHE = {}


def kernel(**inputs):
    x = np.asarray(inputs["x"], dtype=np.float32)
    Bt, T, _ = x.shape
    ncores = 8
    nseq = Bt // ncores
    wd = prep_weights(inputs, T)
    key = (T, nseq)
    if key not in _NC_CACHE:
        nc = bass.Bass("TRN2", target_bir_lowering=False)
        build(nc, T, nseq)
        _NC_CACHE[key] = nc
    nc = _NC_CACHE[key]
    in_maps = []
    for c in range(ncores):
        m = dict(wd)
        m["x"] = np.ascontiguousarray(x[c * nseq:(c + 1) * nseq])
        in_maps.append(m)
    res = run_bass_kernel_spmd(nc, in_maps, core_ids=list(range(ncores)))
    out = np.concatenate([res.results[c]["out"] for c in range(ncores)], axis=0)
    return out.astype(np.float32)
```

```python
import contextlib
import concourse.bass as bass
import concourse.mybir as mybir

F32 = mybir.dt.float32
BF16 = mybir.dt.bfloat16
AF = mybir.ActivationFunctionType
ALU = mybir.AluOpType
AX = mybir.AxisListType

ENGS = ("pe", "act", "dve", "pool", "sp")


class Op:
    __slots__ = ("eng", "fn", "deps", "dma", "idx", "sig", "signal")

    def __init__(self, eng, fn, dma):
        self.eng = eng
        self.fn = fn
        self.deps = []
        self.dma = dma
        self.sig = None
        self.signal = False


class Prog:
    def __init__(self, nc):
        self.nc = nc
        self.ops = []
        self.last_w = {}
        self.readers = {}
        self.final_waits = []

    def op(self, eng, fn, reads=(), writes=(), dma=None):
        o = Op(eng, fn, dma)
        o.idx = len(self.ops)
        pfr = [r for r in reads if r.startswith("pf")]
        if pfr:
            reads = [r for r in reads if not r.startswith("pf")]
            writes = list(writes) + [r for r in pfr if r not in writes]
        deps = set()
        for r in reads:
            w = self.last_w.get(r)
            if w is not None:
                deps.add(w)
        for wkey in writes:
            w = self.last_w.get(wkey)
            if w is not None:
                deps.add(w)
            for rd in self.readers.get(wkey, ()):
                deps.add(rd)
        for d in deps:
            if d is o:
                continue
            if d.dma is None and d.eng == eng:
                if eng == "pe":
                    continue
            o.deps.append(d)
        for r in reads:
            self.readers.setdefault(r, []).append(o)
        for wkey in writes:
            self.last_w[wkey] = o
            self.readers[wkey] = []
        self.ops.append(o)
        return o

    def barrier(self):
        last = {}
        for o in self.ops:
            if o.fn is None:
                continue
            k = ("dma", o.dma) if o.dma is not None else ("eng", o.eng)
            last[k] = o
        for e in ENGS:
            b = Op(e, None, None)
            b.idx = len(self.ops)
            b.deps = [d for k, d in last.items() if not (k == ("eng", e))]
            self.ops.append(b)
        self.last_w = {}
        self.readers = {}

    def emit(self, stack):
        nc = self.nc
        for o in self.ops:
            for d in o.deps:
                d.signal = True
        for o in self.final_waits:
            o.signal = True
        cnt = {}
        for o in self.ops:
            if o.dma is not None:
                k = ("dma", o.dma)
                cnt[k] = cnt.get(k, 0) + 16
                o.sig = (k, cnt[k])
            elif o.signal:
                k = ("eng", o.eng)
                cnt[k] = cnt.get(k, 0) + 1
                o.sig = (k, cnt[k])
        sems = {}
        for k in cnt:
            nm = "s_" + "_".join(str(x) for x in k)
            sems[k] = stack.enter_context(nc.semaphore(nm))
        per_eng = {e: [o for o in self.ops if o.eng == e] for e in ENGS}
        finals = list(self.final_waits)
        block = stack.enter_context(nc.Block())
        nwaits = [0]

        def run(eng_name, eobj):
            waited = {}
            for o in per_eng[eng_name]:
                need = {}
                for d in o.deps:
                    k, v = d.sig
                    if waited.get(k, 0) >= v:
                        continue
                    if need.get(k, 0) < v:
                        need[k] = v
                for k, v in need.items():
                    eobj.wait_ge(sems[k], v)
                    waited[k] = v
                    nwaits[0] += 1
                if o.fn is None:
                    continue
                ins = o.fn(eobj)
                if o.dma is not None:
                    ins.then_inc(sems[o.sig[0]], 16)
                elif o.signal:
                    ins.then_inc(sems[o.sig[0]], 1)
            if eng_name == "sp":
                need = {}
                for d in finals:
                    k, v = d.sig
                    if need.get(k, 0) < v:
                        need[k] = v
                for k, v in need.items():
                    if waited.get(k, 0) < v:
                        eobj.wait_ge(sems[k], v)

        @block.tensor
        def _(e):
            run("pe", e)

        @block.scalar
        def _(e):
            run("act", e)

        @block.vector
        def _(e):
            run("dve", e)

        @block.gpsimd
        def _(e):
            run("pool", e)

        @block.sync
        def _(e):
            run("sp", e)
        return nwaits[0]

import numpy as np
import os
from concourse.bass_utils import run_bass_kernel_spmd

D = 1024
NH = 8
DH = 128
DFF = 4096
EPS = 1e-6
ARENA = 105600


class StopBuild(Exception):
    pass


class Arena:
    def __init__(self, t):
        self.t = t
        self.off = 0
        self.peak = 0

    def bf(self, n):
        n2 = (n + 15) // 16 * 16
        o = self.off
        self.off += n2
        self.peak = max(self.peak, self.off)
        assert self.off <= ARENA, ("arena overflow", self.off)
        return self.t[:, o:o + n]

    def f32(self, n):
        n2 = (2 * n + 15) // 16 * 16
        o = self.off
        self.off += n2
        self.peak = max(self.peak, self.off)
        assert self.off <= ARENA, ("arena overflow", self.off)
        return self.t[:, o:o + 2 * n].bitcast(F32)

    def mark(self):
        return self.off

    def release(self, m):
        self.off = m


class K:
    def __init__(self, P):
        self.P = P

    def mm(self, out, lhsT, rhs, start, stop, r, w):
        return self.P.op("pe", lambda e: e.matmul(out, lhsT=lhsT, rhs=rhs, start=start, stop=stop), r, w)

    def tr(self, out, in_, ident, r, w):
        return self.P.op("pe", lambda e: e.transpose(out=out, in_=in_, identity=ident), r, w)

    def act(self, out, in_, func, r, w, scale=None, bias=None, accum=None):
        kw = {}
        if scale is not None:
            kw["scale"] = scale
        if bias is not None:
            kw["bias"] = bias
        if accum is not None:
            kw["accum_out"] = accum
        return self.P.op("act", lambda e: e.activation(out=out, in_=in_, func=func, **kw), r, w)

    def cp(self, eng, out, in_, r, w):
        if eng == "act":
            return self.P.op("act", lambda e: e.copy(out=out, in_=in_), r, w)
        return self.P.op(eng, lambda e: e.tensor_copy(out=out, in_=in_), r, w)

    def tt(self, eng, out, in0, in1, op, r, w):
        return self.P.op(eng, lambda e: e.tensor_tensor(out=out, in0=in0, in1=in1, op=op), r, w)

    def ts(self, eng, out, in0, s1, op0, r, w, s2=None, op1=None):
        if op1 is None:
            return self.P.op(eng, lambda e: e.tensor_scalar(out=out, in0=in0, scalar1=s1, scalar2=None, op0=op0), r, w)
        return self.P.op(eng, lambda e: e.tensor_scalar(out=out, in0=in0, scalar1=s1, scalar2=s2, op0=op0, op1=op1), r, w)

    def stt(self, out, in0, scalar, in1, op0, op1, r, w):
        return self.P.op("dve", lambda e: e.scalar_tensor_tensor(out=out, in0=in0, scalar=scalar, in1=in1, op0=op0, op1=op1), r, w)

    def memset(self, eng, ap, val, w):
        return self.P.op(eng, lambda e: e.memset(ap, val), (), w)

    def dma(self, eng, out, in_, r, w, key):
        return self.P.op(eng, lambda e: e.dma_start(out=out, in_=in_), r, w, dma=key)


def build(nc, T, NSEQ, dbg=False, stop=None):
    NT = T // 128
    BW = min(512, T)
    NB = T // BW
    TPB = BW // 128
    NT8 = NT * 8
    dram = lambda n, s, kind="ExternalInput": nc.dram_tensor(n, list(s), F32, kind=kind).ap()
    x_d = dram("x", [NSEQ, T, D])
    out_d = dram("out", [NSEQ, T, D], "ExternalOutput")
    wgdn_d = dram("w_gdn", [NH, 128, 8 * 512])
    wfox_d = dram("w_fox", [NH, 128, 8 * 384])
    wsm_d = dram("w_small", [128, 8 * 24])
    wgate_d = dram("w_gate", [8, 128, 8 * 256])
    wpab_d = dram("w_pab", [8, 128, 8 * 256])
    wout_d = dram("w_o", [128, 8 * 1024])
    wup_d = dram("w_up", [32, 128, 8 * 128])
    wdn_d = dram("w_down", [32, 128, 1024])
    convw_d = dram("convw", [128, 24 * 4])
    gmix_d = dram("gmix", [1, D])
    gmlp_d = dram("gmlp", [1, D])
    gnorm4_d = dram("gnorm4", [1, 512])
    alog_d = dram("alog_t", [1, NT8])
    dtb_d = dram("dtb_t", [1, NT8])
    fb_d = dram("fb_t", [1, NT8])
    fqg_d = dram("fqg", [128, 1])
    fkg_d = dram("fkg", [128, 1])
    dbg_d = {}
    if dbg:
        dbg_d["uT"] = dram("d_uT", [128, 8 * T], "ExternalOutput")
        dbg_d["oaT"] = dram("d_oaT", [128, 8 * T], "ExternalOutput")
        dbg_d["obT"] = dram("d_obT", [128, 8 * T], "ExternalOutput")
        dbg_d["tab"] = dram("d_tab", [128, 6 * NT8], "ExternalOutput")
        dbg_d["h"] = dram("d_h", [128, NT * D], "ExternalOutput")

    with contextlib.ExitStack() as st:
        P = Prog(nc)
        k = K(P)
        arena_t = st.enter_context(nc.sbuf_tensor("arena", [128, ARENA], BF16))
        A = Arena(arena_t)
        pf = [st.enter_context(nc.psum_tensor("pf%d" % i, [128, 512], F32)) for i in range(8)]
        pbf = pf[7][:, :].bitcast(BF16)

        def PF(b, c0=0, c1=512):
            return ["pf%d" % b]

        def pq(b, q, n=128):
            return pf[b][:, q * 128:q * 128 + n]

        def pbs(s):
            return pbf[:, s * 128:(s + 1) * 128]

        ident_bf = A.bf(128)
        ones_bf = A.bf(128)
        onesm_bf = A.bf(128)
        mUi_bf = A.bf(128)
        ident_f = A.f32(128)
        ones_f = A.f32(128)
        triU_f = A.f32(128)
        mUs_bd = A.f32(128)
        mLs_bd = A.f32(128)
        mL_off = A.f32(128)
        gmix_bc = A.f32(D)
        gmlp_bc = A.f32(D)
        gnorm4 = A.f32(512)
        alog_bc = A.f32(NT8)
        dtb_bc = A.f32(NT8)
        fb_bc = A.f32(NT8)
        cw = A.f32(96)
        fqg = A.f32(1)
        fkg = A.f32(1)
        eps_c = A.f32(1)
        one_c = A.f32(1)

        k.memset("pool", ones_bf, 1.0, ["ones_bf"])
        k.memset("pool", onesm_bf, 1.0 / 128, ["onesm_bf"])
        k.memset("pool", ones_f, 1.0, ["ones_f"])
        k.memset("pool", eps_c, EPS, ["eps_c"])
        k.memset("pool", one_c, 1.0, ["one_c"])

        def asel(out, in_, step, cm, cmp, r, w):
            P.op("pool", lambda e: e.affine_select(out=out, in_=in_, pattern=[[step, 128]], compare_op=cmp, fill=0.0,
                                                   base=0, channel_multiplier=cm), r, w)
        asel(ident_bf, ones_bf, 1, -1, ALU.is_equal, ["ones_bf"], ["ident_bf"])
        asel(ident_f, ones_f, 1, -1, ALU.is_equal, ["ones_f"], ["ident_f"])
        asel(mUi_bf, ones_bf, 1, -1, ALU.is_ge, ["ones_bf"], ["mUi_bf"])
        asel(triU_f, ones_f, 1, -1, ALU.is_ge, ["ones_f"], ["triU_f"])
        asel(mUs_bd, ones_f, 1, -1, ALU.is_gt, ["ones_f"], ["mUs_bd"])
        asel(mLs_bd, ones_f, -1, 1, ALU.is_gt, ["ones_f"], ["mLs_bd"])
        k.memset("pool", mUs_bd[0:64, 64:128], 0.0, ["mUs_bd"])
        k.memset("pool", mLs_bd[64:128, 0:64], 0.0, ["mLs_bd"])
        k.memset("pool", mL_off, 0.0, ["mL_off"])
        k.memset("pool", mL_off[64:128, 0:64], 1.0, ["mL_off"])
        k.dma("sp", gmix_bc, gmix_d[0:1, :].partition_broadcast(128), [], ["gmix_bc"], "cst1")
        k.dma("sp", gmlp_bc, gmlp_d[0:1, :].partition_broadcast(128), [], ["gmlp_bc"], "cst2")
        k.dma("sp", gnorm4, gnorm4_d[0:1, :].partition_broadcast(128), [], ["gnorm4"], "cst3")
        k.dma("sp", alog_bc, alog_d[0:1, :].partition_broadcast(128), [], ["alog_bc"], "cst4")
        k.dma("sp", dtb_bc, dtb_d[0:1, :].partition_broadcast(128), [], ["dtb_bc"], "cst5")
        k.dma("sp", fb_bc, fb_d[0:1, :].partition_broadcast(128), [], ["fb_bc"], "cst6")
        k.dma("sp", cw, convw_d[:, :], [], ["cw"], "cst7")
        k.dma("sp", fqg, fqg_d[:, :], [], ["fqg"], "cst8")
        k.dma("sp", fkg, fkg_d[:, :], [], ["fkg"], "cst9")

        a1 = A.mark()
        uT_flat = A.bf(8 * T)
        oaT_flat = A.bf(8 * T)
        obT_flat = A.bf(8 * T)
        a1_end = A.mark()
        uT = uT_flat.rearrange("p (c t) -> p c t", c=8)
        oaT = oaT_flat.rearrange("p (c t) -> p c t", c=8)
        obT = obT_flat.rearrange("p (c t) -> p c t", c=8)
        A.release(a1)
        h_flat = A.f32(NT * D)
        hnT_flat = A.bf(8 * T)
        assert A.mark() <= a1_end
        A.release(a1_end)
        h3 = h_flat.rearrange("p (i d) -> p i d", i=NT)
        hnT = hnT_flat.rearrange("p (c t) -> p c t", c=8)

        tabs = {}
        for nm in ["g", "G", "beta", "nbeta", "kds", "decl", "lf", "c", "tot", "cpre", "cend", "ty", "ta", "tb"]:
            tabs[nm] = A.f32(NT8)
        btab_flat = A.f32(8 * NT * NT)
        btab = btab_flat.rearrange("p (h j i) -> p h j i", h=8, j=NT)
        m0 = A.mark()

        def t3(ap):
            return ap.rearrange("p (i h) -> p i h", h=8)

        outs = []

        def chk(name):
            if stop == name:
                raise StopBuild()

        for s in range(NSEQ):
          try:
              A.release(m0)
              xt = [A.f32(D), A.f32(D)]
              junk = A.bf(D)
              ubf = [A.bf(D), A.bf(D)]
              ssc = [A.f32(1), A.f32(1)]
              for i in range(NT):
                  sl = i % 2
                  k.dma("sp", xt[sl], x_d[s, i * 128:(i + 1) * 128, :], [], ["xt%d" % sl], "xt%d" % sl)
                  k.act(junk, xt[sl], AF.Square, ["xt%d" % sl], ["junk", "ss%d" % sl], accum=ssc[sl])
                  k.act(ssc[sl], ssc[sl], AF.Ln, ["ss%d" % sl, "eps_c"], ["ss%d" % sl], scale=1.0 / D, bias=eps_c)
                  k.act(ssc[sl], ssc[sl], AF.Exp, ["ss%d" % sl], ["ss%d" % sl], scale=-0.5)
                  k.stt(ubf[sl], xt[sl], ssc[sl], gmix_bc, ALU.mult, ALU.mult, ["xt%d" % sl, "ss%d" % sl, "gmix_bc"], ["ubf%d" % sl])
                  for c in range(8):
                      k.tr(pbs(c), ubf[sl][:, c * 128:(c + 1) * 128], ident_bf, ["ubf%d" % sl, "ident_bf"], ["pf7"])
                  k.cp("act", uT[:, :, i * 128:(i + 1) * 128], pbf.rearrange("p (c t) -> p c t", c=8),
                       ["pf7"], ["uT%d" % i])
              uT_all = ["uT%d" % i for i in range(NT)]

              def uTb(b):
                  return ["uT%d" % i for i in range(b * TPB, (b + 1) * TPB)]

              if stop == "P1":
                  outs.append(k.dma("pool", dbg_d["uT"][:, :], uT_flat, uT_all, [], "dbgp_uT"))
                  break
              P.barrier()
              A.release(m0)
              wsm = A.bf(8 * 24)
              wsm3 = wsm.rearrange("p (c n) -> p c n", c=8)
              k.dma("pool", wsm, wsm_d[:, :], [], ["wsm"], "wsm")
              psS = pf[0][:, 0:NT * 24]
              for i in range(NT):
                  for c in range(8):
                      k.mm(pf[0][:, i * 24:(i + 1) * 24], uT[:, c, i * 128:(i + 1) * 128], wsm3[:, c, :], c == 0, c == 7,
                           ["uT%d" % i, "wsm"], PF(0))
              ps3 = psS.rearrange("p (i n) -> p i n", n=24)
              ga, gb, ff = ps3[:, :, 0:8], ps3[:, :, 8:16], ps3[:, :, 16:24]
              ty, ta, tb = tabs["ty"], tabs["ta"], tabs["tb"]
              k.tt("dve", t3(ty), ga, t3(dtb_bc), ALU.add, PF(0) + ["dtb_bc"], ["ty"])
              k.stt(ta, ty, -1.0, ty, ALU.mult, ALU.max, ["ty"], ["ta"])
              k.act(ta, ta, AF.Exp, ["ta"], ["ta"], scale=-1.0)
              k.act(ta, ta, AF.Ln, ["ta", "one_c"], ["ta"], bias=one_c)
              k.stt(ty, ty, 0.0, ta, ALU.max, ALU.add, ["ty", "ta"], ["ty"])
              k.act(tb, alog_bc, AF.Exp, ["alog_bc"], ["tb"])
              k.stt(tabs["g"], ty, -1.0, tb, ALU.mult, ALU.mult, ["ty", "tb"], ["g"])
              k.act(t3(tabs["beta"]), gb, AF.Sigmoid, PF(0), ["beta"])
              k.ts("dve", tabs["nbeta"], tabs["beta"], -1.0, ALU.mult, ["beta"], ["nbeta"])
              k.tt("dve", t3(ty), ff, t3(fb_bc), ALU.add, PF(0) + ["fb_bc"], ["ty"])
              k.stt(ta, ty, -1.0, ty, ALU.mult, ALU.max, ["ty"], ["ta"])
              k.act(ta, ta, AF.Exp, ["ta"], ["ta"], scale=-1.0)
              k.act(ta, ta, AF.Ln, ["ta", "one_c"], ["ta"], bias=one_c)
              k.stt(tabs["lf"], ty, 0.0, ta, ALU.min, ALU.subtract, ["ty", "ta"], ["lf"])
              k.mm(pf[1][:, 0:NT8], triU_f, tabs["g"], True, True, ["triU_f", "g"], ["pf1"])
              k.mm(pf[1][:, 128:128 + NT8], ones_f, tabs["g"], True, True, ["ones_f", "g"], ["pf1"])
              k.mm(pf[1][:, 256:256 + NT8], triU_f, tabs["lf"], True, True, ["triU_f", "lf"], ["pf1"])
              k.mm(pf[1][:, 384:384 + NT8], ones_f, tabs["lf"], True, True, ["ones_f", "lf"], ["pf1"])
              k.cp("act", tabs["G"], pf[1][:, 0:NT8], ["pf1"], ["G"])
              k.act(tabs["decl"], pf[1][:, 128:128 + NT8], AF.Exp, ["pf1"], ["decl"])
              k.tt("dve", ty, pf[1][:, 128:128 + NT8], tabs["G"], ALU.subtract, ["pf1", "G"], ["ty"])
              k.act(tabs["kds"], ty, AF.Exp, ["ty"], ["kds"])
              k.cp("act", tabs["tot"], pf[1][:, 384:384 + NT8], ["pf1"], ["tot"])
              k.memset("dve", tabs["cpre"][:, 0:8], 0.0, ["cpre"])
              for i in range(1, NT):
                  k.tt("dve", tabs["cpre"][:, i * 8:(i + 1) * 8], tabs["cpre"][:, (i - 1) * 8:i * 8],
                       tabs["tot"][:, (i - 1) * 8:i * 8], ALU.add, ["cpre", "tot"], ["cpre"])
              k.tt("dve", tabs["c"], pf[1][:, 256:256 + NT8], tabs["cpre"], ALU.add, ["pf1", "cpre"], ["c"])
              k.tt("dve", tabs["cend"], tabs["cpre"], tabs["tot"], ALU.add, ["cpre", "tot"], ["cend"])
              cend3 = t3(tabs["cend"])
              n_ = 0
              for h in range(NH):
                  for j in range(NT):
                      eng = "dve" if n_ % 2 == 0 else "pool"
                      n_ += 1
                      col = j * 8 + h
                      k.ts(eng, btab[:, h, j, :], cend3[:, :, h], tabs["c"][:, col:col + 1], ALU.subtract, ["cend", "c"], ["btab"])

              if stop == "SMALL":
                  outs.append(k.dma("pool", dbg_d["uT"][:, :], uT_flat, uT_all, [], "dbgp_uT"))
                  for ti, nm in enumerate(["g", "G", "beta", "kds", "decl", "c"]):
                      outs.append(k.dma("sp", dbg_d["tab"][:, ti * NT8:(ti + 1) * NT8], tabs[nm], [nm], [], "dbg"))
                  break
              P.barrier()
              A.release(m0)
              wg = [A.bf(8 * 512)]
              GRP = min(NT, int(os.environ.get("GDN_GRP", "4")))
              GW = GRP * 128
              NG = NT // GRP
              qT, kT, vT = A.bf(T), A.bf(T), A.bf(T)
              Kd_f, V_f, gz_f = A.bf(T), A.f32(T), A.f32(T)
              Kd = Kd_f.rearrange("p (i d) -> p i d", d=128)
              Vt = V_f.rearrange("p (i d) -> p i d", d=128)
              gz = gz_f.rearrange("p (i d) -> p i d", d=128)
              S_f = A.f32(128)
              S_bf, R_bf, vnew, on_bf, junk128 = [A.bf(128) for _ in range(5)]
              ssq = A.f32(1)
              mp = A.mark()
              sq = [A.bf(BW), A.bf(BW)]
              cb = [A.f32(BW + 3), A.f32(BW + 3)]
              acc = [A.f32(BW), A.f32(BW)]
              lnt = [A.f32(BW), A.f32(BW)]
              rs = [A.f32(BW), A.f32(BW)]
              tmpz = A.f32(BW)
              A.release(mp)
              grep_ = [A.f32(128), A.f32(128)]
              brep_ = [A.f32(128), A.f32(128)]
              E_, Y_, M1_, DuI, DuB, DlB, DlO, bbc = [A.f32(GW) for _ in range(8)]
              eG = A.bf(GW)
              Qb = [A.bf(GW), A.bf(GW)]
              Pb = [A.bf(GW), A.bf(GW)]
              Mb = [A.bf(GW), A.bf(GW)]
              Ao, Td, X1 = [A.bf(GW) for _ in range(3)]
              AttnT = [A.bf(GW), A.bf(GW)]
              QeT = [A.bf(GW), A.bf(GW)]
              KeT = [A.bf(GW), A.bf(GW)]
              TpT = [A.bf(GW), A.bf(GW)]

              def fm_norm(src, src_r, dst, dst_w, ones_t, ones_r, mult, mult_r, pbank, par):
                  k.act(sq[par], src, AF.Square, src_r, ["sq%d" % par])
                  k.mm(pf[pbank][:, 0:BW], ones_t, sq[par], True, True, ["sq%d" % par] + ones_r, PF(pbank))
                  k.act(lnt[par], pf[pbank][:, 0:BW], AF.Ln, PF(pbank) + ["eps_c"], ["lnt%d" % par], bias=eps_c)
                  k.act(rs[par], lnt[par], AF.Exp, ["lnt%d" % par], ["rs%d" % par], scale=-0.5)
                  k.stt(dst, src, mult, rs[par], ALU.mult, ALU.mult, src_r + ["rs%d" % par] + mult_r, dst_w)

              pcnt = 0
              for h in range(NH):
                  ws = 0
                  wgs = wg[ws]
                  wg3 = wgs.rearrange("p (c n) -> p c n", c=8)
                  if h == 0:
                      k.dma("pool", wgs, wgdn_d[h, :, :], [], ["wg%d" % ws], "wg%d" % ws)
                  def zgate(ws=ws, wg3=wg3):
                      for i in range(NT):
                          q4 = i % 4
                          for c in range(8):
                              k.mm(pq(4, q4), uT[:, c, i * 128:(i + 1) * 128], wg3[:, c, 384:512], c == 0, c == 7,
                                   ["uT%d" % i, "wg%d" % ws], ["pf4"])
                          if q4 == TPB - 1 or i == NT - 1:
                              n4 = q4 + 1
                              i0 = i - q4
                              k.act(tmpz[:, 0:n4 * 128], pf[4][:, 0:n4 * 128], AF.Silu, ["pf4"], ["tmpz"])
                              k.tt("dve", gz_f[:, i0 * 128:(i0 + n4) * 128], tmpz[:, 0:n4 * 128], gnorm4[:, 0:n4 * 128], ALU.mult,
                                   ["tmpz", "gnorm4"], ["gz%d" % ii for ii in range(i0, i0 + n4)])
                          yield
                  zgen = zgate()
                  for grp in range(3):
                      chunk = grp * 8 + h
                      dstT = (qT, kT, vT)[grp]
                      dnm = ("qT", "kT", "vT")[grp]
                      for b0 in range(0, NB, 2):
                        bl = [b_ for b_ in (b0, b0 + 1) if b_ < NB]
                        ctx = {}
                        for b in bl:
                          pb = b % 2
                          pr = pcnt % 2
                          pcnt += 1
                          cbn, accn = "cb%d" % pr, "acc%d" % pr
                          cbc, accc = cb[pr], acc[pr]
                          for c in range(8):
                              k.mm(pf[pb][:, 0:BW], wg3[:, c, grp * 128:(grp + 1) * 128], uT[:, c, b * BW:(b + 1) * BW],
                                   c == 0, c == 7, ["wg%d" % ws] + uTb(b), PF(pb))
                          if b == 0:
                              k.memset("pool", cbc[:, 0:3], 0.0, [cbn])
                          else:
                              k.cp("act", cbc[:, 0:3], cb[1 - pr][:, BW:BW + 3], ["cb%d" % (1 - pr)], [cbn])
                          k.cp("act", cbc[:, 3:3 + BW], pf[pb][:, 0:BW], PF(pb), [cbn])
                          k.ts("dve", accc, cbc[:, 0:BW], cw[:, chunk * 4:chunk * 4 + 1], ALU.mult, [cbn, "cw"], [accn])
                          for tap in range(1, 4):
                              k.stt(accc, cbc[:, tap:tap + BW], cw[:, chunk * 4 + tap:chunk * 4 + tap + 1], accc, ALU.mult, ALU.add,
                                    [cbn, "cw", accn], [accn])
                          k.act(accc, accc, AF.Silu, [accn], [accn])
                          ctx[b] = (pr, accn, accc)
                        for b in bl:
                          pr, accn, accc = ctx[b]
                          dsl = dstT[:, b * BW:(b + 1) * BW]
                          dw = ["%s%d" % (dnm, i) for i in range(b * TPB, (b + 1) * TPB)]
                          if grp < 2:
                              fm_norm(accc, [accn], dsl, dw, ones_bf, ["ones_bf"], (DH ** -0.5) if grp == 0 else 1.0, [], 2 + pr, pr)
                          else:
                              k.cp("pool", dsl, accc, [accn], dw)
                          next(zgen, None)
                          next(zgen, None)
                  chk("GDN_a")
                  for _ in zgen:
                      pass
                  chk("GDN_b")
                  for i in range(NT):
                      col = i * 8 + h
                      sk = (2 * i) % 8
                      k.tr(pbs(sk), kT[:, i * 128:(i + 1) * 128], ident_bf, ["kT%d" % i, "ident_bf"], ["pf7"])
                      k.act(Kd[:, i, :], pbs(sk), AF.Copy, ["pf7", "kds"], ["Kd%d" % i], scale=tabs["kds"][:, col:col + 1])
                      sv = (2 * i + 1) % 8
                      k.tr(pbs(sv), vT[:, i * 128:(i + 1) * 128], ident_bf, ["vT%d" % i, "ident_bf"], ["pf7"])
                      k.cp("act", Vt[:, i, :], pbs(sv), ["pf7"], ["V%d" % i])
                  chk("GDN_c")
                  P.barrier()
                  if h + 1 < NH:
                      k.dma("pool", wgs, wgdn_d[h + 1, :, :], [], ["wg%d" % ws], "wg%d" % ws)
                  k.memset("pool", S_f, 0.0, ["S_f"])
                  k.memset("pool", S_bf, 0.0, ["S_bf"])

                  def v3(ap):
                      return ap.rearrange("p (g f) -> p g f", g=GRP)

                  def bcm(m):
                      return m.unsqueeze(1).to_broadcast([128, GRP, 128])

                  def colbc(tab, i0, h_):
                      return t3(tab)[:, i0:i0 + GRP, h_].unsqueeze(2).to_broadcast([128, GRP, 128])

                  def gs(t):
                      return slice(t * 128, (t + 1) * 128)

                  def prep(g, h=h):
                      i0 = g * GRP
                      gp = g % 2
                      b0v, b1v, b2v, b3v = [pf[b_][:, 0:GW] for b_ in range(4)]
                      for t in range(GRP):
                          col = (i0 + t) * 8 + h
                          r2 = t % 2
                          k.cp("dve", grep_[r2], tabs["g"][:, col:col + 1].to_broadcast([128, 128]), ["g"], ["grep%d" % r2])
                          k.cp("dve", brep_[r2], tabs["beta"][:, col:col + 1].to_broadcast([128, 128]), ["beta"], ["brep%d" % r2])
                          k.mm(pq(0, t), grep_[r2], triU_f, True, True, ["grep%d" % r2, "triU_f"], ["pf0"])
                          k.mm(pq(1, t), brep_[r2], ident_f, True, True, ["brep%d" % r2, "ident_f"], ["pf1"])
                      k.cp("act", bbc, b1v, ["pf1"], ["bbc"])
                      k.act(eG, b0v, AF.Exp, ["pf0"], ["eG"])
                      k.tt("dve", v3(E_), v3(b0v), colbc(tabs["G"], i0, h), ALU.subtract, ["pf0", "G"], ["E"])
                      k.ts("dve", Y_, E_, 0.0, ALU.max, ["E"], ["Y"])
                      k.ts("dve", E_, E_, 0.0, ALU.min, ["E"], ["E"])
                      k.act(Y_, Y_, AF.Exp, ["Y"], ["Y"], scale=-1.0)
                      k.act(E_, E_, AF.Exp, ["E"], ["E"])
                      k.tt("pool", v3(M1_), v3(bbc), bcm(mUs_bd), ALU.mult, ["bbc", "mUs_bd"], ["M1"])
                      k.tt("dve", v3(DuI), v3(E_), bcm(triU_f), ALU.mult, ["E", "triU_f"], ["DuI"])
                      k.tt("dve", DuB, E_, M1_, ALU.mult, ["E", "M1"], ["DuB"])
                      k.tt("dve", v3(DlB), v3(Y_), bcm(mLs_bd), ALU.mult, ["Y", "mLs_bd"], ["DlB"])
                      k.tt("pool", v3(DlB), v3(DlB), colbc(tabs["beta"], i0, h), ALU.mult, ["DlB", "beta"], ["DlB"])
                      k.tt("dve", v3(DlO), v3(Y_), bcm(mL_off), ALU.mult, ["Y", "mL_off"], ["DlO"])
                      k.tt("pool", v3(DlO), v3(DlO), colbc(tabs["beta"], i0, h), ALU.mult, ["DlO", "beta"], ["DlO"])
                      yield
                      for t in range(GRP):
                          i = i0 + t
                          kTi = kT[:, i * 128:(i + 1) * 128]
                          qTi = qT[:, i * 128:(i + 1) * 128]
                          k.mm(pq(2, t), kTi, kTi, True, True, ["kT%d" % i], ["pf2"])
                          k.mm(pq(3, t), kTi, qTi, True, True, ["kT%d" % i, "qT%d" % i], ["pf3"])
                      kr = ["kT%d" % (i0 + t) for t in range(GRP)]
                      qr = ["qT%d" % (i0 + t) for t in range(GRP)]
                      k.tt("dve", Qb[0], b2v, DlB, ALU.mult, ["pf2", "DlB"], ["Q0"])
                      k.tt("dve", Ao, b2v, DlO, ALU.mult, ["pf2", "DlO"], ["Ao"])
                      k.tt("dve", Pb[0], b2v, DuB, ALU.mult, ["pf2", "DuB"], ["P0"])
                      k.tt("dve", AttnT[gp], b3v, DuI, ALU.mult, ["pf3", "DuI"], ["AttnT%d" % gp])
                      k.tt("dve", v3(Mb[0]), bcm(ident_bf), v3(Pb[0]), ALU.subtract, ["ident_bf", "P0"], ["M0"])
                      k.tt("dve", QeT[gp], qT[:, i0 * 128:i0 * 128 + GW], eG, ALU.mult, qr + ["eG"], ["QeT%d" % gp])
                      k.tt("dve", KeT[gp], kT[:, i0 * 128:i0 * 128 + GW], eG, ALU.mult, kr + ["eG"], ["KeT%d" % gp])
                      cur = 0
                      for lvl in range(1, 6):
                          nxt = 1 - cur
                          for t in range(GRP):
                              if lvl < 5:
                                  k.mm(pq(0, t), Qb[cur][:, gs(t)], Pb[cur][:, gs(t)], True, True, ["Q%d" % cur, "P%d" % cur], ["pf0"])
                              k.mm(pq(1, t), Pb[cur][:, gs(t)], Qb[cur][:, gs(t)], True, True, ["Q%d" % cur, "P%d" % cur], ["pf1"])
                          if lvl < 5:
                              k.cp("act", Pb[nxt], b0v, ["pf0"], ["P%d" % nxt])
                          k.cp("dve", Qb[nxt], b1v, ["pf1"], ["Q%d" % nxt])
                          for t in range(GRP):
                              k.mm(pq(2, t), ident_bf, Mb[cur][:, gs(t)], True, False, ["ident_bf", "M%d" % cur], ["pf2"])
                              k.mm(pq(2, t), Qb[nxt][:, gs(t)], Mb[cur][:, gs(t)], False, True, ["Q%d" % nxt, "M%d" % cur], ["pf2"])
                          k.cp("act", Mb[nxt], b2v, ["pf2"], ["M%d" % nxt])
                          if lvl == 5:
                              k.cp("dve", Y_, b2v, ["pf2"], ["Y"])
                          cur = nxt
                          if lvl == 2:
                              yield
                      yield
                      Ud = Mb[cur]
                      Udn = "M%d" % cur
                      for t in range(GRP):
                          k.tr(pbs(t), Ud[:, gs(t)], ident_bf, [Udn, "ident_bf"], ["pf7"])
                      k.cp("act", Td, pbf[:, 0:GW], ["pf7"], ["Td"])
                      for t in range(GRP):
                          k.mm(pq(3, t), Ao[:, gs(t)], Ud[:, gs(t)], True, True, ["Ao", Udn], ["pf3"])
                      k.cp("act", X1, b3v, ["pf3"], ["X1"])
                      for t in range(GRP):
                          k.mm(pq(0, t), Td[:, gs(t)], X1[:, gs(t)], True, True, ["Td", "X1"], ["pf0"])
                      k.tt("dve", Y_, Y_, b0v, ALU.subtract, ["Y", "pf0"], ["Y"])
                      k.tt("pool", v3(TpT[gp]), v3(Y_), colbc(tabs["beta"], i0, h), ALU.mult, ["Y", "beta"], ["TpT%d" % gp])
                      yield

                  def scan_step(i, h=h):
                      g = i // GRP
                      t = i % GRP
                      gp = g % 2
                      d2 = i % 2
                      col = i * 8 + h
                      k.mm(pq(4, 0), KeT[gp][:, gs(t)], S_bf, True, True, ["KeT%d" % gp, "S_bf"], ["pf4"])
                      k.tt("dve", R_bf, Vt[:, i, :], pq(4, 0), ALU.subtract, ["V%d" % i, "pf4"], ["R_bf"])
                      k.mm(pq(4, 1), TpT[gp][:, gs(t)], R_bf, True, True, ["TpT%d" % gp, "R_bf"], ["pf4"])
                      k.cp("act", vnew, pq(4, 1), ["pf4"], ["vnew"])
                      k.mm(pq(5 + d2, 0), QeT[gp][:, gs(t)], S_bf, True, False, ["QeT%d" % gp, "S_bf"], ["pf%d" % (5 + d2)])
                      k.mm(pq(5 + d2, 0), AttnT[gp][:, gs(t)], vnew, False, True, ["AttnT%d" % gp, "vnew"], ["pf%d" % (5 + d2)])
                      k.mm(pq(4, 3), Kd[:, i, :], vnew, True, True, ["Kd%d" % i, "vnew"], ["pf4"])
                      k.stt(S_f, S_f, tabs["decl"][:, col:col + 1], pq(4, 3), ALU.mult, ALU.add, ["S_f", "decl", "pf4"], ["S_f"])
                      k.cp("act", S_bf, S_f, ["S_f"], ["S_bf"])
                      k.act(junk128, pq(5 + d2, 0), AF.Square, ["pf%d" % (5 + d2)], ["junk128", "ssq"], accum=ssq)
                      k.act(ssq, ssq, AF.Ln, ["ssq", "eps_c"], ["ssq"], scale=1.0 / DH, bias=eps_c)
                      k.act(ssq, ssq, AF.Exp, ["ssq"], ["ssq"], scale=-0.5)
                      k.stt(on_bf, pq(5 + d2, 0), ssq, gz[:, i, :], ALU.mult, ALU.mult, ["pf%d" % (5 + d2), "ssq", "gz%d" % i], ["on_bf"])
                      so = 4 + (i % 4)
                      k.tr(pbs(so), on_bf, ident_bf, ["on_bf", "ident_bf"], ["pf7"])
                      k.cp("act", oaT[:, h, i * 128:(i + 1) * 128], pbs(so), ["pf7"], ["oaT%d_%d" % (h, i)])

                  for _ in prep(0):
                      pass
                  NST = 4
                  per = (NST + GRP - 1) // GRP
                  for g in range(NG):
                      gen = prep(g + 1) if g + 1 < NG else iter(())
                      for t in range(GRP):
                          scan_step(g * GRP + t)
                          for _ in range(per):
                              next(gen, None)
                      for _ in gen:
                          pass
                  P.barrier()
                  chk("GDN_h0")

              chk("GDN2")
              P.barrier()
              A.release(m0)
              wf = [A.bf(8 * 384), A.bf(8 * 384)]
              fqT, fkT, fV_f = A.bf(T), A.bf(T), A.bf(T)
              fV = fV_f.rearrange("p (i d) -> p i d", d=128)
              sq = [A.bf(BW), A.bf(BW)]
              lnt = [A.f32(BW), A.f32(BW)]
              rs = [A.f32(BW), A.f32(BW)]
              wff = A.f32(8 * 384)
              PT = [A.bf(BW), A.bf(BW)]
              rden = [A.f32(BW), A.f32(BW)]
              scale = DH ** -0.5
              for h in range(NH):
                  ws = h % 2
                  wfs = wf[ws]
                  wf3 = wfs.rearrange("p (c n) -> p c n", c=8)
                  k.dma("sp", wff, wfox_d[h, :, :], [], ["wff"], "wff")
                  for cc in range(3):
                      k.cp("pool", wfs[:, cc * 1024:(cc + 1) * 1024], wff[:, cc * 1024:(cc + 1) * 1024], ["wff"], ["wf%d" % ws])
                  def fvproj(ws=ws, wf3=wf3):
                      for i in range(NT):
                          q4 = i % 4
                          for c in range(8):
                              k.mm(pq(5, q4), uT[:, c, i * 128:(i + 1) * 128], wf3[:, c, 256:384], c == 0, c == 7,
                                   ["uT%d" % i, "wf%d" % ws], ["pf5"])
                          if q4 == TPB - 1 or i == NT - 1:
                              n4 = q4 + 1
                              i0 = i - q4
                              k.cp("act", fV_f[:, i0 * 128:(i0 + n4) * 128], pf[5][:, 0:n4 * 128], ["pf5"],
                                   ["fV%d" % ii for ii in range(i0, i0 + n4)])
                          yield
                  fvgen = fvproj()
                  for grp in range(2):
                      dstT = (fqT, fkT)[grp]
                      dnm = ("fqT", "fkT")[grp]
                      gcol = (fqg, fkg)[grp]
                      gnm = ("fqg", "fkg")[grp]
                      for b in range(NB):
                          pb = b % 2
                          for c in range(8):
                              k.mm(pf[pb][:, 0:BW], wf3[:, c, grp * 128:(grp + 1) * 128], uT[:, c, b * BW:(b + 1) * BW],
                                   c == 0, c == 7, ["wf%d" % ws] + uTb(b), PF(pb))
                          dw = ["%s%d" % (dnm, i) for i in range(b * TPB, (b + 1) * TPB)]
                          pr = pcnt % 2
                          pcnt += 1
                          fm_norm(pf[pb][:, 0:BW], PF(pb), dstT[:, b * BW:(b + 1) * BW], dw, onesm_bf, ["onesm_bf"],
                                  gcol, [gnm], (2, 7)[pr], pr)
                          next(fvgen, None)
                          next(fvgen, None)
                  for _ in fvgen:
                      pass
                  cnt = 0
                  for Ib in range(NB):
                      bo, bd = ((5, 6), (0, 1))[Ib % 2]
                      last_j = (Ib + 1) * TPB - 1
                      for j in range(last_j + 1):
                          r0 = max(0, j - Ib * TPB)
                          qoff = r0 * 128
                          sb_ = 3 + (cnt % 2)
                          ptb = cnt % 2
                          cnt += 1
                          qr = ["fqT%d" % (Ib * TPB + r) for r in range(r0, TPB)]
                          k.mm(pf[sb_][:, qoff:BW], fkT[:, j * 128:(j + 1) * 128], fqT[:, Ib * BW + qoff:(Ib + 1) * BW], True, True,
                               ["fkT%d" % j] + qr, PF(sb_))
                          for r in range(r0, TPB):
                              i = Ib * TPB + r
                              k.act(PT[ptb][:, r * 128:(r + 1) * 128], pf[sb_][:, r * 128:(r + 1) * 128], AF.Exp,
                                    PF(sb_) + ["btab"], ["PT%d" % ptb], scale=scale, bias=btab[:, h, j, i:i + 1])
                              if i == j:
                                  k.tt("dve", PT[ptb][:, r * 128:(r + 1) * 128], PT[ptb][:, r * 128:(r + 1) * 128], mUi_bf, ALU.mult,
                                       ["PT%d" % ptb, "mUi_bf"], ["PT%d" % ptb])
                          k.mm(pf[bo][:, qoff:BW], fV[:, j, :], PT[ptb][:, qoff:BW], j == 0, j == last_j,
                               ["fV%d" % j, "PT%d" % ptb], PF(bo))
                          k.mm(pf[bd][:, qoff:BW], ones_bf, PT[ptb][:, qoff:BW], j == 0, j == last_j,
                               ["ones_bf", "PT%d" % ptb], PF(bd))
                      rdn = rden[Ib % 2]
                      k.P.op("dve", lambda e, rdn=rdn, bd=bd: e.reciprocal(out=rdn, in_=pf[bd][:, 0:BW]), PF(bd), ["rden%d" % (Ib % 2)])
                      k.tt("dve", obT[:, h, Ib * BW:(Ib + 1) * BW], pf[bo][:, 0:BW], rdn, ALU.mult, PF(bo) + ["rden%d" % (Ib % 2)],
                           ["obT%d_%d" % (h, Ib)])

              if dbg and s == 0:
                  outs.append(k.dma("pool", dbg_d["uT"][:, :], uT_flat, uT_all, [], "dbgp_uT"))
                  outs.append(k.dma("pool", dbg_d["oaT"][:, :], oaT_flat, ["oaT%d_%d" % (h, i) for h in range(NH) for i in range(NT)], [], "dbgp_oaT"))
                  outs.append(k.dma("pool", dbg_d["obT"][:, :], obT_flat, ["obT%d_%d" % (h, b) for h in range(NH) for b in range(NB)], [], "dbgp_obT"))
                  for ti, nm in enumerate(["g", "G", "beta", "kds", "decl", "c"]):
                      outs.append(k.dma("sp", dbg_d["tab"][:, ti * NT8:(ti + 1) * NT8], tabs[nm], [nm], [], "dbg"))

              chk("FOX")
              P.barrier()
              A.release(m0)
              mT_flat = A.bf(8 * T)
              mT = mT_flat.rearrange("p (c t) -> p c t", c=8)
              m_mT = A.mark()
              wga = [A.bf(8 * 256), A.bf(8 * 256)]
              wpa = [A.bf(8 * 256), A.bf(8 * 256)]
              sA, sB, mA, mB = [A.f32(BW) for _ in range(4)]
              wgaf = A.f32(8 * 256)
              wpaf = A.f32(8 * 256)
              oa_all = {b: ["oaT%d_%d" % (h, i) for h in range(NH) for i in range(b * TPB, (b + 1) * TPB)] for b in range(NB)}
              ob_all = {b: ["obT%d_%d" % (h, b) for h in range(NH)] for b in range(NB)}
              for n in range(8):
                  ws = n % 2
                  k.dma("sp", wgaf, wgate_d[n, :, :], [], ["wgaf"], "wgaf")
                  k.dma("sp", wpaf, wpab_d[n, :, :], [], ["wpaf"], "wpaf")
                  for cc in range(2):
                      k.cp("pool", wga[ws][:, cc * 1024:(cc + 1) * 1024], wgaf[:, cc * 1024:(cc + 1) * 1024], ["wgaf"], ["wga%d" % ws])
                      k.cp("pool", wpa[ws][:, cc * 1024:(cc + 1) * 1024], wpaf[:, cc * 1024:(cc + 1) * 1024], ["wpaf"], ["wpa%d" % ws])
                  wga3 = wga[ws].rearrange("p (c n) -> p c n", c=8)
                  wpa3 = wpa[ws].rearrange("p (c n) -> p c n", c=8)
                  for b in range(NB):
                      bs = slice(b * BW, (b + 1) * BW)
                      o4 = 4 * ((n * NB + b) % 2)
                      for c in range(8):
                          k.mm(pf[o4 + 0][:, 0:BW], wpa3[:, c, 0:128], oaT[:, c, bs], c == 0, c == 7, ["wpa%d" % ws] + oa_all[b], PF(o4 + 0))
                      for c in range(8):
                          k.mm(pf[o4 + 1][:, 0:BW], wga3[:, c, 0:128], uT[:, c, bs], c == 0, c == 7, ["wga%d" % ws] + uTb(b), PF(o4 + 1))
                      for c in range(8):
                          k.mm(pf[o4 + 2][:, 0:BW], wpa3[:, c, 128:256], obT[:, c, bs], c == 0, c == 7, ["wpa%d" % ws] + ob_all[b], PF(o4 + 2))
                      for c in range(8):
                          k.mm(pf[o4 + 3][:, 0:BW], wga3[:, c, 128:256], uT[:, c, bs], c == 0, c == 7, ["wga%d" % ws] + uTb(b), PF(o4 + 3))
                      k.act(sA, pf[o4 + 1][:, 0:BW], AF.Sigmoid, PF(o4 + 1), ["sA"])
                      k.act(sB, pf[o4 + 3][:, 0:BW], AF.Sigmoid, PF(o4 + 3), ["sB"])
                      k.tt("dve", mA, pf[o4 + 0][:, 0:BW], sA, ALU.mult, PF(o4 + 0) + ["sA"], ["mA"])
                      k.tt("dve", mB, pf[o4 + 2][:, 0:BW], sB, ALU.mult, PF(o4 + 2) + ["sB"], ["mB"])
                      k.tt("dve", mT[:, n, bs], mA, mB, ALU.add, ["mA", "mB"], ["mT%d_%d" % (n, b)])

              P.barrier()
              A.release(m_mT)
              wo = A.bf(8 * 1024)
              wo3 = wo.rearrange("p (c n) -> p c n", c=8)
              xt = [A.f32(D), A.f32(D)]
              hnb = [A.bf(D), A.bf(D)]
              junk = A.bf(D)
              ssc = [A.f32(1), A.f32(1)]
              k.dma("pool", wo, wout_d[:, :], [], ["wo"], "wo")
              for i in range(NT):
                  sl = i % 2
                  k.dma("sp", xt[sl], x_d[s, i * 128:(i + 1) * 128, :], [], ["xt%d" % sl], "xt%d" % sl)
                  for half in range(2):
                      for c in range(8):
                          k.mm(pf[half][:, :], mT[:, c, i * 128:(i + 1) * 128], wo3[:, c, half * 512:(half + 1) * 512], c == 0, c == 7,
                               ["wo"] + ["mT%d_%d" % (n_, i // TPB) for n_ in range(8)], PF(half))
                      k.tt("dve", h3[:, i, half * 512:(half + 1) * 512], pf[half][:, :], xt[sl][:, half * 512:(half + 1) * 512], ALU.add,
                           PF(half) + ["xt%d" % sl], ["h%d_%d" % (i, half)])
                  hr = ["h%d_0" % i, "h%d_1" % i]
                  k.act(junk, h3[:, i, :], AF.Square, hr, ["junk", "ss%d" % sl], accum=ssc[sl])
                  k.act(ssc[sl], ssc[sl], AF.Ln, ["ss%d" % sl, "eps_c"], ["ss%d" % sl], scale=1.0 / D, bias=eps_c)
                  k.act(ssc[sl], ssc[sl], AF.Exp, ["ss%d" % sl], ["ss%d" % sl], scale=-0.5)
                  k.stt(hnb[sl], h3[:, i, :], ssc[sl], gmlp_bc, ALU.mult, ALU.mult, hr + ["ss%d" % sl, "gmlp_bc"], ["hnb%d" % sl])
                  for c in range(8):
                      k.tr(pbs(c), hnb[sl][:, c * 128:(c + 1) * 128], ident_bf, ["hnb%d" % sl, "ident_bf"], ["pf7"])
                  k.cp("act", hnT[:, :, i * 128:(i + 1) * 128], pbf.rearrange("p (c t) -> p c t", c=8),
                       ["pf7"], ["hnT%d" % i])
              if dbg and s == 0:
                  outs.append(k.dma("sp", dbg_d["h"][:, :], h_flat, ["h%d_%d" % (i, hf) for i in range(NT) for hf in range(2)], [], "dbg"))

              chk("POSTH")
              P.barrier()
              A.release(m0)
              aT_flat = A.bf(32 * BW)
              aT = aT_flat.rearrange("p (f t) -> p f t", f=32)
              wup = [A.bf(1024) for _ in range(4)]
              wdn = [A.bf(1024) for _ in range(4)]
              wstg = [A.f32(1024) for _ in range(3)]
              scnt = 0
              rtmp = [A.f32(BW), A.f32(BW)]
              otile = [A.f32(D), A.f32(D)]
              for b in range(NB):
                  bs = slice(b * BW, (b + 1) * BW)
                  hb = ["hnT%d" % i for i in range(b * TPB, (b + 1) * TPB)]
                  for f in range(32):
                      ws = f % 4
                      sg = scnt % 3
                      scnt += 1
                      k.dma("sp", wstg[sg], wup_d[f, :, :], [], ["wstg%d" % sg], "wstg%d" % sg)
                      k.cp("pool", wup[ws], wstg[sg], ["wstg%d" % sg], ["wup%d" % ws])
                      wu3 = wup[ws].rearrange("p (c n) -> p c n", c=8)
                      pb = f % 2
                      for c in range(8):
                          k.mm(pf[pb][:, 0:BW], wu3[:, c, :], hnT[:, c, bs], c == 0, c == 7, ["wup%d" % ws] + hb, PF(pb))
                      k.act(rtmp[pb], pf[pb][:, 0:BW], AF.Relu, PF(pb), ["rtmp%d" % pb])
                      k.tt("dve", aT[:, f, :], rtmp[pb], rtmp[pb], ALU.mult, ["rtmp%d" % pb], ["aT%d" % f])
                  for f in range(32):
                      ws = f % 4
                      sg = scnt % 3
                      scnt += 1
                      k.dma("sp", wstg[sg], wdn_d[f, :, :], [], ["wstg%d" % sg], "wstg%d" % sg)
                      k.cp("pool", wdn[ws], wstg[sg], ["wstg%d" % sg], ["wdn%d" % ws])
                      for t in range(TPB):
                          for half in range(2):
                              bk = t * 2 + half
                              k.mm(pf[bk][:, :], aT[:, f, t * 128:(t + 1) * 128], wdn[ws][:, half * 512:(half + 1) * 512], f == 0, f == 31,
                                   ["aT%d" % f, "wdn%d" % ws], PF(bk))
                  for t in range(TPB):
                      i = b * TPB + t
                      sl = i % 2
                      for half in range(2):
                          bk = t * 2 + half
                          k.tt("dve", otile[sl][:, half * 512:(half + 1) * 512], pf[bk][:, :], h3[:, i, half * 512:(half + 1) * 512], ALU.add,
                               PF(bk) + ["h%d_%d" % (i, half)], ["ot%d_%d" % (sl, half)])
                      outs.append(k.dma("sp", out_d[s, i * 128:(i + 1) * 128, :], otile[sl], ["ot%d_0" % sl, "ot%d_1" % sl], [], "out%d" % sl))
              P.barrier()

          except StopBuild:
            outs.append(k.dma("pool", dbg_d["uT"][:, :], uT_flat, [], [], "dbgp_stop"))
            break
        P.final_waits += outs
        nw = P.emit(st)
        print("ops", len(P.ops), "waits", nw, "arena peak", A.peak)
    return nc


def prep_weights(inp, T):
    NT = T // 128
    f = lambda a: np.ascontiguousarray(np.asarray(a, dtype=np.float32))
    w_in = np.asarray(inp["w_in"])[0]
    offs = np.cumsum([0, 1024, 1024, 1024, 1024, 8, 8, 1024, 1024, 1024, 8, 1024, 1024])
    gq, gk, gv, gz, ga, gb, fq, fk, fv, ff, gta, gtb = [w_in[:, offs[i]:offs[i + 1]] for i in range(12)]

    def pc(w):
        return w.reshape(8, 128, -1).transpose(1, 0, 2)
    w_gdn = np.stack([pc(np.concatenate([g_[:, h * 128:(h + 1) * 128] for g_ in (gq, gk, gv, gz)], 1)).reshape(128, -1) for h in range(8)])
    w_fox = np.stack([pc(np.concatenate([g_[:, h * 128:(h + 1) * 128] for g_ in (fq, fk, fv)], 1)).reshape(128, -1) for h in range(8)])
    w_small = pc(np.concatenate([ga, gb, ff], 1)).reshape(128, -1)
    w_gate = np.stack([pc(np.concatenate([gta[:, n * 128:(n + 1) * 128], gtb[:, n * 128:(n + 1) * 128]], 1)).reshape(128, -1) for n in range(8)])
    pa = np.asarray(inp["w_proj_gdn"])[0]
    pb = np.asarray(inp["w_proj_fox"])[0]
    w_pab = np.stack([pc(np.concatenate([pa[:, n * 128:(n + 1) * 128], pb[:, n * 128:(n + 1) * 128]], 1)).reshape(128, -1) for n in range(8)])
    w_o = pc(np.asarray(inp["w_out"])[0]).reshape(128, -1)
    wu = np.asarray(inp["w_up"])[0]
    w_up = np.stack([pc(wu[:, fch * 128:(fch + 1) * 128]).reshape(128, -1) for fch in range(32)])
    w_down = np.asarray(inp["w_down"])[0].reshape(32, 128, 1024)
    cwt = np.asarray(inp["gdn_conv_w"])[0]
    convw = cwt.reshape(4, 24, 128).transpose(2, 1, 0).reshape(128, 96)
    d = {
        "w_gdn": w_gdn, "w_fox": w_fox, "w_small": w_small, "w_gate": w_gate, "w_pab": w_pab, "w_o": w_o,
        "w_up": w_up, "w_down": w_down, "convw": convw,
        "gmix": np.asarray(inp["norm_mix_g"])[0][None, :], "gmlp": np.asarray(inp["norm_mlp_g"])[0][None, :],
        "gnorm4": np.tile(np.asarray(inp["gdn_norm_g"])[0], 4)[None, :],
        "alog_t": np.tile(np.asarray(inp["gdn_a_log"])[0], NT)[None, :],
        "dtb_t": np.tile(np.asarray(inp["gdn_dt_bias"])[0], NT)[None, :],
        "fb_t": np.tile(np.asarray(inp["fox_f_bias"])[0], NT)[None, :],
        "fqg": np.asarray(inp["fox_q_norm_g"])[0][:, None], "fkg": np.asarray(inp["fox_k_norm_g"])[0][:, None],
    }
    return {k_: f(v) for k_, v in d.items()}


_NC_CACHE = {}


def kernel(**inputs):
    x = np.asarray(inputs["x"], dtype=np.float32)
    Bt, T, _ = x.shape
    ncores = 8
    nseq = Bt // ncores
    wd = prep_weights(inputs, T)
    key = (T, nseq)
    if key not in _NC_CACHE:
        nc = bass.Bass("TRN2", target_bir_lowering=False)
        build(nc, T, nseq)
        _NC_CACHE[key] = nc
    nc = _NC_CACHE[key]
    in_maps = []
    for c in range(ncores):
        m = dict(wd)
        m["x"] = np.ascontiguousarray(x[c * nseq:(c + 1) * nseq])
        in_maps.append(m)
    res = run_bass_kernel_spmd(nc, in_maps, core_ids=list(range(ncores)))
    out = np.concatenate([res.results[c]["out"] for c in range(ncores)], axis=0)
    return out.astype(np.float32)
```
